# Optimizing a Trainium2 kernel written in Bass

```python
import math
import jax, jax.numpy as jnp
from jax import lax
import numpy as np

D_MODEL = 2048
BATCH = 8
SEQ = 2048
DEPTH = 1

HEAD_DIM = 64
N_HEADS_SWA = 16
N_KV_SWA = 4
N_HEADS_SB = 16
WINDOW = 128
BLOCK = 128
MEM_LEN = 256
N_HEADS_MEM = 4
HEAD_DIM_MEM = D_MODEL // N_HEADS_MEM
D_SWA = N_HEADS_SWA * HEAD_DIM
D_KV_SWA = N_KV_SWA * HEAD_DIM
D_SB = N_HEADS_SB * HEAD_DIM
D_MIX = D_SWA + D_SB
D_IN = D_SWA + 2 * D_KV_SWA + 3 * D_SB
D_FF = -(-8 * D_MODEL // (3 * 256)) * 256
ALPHA = (2.0 * DEPTH) ** 0.25
BETA = (8.0 * DEPTH) ** -0.25
LN_EPS = 1e-5
RMS_EPS = 1e-6

kernel_name = "hymba_swa_sink_stickbreak_deepnorm_layer"


def _alibi_slopes(n):
    return jnp.asarray(2.0 ** (-8.0 * np.arange(1, n + 1) / n), dtype=jnp.float32)


def layer_norm(x, g, b):
    xf = x.astype(jnp.float32)
    mu = jnp.mean(xf, axis=-1, keepdims=True)
    var = jnp.mean(jnp.square(xf - mu), axis=-1, keepdims=True)
    y = (xf - mu) * lax.rsqrt(var + LN_EPS)
    return (y * g.astype(jnp.float32) + b.astype(jnp.float32)).astype(x.dtype)


def head_rmsnorm(o, g):
    H, D = o.shape[-2:]
    of = o.astype(jnp.float32)
    y = of * lax.rsqrt(jnp.mean(jnp.square(of), axis=-1, keepdims=True) + RMS_EPS)
    return (y * g.reshape(H, D).astype(jnp.float32)).astype(o.dtype)


def swa_sink_attention(q, k, v, sinks):
    B, S, HQ, D = q.shape
    HKV = k.shape[2]
    G = HQ // HKV
    nb = S // BLOCK
    qb = q.reshape(B, nb, BLOCK, HKV, G, D)

    def band(t):
        tb = t.reshape(B, nb, BLOCK, HKV, D)
        prev = jnp.pad(tb, ((0, 0), (1, 0), (0, 0), (0, 0), (0, 0)))[:, :-1]
        return jnp.concatenate([prev, tb], axis=2)

    kb, vb = band(k), band(v)
    scores = jnp.einsum('bnqkgd,bnskd->bnkgqs', qb, kb).astype(jnp.float32) / math.sqrt(D)
    qi = jnp.arange(BLOCK)[:, None]
    kj = jnp.arange(2 * BLOCK)[None, :]
    dist = qi + BLOCK - kj
    key_pos = jnp.arange(nb)[:, None, None] * BLOCK - BLOCK + kj
    valid = (dist >= 0) & (dist < WINDOW) & (key_pos >= 0)
    slopes = _alibi_slopes(HQ).reshape(HKV, G)
    scores = scores - slopes[None, None, :, :, None, None] * dist.astype(jnp.float32)
    scores = jnp.where(valid[None, :, None, None], scores, -jnp.inf)
    sink = sinks.astype(jnp.float32).reshape(HKV, G)[None, None, :, :, None, None]
    m = jnp.maximum(jnp.max(scores, axis=-1, keepdims=True), sink)
    p = jnp.exp(scores - m)
    denom = jnp.sum(p, axis=-1, keepdims=True) + jnp.exp(sink - m)
    out = jnp.einsum('bnkgqs,bnskd->bnqkgd', (p / denom).astype(v.dtype), vb)
    return out.reshape(B, S, HQ, D)


def stick_breaking_attention(q, k, v):
    B, S, H, D = q.shape
    nb = S // BLOCK
    outs = []
    for n in range(nb):
        t0, t1 = n * BLOCK, (n + 1) * BLOCK
        kn, vn = k[:, :t1], v[:, :t1]
        z = jnp.einsum('bqhd,bshd->bhqs', q[:, t0:t1], kn).astype(jnp.float32) / math.sqrt(D)
        causal = jnp.arange(t1)[None, :] < (t0 + jnp.arange(BLOCK))[:, None]
        log_beta = jax.nn.log_sigmoid(z)
        log_1m = jnp.where(causal, jax.nn.log_sigmoid(-z), 0.0)
        between = lax.cumsum(log_1m, axis=log_1m.ndim - 1, reverse=True) - log_1m
        a = jnp.where(causal, jnp.exp(log_beta + between), 0.0)
        outs.append(jnp.einsum('bhqs,bshd->bqhd', a.astype(v.dtype), vn))
    return jnp.concatenate(outs, axis=1)


def memory_cross_attention(h, mem, w_q, w_kv, w_o):
    B, S, _ = h.shape
    q = (h @ w_q).reshape(B, S, N_HEADS_MEM, HEAD_DIM_MEM)
    k, v = jnp.split(mem @ w_kv, 2, axis=-1)
    k = k.reshape(B, -1, N_HEADS_MEM, HEAD_DIM_MEM)
    v = v.reshape(B, -1, N_HEADS_MEM, HEAD_DIM_MEM)
    s = jnp.einsum('bqhd,bmhd->bhqm', q, k).astype(jnp.float32) / math.sqrt(HEAD_DIM_MEM)
    p = jax.nn.softmax(s, axis=-1).astype(v.dtype)
    o = jnp.einsum('bhqm,bmhd->bqhd', p, v).reshape(B, S, D_MODEL)
    return o @ w_o


def setup_inputs(seed: int = 0) -> dict:
    key = jax.random.key(seed)
    ks = jax.random.split(key, 20)
    f32 = jnp.float32

    def nrm(k, shape, scale):
        return jax.random.normal(k, shape, f32) * scale

    d = D_MODEL
    col_scale = jnp.concatenate([
        jnp.ones((D_SWA + D_KV_SWA,), f32), jnp.full((D_KV_SWA,), BETA, f32),
        jnp.ones((2 * D_SB,), f32), jnp.full((D_SB,), BETA, f32)])
    kv_scale = jnp.concatenate([jnp.ones((d,), f32), jnp.full((d,), BETA, f32)])
    return {
        "x": jax.random.normal(ks[0], (BATCH, SEQ, d), f32),
        "mem": jax.random.normal(ks[1], (BATCH, MEM_LEN, d), f32),
        "w_in": nrm(ks[2], (DEPTH, d, D_IN), d ** -0.5) * col_scale,
        "sinks": nrm(ks[3], (DEPTH, N_HEADS_SWA), 0.5),
        "g_swa": 1.0 + nrm(ks[4], (DEPTH, D_SWA), 0.02),
        "g_sb": 1.0 + nrm(ks[5], (DEPTH, D_SB), 0.02),
        "w_o": nrm(ks[6], (DEPTH, D_MIX, d), BETA * D_MIX ** -0.5),
        "ln1_g": 1.0 + nrm(ks[7], (DEPTH, d), 0.02),
        "ln1_b": nrm(ks[8], (DEPTH, d), 0.02),
        "w_q_mem": nrm(ks[9], (DEPTH, d, d), d ** -0.5),
        "w_kv_mem": nrm(ks[10], (DEPTH, d, 2 * d), d ** -0.5) * kv_scale,
        "w_o_mem": nrm(ks[11], (DEPTH, d, d), BETA * d ** -0.5),
        "ln2_g": 1.0 + nrm(ks[12], (DEPTH, d), 0.02),
        "ln2_b": nrm(ks[13], (DEPTH, d), 0.02),
        "w_gate_up": nrm(ks[14], (DEPTH, d, 2 * D_FF), d ** -0.5),
        "w_down": nrm(ks[15], (DEPTH, D_FF, d), BETA * D_FF ** -0.5),
        "ln3_g": 1.0 + nrm(ks[16], (DEPTH, d), 0.02),
        "ln3_b": nrm(ks[17], (DEPTH, d), 0.02),
    }


def reference(x, mem, w_in, sinks, g_swa, g_sb, w_o, ln1_g, ln1_b, w_q_mem, w_kv_mem,
              w_o_mem, ln2_g, ln2_b, w_gate_up, w_down, ln3_g, ln3_b):
    B, S, _ = x.shape
    splits = np.cumsum([D_SWA, D_KV_SWA, D_KV_SWA, D_SB, D_SB]).tolist()
    h = x
    for l in range(DEPTH):
        q_a, k_a, v_a, q_b, k_b, v_b = jnp.split(h @ w_in[l], splits, axis=-1)
        o_a = swa_sink_attention(
            q_a.reshape(B, S, N_HEADS_SWA, HEAD_DIM),
            k_a.reshape(B, S, N_KV_SWA, HEAD_DIM),
            v_a.reshape(B, S, N_KV_SWA, HEAD_DIM), sinks[l])
        o_b = stick_breaking_attention(
            q_b.reshape(B, S, N_HEADS_SB, HEAD_DIM),
            k_b.reshape(B, S, N_HEADS_SB, HEAD_DIM),
            v_b.reshape(B, S, N_HEADS_SB, HEAD_DIM))
        o_a = head_rmsnorm(o_a, g_swa[l]).reshape(B, S, D_SWA)
        o_b = head_rmsnorm(o_b, g_sb[l]).reshape(B, S, D_SB)
        mix = jnp.concatenate([o_a, o_b], axis=-1) @ w_o[l]
        h = layer_norm(ALPHA * h + mix, ln1_g[l], ln1_b[l])
        c = memory_cross_attention(h, mem, w_q_mem[l], w_kv_mem[l], w_o_mem[l])
        h = layer_norm(ALPHA * h + c, ln2_g[l], ln2_b[l])
        gate, up = jnp.split(h @ w_gate_up[l], 2, axis=-1)
        f = (jax.nn.silu(gate) * up) @ w_down[l]
        h = layer_norm(ALPHA * h + f, ln3_g[l], ln3_b[l])
    return h
```

```python
import math
import numpy as np
import concourse.bass as bass
import concourse.mybir as mybir
from concourse.bass_utils import run_bass_kernel_spmd

AF = mybir.ActivationFunctionType
ALU = mybir.AluOpType
AX = mybir.AxisListType
F32 = mybir.dt.float32
BF16 = mybir.dt.bfloat16

D = 2048
DFF = 5632
NFC = DFF // 128
ALPHA = 2.0 ** 0.25
LN_EPS = 1e-5
RMS_EPS = 1e-6
MEM = 256
NEG = -30000.0
SLOPES = [2.0 ** (-8.0 * (h + 1) / 16) for h in range(16)]
QSCALE_MEM = 1.0 / math.sqrt(512.0)

CF_ID, CF_BD, CF_ND, CF_ND0, CF_G, CF_SK, CF_M01, NCF = 0, 128, 256, 512, 768, 784, 800, 928
CB_ID, CB_NT, CB_NSL, CB_MK, NCB = 0, 128, 256, 384, 512

W_IN2 = 4864


class Buf:
    __slots__ = ("name", "w", "r")

    def __init__(self, name):
        self.name = name
        self.w = None
        self.r = {}


class _Eng:
    def __init__(self, name, h, sem):
        self.name, self.h, self.sem, self.cnt, self.waited = name, h, sem, 0, {}


class _Stream:
    def __init__(self, name, sem):
        self.name, self.sem, self.cnt = name, sem, 0


class Sched:
    def __init__(self, nc):
        self.nc = nc
        self.E = {}
        for n, h in (("pe", nc.tensor), ("act", nc.scalar), ("dve", nc.vector)):
            self.E[n] = _Eng(n, h, nc.alloc_semaphore("s_" + n))
        for n, h in (("pool", nc.gpsimd), ("sp", nc.sync)):
            self.E[n] = _Eng(n, h, None)
        self.streams = {}
        self.sems = {}
        for n in ("pe", "act", "dve"):
            self.sems[n] = self.E[n].sem

    def stream(self, name):
        if name not in self.streams:
            st = _Stream(name, self.nc.alloc_semaphore("d_" + name))
            self.streams[name] = st
            self.sems["d_" + name] = st.sem
        return self.streams[name]

    def _deps(self, reads, writes):
        deps = {}

        def add(kv):
            k, v = kv
            if deps.get(k, 0) < v:
                deps[k] = v
        for b in reads:
            if b.w is not None:
                add(b.w)
        for b in writes:
            if b.w is not None:
                add(b.w)
            for kv in b.r.items():
                add(kv)
        return deps

    def _wait(self, eng, deps):
        for k, v in deps.items():
            if k == eng.name and eng.name == "pe":
                continue
            if eng.waited.get(k, 0) < v:
                eng.h.wait_ge(self.sems[k], v)
                eng.waited[k] = v

    def _mark(self, key, val, reads, writes):
        for b in reads:
            if b.r.get(key, 0) < val:
                b.r[key] = val
        for b in writes:
            b.w = (key, val)
            b.r = {}

    def op(self, en, fn, reads=(), writes=()):
        eng = self.E[en]
        self._wait(eng, self._deps(reads, writes))
        ins = fn(eng.h)
        eng.cnt += 1
        ins.then_inc(eng.sem, 1)
        self._mark(en, eng.cnt, reads, writes)

    def pe(self, fns, reads=(), writes=()):
        eng = self.E["pe"]
        self._wait(eng, self._deps(reads, writes))
        ins = None
        for fn in fns:
            ins = fn(eng.h)
        eng.cnt += 1
        ins.then_inc(eng.sem, 1)
        self._mark("pe", eng.cnt, reads, writes)

    def dma(self, qn, out, in_, reads, writes, stream):
        q = self.E[qn]
        st = self.stream(stream)
        key = "d_" + stream
        deps = self._deps(reads, writes)
        if st.cnt > 0:
            deps[key] = max(deps.get(key, 0), st.cnt)
        self._wait(q, deps)
        q.h.dma_start(out=out, in_=in_).then_inc(st.sem, 16)
        st.cnt += 16
        self._mark(key, st.cnt, reads, writes)

    def barrier(self):
        tgt = {n: self.E[n].cnt for n in ("pe", "act", "dve")}
        for st in self.streams.values():
            tgt["d_" + st.name] = st.cnt
        for e in self.E.values():
            for k, v in tgt.items():
                if k == e.name or v == 0:
                    continue
                if e.waited.get(k, 0) < v:
                    e.h.wait_ge(self.sems[k], v)
                    e.waited[k] = v

    def finish(self):
        sp = self.E["sp"]
        for st in self.streams.values():
            if st.cnt and sp.waited.get("d_" + st.name, 0) < st.cnt:
                sp.h.wait_ge(st.sem, st.cnt)
                sp.waited["d_" + st.name] = st.cnt


def build(S=2048, debug=None):
    G = S // 512
    NB = S // 128
    nc = bass.Bass("TRN2", target_bir_lowering=False)
    sc = Sched(nc)

    def dram(name, shape, kind="ExternalInput"):
        return nc.dram_tensor(name, shape, F32, kind=kind).ap()

    xT_d = dram("xT", [D, S])
    x_d = dram("x", [S, D])
    memT_d = dram("memT", [D, MEM])
    win_d = dram("w_in2", [19, 128, 4096])
    wo_d = dram("w_o", [8, 128, 4096])
    wq_d = dram("w_q_mem", [8, 128, 4096])
    wkv_d = dram("w_kv_mem", [16, 128, 4096])
    wom_d = dram("w_o_mem", [8, 128, 4096])
    wgu_d = dram("w_gate_up", [NFC, 128, 4096])
    wdn_d = dram("w_down", [32, 128, 11 * 256])
    WB = G > 1
    scr = {}
    if WB:
        for nm, shp in (("w_in2", [19, 128, 4096]), ("w_o", [8, 128, 4096]), ("w_q_mem", [8, 128, 4096]),
                        ("w_o_mem", [8, 128, 4096]), ("w_gate_up", [NFC, 128, 4096]), ("w_down", [32, 128, 11 * 256])):
            scr[nm] = nc.dram_tensor(nm + "_bf", shp, BF16, kind="Internal").ap()
    srcs = {"w_in2": win_d, "w_o": wo_d, "w_q_mem": wq_d, "w_o_mem": wom_d, "w_gate_up": wgu_d, "w_down": wdn_d}
    bScr = {}
    cf_d = dram("cf32", [128, NCF])
    cb_d = dram("cb16", [128, NCB])
    lnp_d = dram("lnp", [6, D])
    out_d = dram("out", [S, D], kind="ExternalOutput")
    dbg_d = None
    if debug is not None:
        dbg_d = dram("dbg", [128, 16 * 512], kind="ExternalOutput")

    sb = nc.alloc_sbuf_tensor
    KbT = sb("KbT", [128, 8, S], BF16)
    Vb = sb("Vb", [128, NB, 1024], BF16)
    KaT = sb("KaT", [128, 4, 640], BF16)
    Va = sb("Va", [128, 5, 256], BF16)
    KmT = sb("KmT", [128, 16, MEM], BF16)
    Vm = sb("Vm", [128, 2, D], BF16)
    CF = sb("CF", [128, NCF], F32)
    CB = sb("CB", [128, NCB], BF16)
    RH = sb("RH", [128, 4, D], F32)
    RHb = RH[:, :, :].rearrange("p a b -> p (a b)").bitcast(BF16)
    xT = RHb[:, 0:8192].rearrange("p (c t) -> p c t", t=512)
    QaT = RHb[:, 8192:12288].rearrange("p (c t) -> p c t", t=512)
    QbT = RHb[:, 12288:16384].rearrange("p (c t) -> p c t", t=512)
    memT = RHb[:, 0:4096].rearrange("p (c t) -> p c t", t=MEM)
    RY = sb("RY", [128, 16, 512], BF16)
    WS = [sb("WS%d" % i, [128, 4096], BF16) for i in range(3)]
    RU = sb("RU", [128, 15360], BF16)
    RUf = RU[:, :].bitcast(F32)
    sb_e = [RUf[:, i * 512:(i + 1) * 512] for i in range(3)]
    sb_xc = [RUf[:, 1536 + i * 512:1536 + (i + 1) * 512] for i in range(2)]
    sb_sp = [RU[:, 5120 + i * 512:5120 + (i + 1) * 512] for i in range(2)]
    sb_at = [RU[:, 6144 + i * 512:6144 + (i + 1) * 512] for i in range(2)]
    rms_sq = RUf[:, 3584:4096]
    rms_l = RUf[:, 4096:4608]
    rms_r = RUf[:, 4608:5120]
    swa_S = RUf[:, 5120:6144].rearrange("p (h k) -> p h k", k=256)
    swa_Pn = RU[:, 12288:13312].rearrange("p (h k) -> p h k", k=256)
    swa_PT = RU[:, 13312:14336]
    swa_st = RUf[:, 7168:7232]
    RA = RU[:, 0:8192].rearrange("p (c t) -> p c t", t=512)
    GBg = RUf[:, 0:2048]
    GBb = RUf[:, 2048:4096]
    qmT = [RU[:, 8192 + i * 2048:8192 + (i + 1) * 2048].rearrange("p (c t) -> p c t", t=512)
           for i in range(2)]
    ca_P = [RUf[:, 6144 + i * 256:6144 + (i + 1) * 256] for i in range(2)]
    ca_Pn = RU[:, 13312:13568]
    ca_PT = RU[:, 13568:14592].rearrange("p (m t) -> p m t", t=512)
    sg = RUf[:, 7296:7552].bitcast(BF16)
    ln_st = sb("ln_st", [128, 64], F32)
    PB = [nc.alloc_psum_tensor("PB%d" % i, [128, 512], F32) for i in range(8)]
    bPB = [Buf("PB%d" % i) for i in range(8)]

    bW = [(Buf("W%da" % i), Buf("W%db" % i)) for i in range(3)]
    bRH = [Buf("h%d" % i) for i in range(4)]
    bRY = [Buf("RY%d" % i) for i in range(16)]
    bRA = [Buf("RA%d" % i) for i in range(16)]
    bKbT = {}
    bVb = {}
    bKa = [[Buf("Ka%d_%d" % (k, s)) for s in range(5)] for k in range(4)]
    bVa = [Buf("Va%d" % s) for s in range(5)]
    bKm = Buf("KmT")
    bVm = Buf("Vm")
    bC = Buf("consts")
    bXT = bRH[0:2]
    bQa = bRH[2]
    bQb = bRH[3]

    def B(d, key):
        if key not in d:
            d[key] = Buf(str(key))
        return d[key]

    wslot = [0]

    def wload(src_ap, nk, ncols):
        i = wslot[0] % 3
        wslot[0] += 1
        view = WS[i][:, 0:nk * ncols].rearrange("p (c n) -> p c n", n=ncols)
        ba, bb = bW[i]
        sc.dma("pool", WS[i][:, 0:nk * ncols], src_ap, [], [ba, bb], "w%d" % i)
        return view, ba, bb

    def wload2(nm, t, nk, g):
        if not WB:
            return wload(srcs[nm][t], nk, 256)
        i = wslot[0] % 3
        wslot[0] += 1
        n = nk * 256
        view = WS[i][:, 0:n].rearrange("p (c n) -> p c n", n=256)
        ba, bb = bW[i]
        bs = B(bScr, (nm, t))
        if g == 0:
            sc.dma("pool", WS[i][:, 0:n], srcs[nm][t], [], [ba, bb], "w%d" % i)
            sc.dma("sp", scr[nm][t], WS[i][:, 0:n], [ba, bb], [bs], "ws%d" % i)
        else:
            sc.dma("pool", WS[i][:, 0:n], scr[nm][t], [bs], [ba, bb], "w%d" % i)
        return view, ba, bb

    evac_rr = [0]

    def evac_copy(out, in_, reads, writes, scale=None):
        evac_rr[0] += 1
        if evac_rr[0] % 2 == 0:
            if scale is None:
                sc.op("act", lambda e: e.activation(out=out, in_=in_, func=AF.Copy), reads, writes)
            else:
                sc.op("act", lambda e: e.activation(out=out, in_=in_, func=AF.Copy, scale=scale), reads, writes)
        else:
            if scale is None:
                sc.op("dve", lambda e: e.tensor_copy(out=out, in_=in_), reads, writes)
            else:
                sc.op("dve", lambda e: e.tensor_scalar(out=out, in0=in_, scalar1=scale, scalar2=None,
                                                        op0=ALU.mult), reads, writes)

    def mm(out, lhsT, rhs, start, stop):
        return lambda e: e.matmul(out, lhsT=lhsT, rhs=rhs, start=start, stop=stop)

    def mmx(out, lhsT, rhs, start, stop):
        return lambda e: e.matmul(out, lhsT=lhsT, rhs=rhs, start=start, stop=stop, skip_group_check=True)

    def tp(out, in_, ident):
        return lambda e: e.transpose(out, in_, ident)

    def dbg_dump(ap_list):
        pass

    sc.dma("sp", CF[:, :], cf_d, [], [bC], "cf")
    sc.dma("pool", CB[:, :], cb_d, [], [bC], "cb")
    ident32 = CF[:, CF_ID:CF_ID + 128]
    BDm = CF[:, CF_BD:CF_BD + 128]
    negdist = CF[:, CF_ND:CF_ND + 256]
    negdist0 = CF[:, CF_ND0:CF_ND0 + 256]
    gcol = CF[:, CF_G:CF_G + 16]
    sinks = CF[:, CF_SK:CF_SK + 16]
    mask01 = CF[:, CF_M01:CF_M01 + 128]
    identb = CB[:, CB_ID:CB_ID + 128]
    NegTri = CB[:, CB_NT:CB_NT + 128]
    NegSL = CB[:, CB_NSL:CB_NSL + 128]
    MaskT = CB[:, CB_MK:CB_MK + 128]
    sc.op("dve", lambda e: e.memset(KaT[:, :, 0:128], 0.0), [], [bKa[k][0] for k in range(4)])
    sc.op("dve", lambda e: e.memset(Va[:, 0, :], 0.0), [], [bVa[0]])

    sc.dma("pool", memT, memT_d.rearrange("(c p) m -> p c m", p=128), [], bXT, "xT")
    pbi = [0]

    def next_bank(lo=0, hi=8):
        pbi[0] += 1
        return lo + pbi[0] % (hi - lo)

    for wt in range(8):
        wv, ba, bb = wload(wkv_d[wt], 16, 256)
        for half in range(2):
            c = 2 * wt + half
            bk = next_bank(0, 4)
            sc.pe([mm(PB[bk][:, 0:MEM], wv[:, kc, half * 128:(half + 1) * 128], memT[:, kc, :],
                      kc == 0, kc == 15) for kc in range(16)],
                  reads=[ba, bb, bC] + bXT, writes=[bPB[bk]])
            evac_copy(KmT[:, c, :], PB[bk][:, 0:MEM], [bPB[bk]], [bKm])
    for ct in range(8):
        wv, ba, bb = wload(wkv_d[8 + ct], 16, 256)
        bk = next_bank(0, 4)
        for mb in range(2):
            sc.pe([mm(PB[bk][:, mb * 256:(mb + 1) * 256], memT[:, kc, mb * 128:(mb + 1) * 128], wv[:, kc, :],
                      kc == 0, kc == 15) for kc in range(16)],
                  reads=[ba, bb] + bXT, writes=[bPB[bk]])
        evac_copy(Vm[:, :, ct * 256:(ct + 1) * 256],
                  PB[bk][:, :].rearrange("p (m n) -> p m n", n=256), [bPB[bk]], [bVm])
    sc.barrier()

    def rmsnorm_pair(obank, c):
        ob = PB[obank]
        sc.op("act", lambda e: e.activation(out=rms_sq, in_=ob[:, :], func=AF.Square),
              [bPB[obank]], [b_rms_sq])
        ssb = 5 if obank != 5 else 7
        sc.pe([mm(PB[ssb][:, :], BDm, rms_sq, True, True)], [b_rms_sq, bC], [bPB[ssb]])
        sc.op("act", lambda e: e.activation(out=rms_l, in_=PB[ssb][:, :], func=AF.Ln, scale=1.0 / 64.0,
                                            bias=RMS_EPS), [bPB[ssb]], [b_rms_l])
        sc.op("act", lambda e: e.activation(out=rms_r, in_=rms_l, func=AF.Exp, scale=-0.5),
              [b_rms_l], [b_rms_r])
        sc.op("dve", lambda e: e.scalar_tensor_tensor(out=RY[:, c, :], in0=ob[:, :], scalar=gcol[:, c:c + 1],
                                                      in1=rms_r, op0=ALU.mult, op1=ALU.mult),
              [bPB[obank], b_rms_r, bC], [bRY[c]])

    b_rms_sq, b_rms_l, b_rms_r = Buf("rms_sq"), Buf("rms_l"), Buf("rms_r")
    b_e = [Buf("e%d" % i) for i in range(3)]
    b_xc = [Buf("xc%d" % i) for i in range(2)]
    b_sp = [Buf("sp%d" % i) for i in range(2)]
    b_at = [Buf("at%d" % i) for i in range(2)]
    b_swaS, b_swaPn, b_swaPT, b_swast = Buf("swaS"), Buf("swaPn"), Buf("swaPT"), Buf("swast")
    b_qm = [Buf("qm0"), Buf("qm1")]
    b_caP = [Buf("caP0"), Buf("caP1")]
    b_caPn, b_caPT, b_sg, b_lnst = Buf("caPn"), Buf("caPT"), Buf("sg"), Buf("lnst")
    b_gb = Buf("gb_dummy")

    def stage1(g):
        sc.dma("pool", xT, xT_d[:, g * 512:(g + 1) * 512].rearrange("(c p) t -> p c t", p=128),
               [], bXT, "xT")
        for wt in range(14):
            wv, ba, bb = wload2("w_in2", wt, 16, g)
            for half in range(2):
                c = 2 * wt + half
                bk = next_bank(0, 4)
                sc.pe([mm(PB[bk][:, :], wv[:, kc, half * 128:(half + 1) * 128], xT[:, kc, :], kc == 0, kc == 15)
                       for kc in range(16)], reads=[ba, bb] + bXT, writes=[bPB[bk]])
                if c < 8:
                    evac_copy(QaT[:, c, :], PB[bk][:, :], [bPB[bk]], [bQa], scale=0.125)
                elif c < 12:
                    k = c - 8
                    evac_copy(KaT[:, k, 128:640], PB[bk][:, :], [bPB[bk]], [bKa[k][s] for s in range(1, 5)])
                elif c < 20:
                    evac_copy(QbT[:, c - 12, :], PB[bk][:, :], [bPB[bk]], [bQb], scale=0.125)
                else:
                    evac_copy(KbT[:, c - 20, g * 512:(g + 1) * 512], PB[bk][:, :], [bPB[bk]],
                              [B(bKbT, (c - 20, g))])
        for vt in range(5):
            wv, ba, bb = wload2("w_in2", 14 + vt, 16, g)
            for tp2 in range(2):
                bk = next_bank(0, 4)
                for t2 in range(2):
                    tb = 2 * tp2 + t2
                    sc.pe([mm(PB[bk][:, t2 * 256:(t2 + 1) * 256], xT[:, kc, tb * 128:(tb + 1) * 128], wv[:, kc, :],
                              kc == 0, kc == 15) for kc in range(16)],
                          reads=[ba, bb] + bXT, writes=[bPB[bk]])
                src = PB[bk][:, :].rearrange("p (t n) -> p t n", n=256)
                if vt == 0:
                    evac_copy(Va[:, 1 + 2 * tp2:3 + 2 * tp2, :], src, [bPB[bk]], [bVa[1 + 2 * tp2], bVa[2 + 2 * tp2]])
                else:
                    evac_copy(Vb[:, g * 4 + 2 * tp2:g * 4 + 2 * tp2 + 2, (vt - 1) * 256:vt * 256], src, [bPB[bk]],
                              [B(bVb, (g * 4 + 2 * tp2, vt - 1)), B(bVb, (g * 4 + 2 * tp2 + 1, vt - 1))])
        if debug == "inproj" and g == 0:
            return True
        for kg in range(4):
            ob = [4, 6]
            for qb in range(4):
                gb = 4 * g + qb
                nd = negdist0 if gb == 0 else negdist
                Sb = [0, 1]
                fns = []
                for hh in range(4):
                    h = 4 * kg + hh
                    base = (h % 2) * 64
                    fns.append(mm(PB[Sb[hh % 2]][:, (hh // 2) * 256:(hh // 2 + 1) * 256],
                                  QaT[base:base + 64, h // 2, qb * 128:(qb + 1) * 128],
                                  KaT[base:base + 64, kg, qb * 128:qb * 128 + 256], True, True))
                sc.pe(fns, reads=[bQa, bKa[kg][qb], bKa[kg][qb + 1]], writes=[bPB[0], bPB[1]])
                for hh in range(4):
                    h = 4 * kg + hh
                    sc.op("dve", lambda e, hh=hh, h=h: e.scalar_tensor_tensor(
                        out=swa_S[:, hh, :], in0=nd, scalar=SLOPES[h],
                        in1=PB[Sb[hh % 2]][:, (hh // 2) * 256:(hh // 2 + 1) * 256], op0=ALU.mult, op1=ALU.add),
                        [bPB[Sb[hh % 2]], bC], [b_swaS])
                if debug == "swa1":
                    return True
                rowmax = swa_st[:, 0:4]
                negm = swa_st[:, 4:8]
                dsk = swa_st[:, 8:12]
                rowsum = swa_st[:, 12:16]
                es = swa_st[:, 16:20]
                rden = swa_st[:, 20:24]
                sk = sinks[:, 4 * kg:4 * kg + 4]
                sc.op("dve", lambda e: e.tensor_reduce(out=rowmax, in_=swa_S, axis=AX.X, op=ALU.max),
                      [b_swaS], [b_swast])
                sc.op("dve", lambda e: e.tensor_tensor(out=rowmax, in0=rowmax, in1=sk, op=ALU.max),
                      [b_swast, bC], [b_swast])
                sc.op("dve", lambda e: e.tensor_scalar(out=negm, in0=rowmax, scalar1=-1.0, scalar2=None,
                                                       op0=ALU.mult), [b_swast], [b_swast])
                sc.op("dve", lambda e: e.tensor_tensor(out=dsk, in0=sk, in1=rowmax, op=ALU.subtract),
                      [b_swast, bC], [b_swast])
                for hh in range(4):
                    sc.op("act", lambda e, hh=hh: e.activation(out=swa_S[:, hh, :], in_=swa_S[:, hh, :], func=AF.Exp,
                                                               bias=negm[:, hh:hh + 1]),
                          [b_swaS, b_swast], [b_swaS])
                sc.op("dve", lambda e: e.tensor_reduce(out=rowsum, in_=swa_S, axis=AX.X, op=ALU.add),
                      [b_swaS], [b_swast])
                sc.op("act", lambda e: e.activation(out=es, in_=dsk, func=AF.Exp), [b_swast], [b_swast])
                sc.op("dve", lambda e: e.tensor_tensor(out=rden, in0=rowsum, in1=es, op=ALU.add),
                      [b_swast], [b_swast])
                sc.op("dve", lambda e: e.reciprocal(out=rden, in_=rden), [b_swast], [b_swast])
                for hh in range(4):
                    sc.op("dve", lambda e, hh=hh: e.tensor_scalar(out=swa_Pn[:, hh, :], in0=swa_S[:, hh, :],
                                                                  scalar1=rden[:, hh:hh + 1], scalar2=None,
                                                                  op0=ALU.mult),
                          [b_swaS, b_swast], [b_swaPn])
                if debug == "swa2":
                    return True
                ptps = PB[2][:, :].bitcast(BF16)
                sc.pe([tp(ptps[:, (hh * 2 + kb) * 128:(hh * 2 + kb + 1) * 128],
                          swa_Pn[:, hh, kb * 128:(kb + 1) * 128], identb)
                       for hh in range(4) for kb in range(2)], [b_swaPn, bC], [bPB[2]])
                evac_copy(swa_PT, ptps, [bPB[2]], [b_swaPT])
                for hh in range(4):
                    h = 4 * kg + hh
                    base = (h % 2) * 64
                    obk = ob[hh // 2]
                    sc.pe([mm(PB[obk][base:base + 64, qb * 128:(qb + 1) * 128],
                              Va[:, qb + kb, kg * 64:(kg + 1) * 64],
                              swa_PT[:, (hh * 2 + kb) * 128:(hh * 2 + kb + 1) * 128], kb == 0, kb == 1)
                           for kb in range(2)],
                          [b_swaPT, bVa[qb], bVa[qb + 1]], [bPB[obk]])
                if debug == "swa3":
                    return True
            if debug == "swa4":
                return True
            rmsnorm_pair(ob[0], 2 * kg)
            rmsnorm_pair(ob[1], 2 * kg + 1)
        if g + 1 < G:
            for k in range(4):
                sc.op("dve", lambda e, k=k: e.tensor_copy(out=KaT[:, k, 0:128], in_=KaT[:, k, 512:640]),
                      [bKa[k][4]], [bKa[k][0]])
            sc.op("dve", lambda e: e.tensor_copy(out=Va[:, 0, :], in_=Va[:, 4, :]), [bVa[4]], [bVa[0]])
        if debug == "swa" and g == G - 1:
            return True
        for p in range(8):
            obk = 4 if p % 2 == 0 else 6
            tiles = []
            for kb in range(4 * g + 3, -1, -1):
                for hd in range(2):
                    tiles.append((hd, kb))
            n = len(tiles)
            Cb = [2, 3]

            def geom(t):
                hd, kb = tiles[t]
                j = kb - 4 * g
                c0 = max(j, 0) * 128
                return hd, kb, j, c0, hd * 64

            def stA(t):
                hd, kb, j, c0, base = geom(t)
                zb = t % 2
                fns = [mm(PB[zb][:, c0:512], KbT[base:base + 64, p, kb * 128:(kb + 1) * 128],
                          QbT[base:base + 64, p, c0:512], True, True)]
                sc.pe(fns, [B(bKbT, (p, kb // 4)), bQb, bC], [bPB[zb]])

            def stB(t):
                hd, kb, j, c0, base = geom(t)
                zb = t % 2
                sc.op("act", lambda e: e.activation(out=sb_e[t % 3][:, c0:512], in_=PB[zb][:, c0:512], func=AF.Exp),
                      [bPB[zb]], [b_e[t % 3]])
                if j >= 0:
                    sc.op("dve", lambda e: e.tensor_tensor(out=sb_e[t % 3][:, c0:c0 + 128], in0=sb_e[t % 3][:, c0:c0 + 128],
                                                           in1=mask01, op=ALU.mult), [b_e[t % 3], bC], [b_e[t % 3]])
                sc.op("act", lambda e: e.activation(out=sb_sp[t % 2][:, c0:512], in_=sb_e[t % 3][:, c0:512],
                                                    func=AF.Ln, bias=1.0), [b_e[t % 3]], [b_sp[t % 2]])

            def stC(t):
                hd, kb, j, c0, base = geom(t)
                first = (kb == 4 * g + 3)
                sc.pe([mmx(PB[Cb[hd]][:, c0:512], NegTri, sb_sp[t % 2][:, c0:512], first, True)],
                      [b_sp[t % 2], bC], [bPB[Cb[hd]]])

            def stD(t):
                hd, kb, j, c0, base = geom(t)
                sc.op("act", lambda e: e.activation(out=sb_xc[t % 2][:, c0:512], in_=PB[Cb[hd]][:, c0:512],
                                                    func=AF.Exp), [bPB[Cb[hd]]], [b_xc[t % 2]])

            def stE(t):
                hd, kb, j, c0, base = geom(t)
                last = (kb == 0)
                sc.pe([mmx(PB[Cb[hd]][:, c0:512], NegSL, sb_sp[t % 2][:, c0:512], False, True)],
                      [b_sp[t % 2], bC], [bPB[Cb[hd]]])
                sc.op("dve", lambda e: e.tensor_tensor(out=sb_at[t % 2][:, c0:512], in0=sb_e[t % 3][:, c0:512],
                                                       in1=sb_xc[t % 2][:, c0:512], op=ALU.mult),
                      [b_e[t % 3], b_xc[t % 2]], [b_at[t % 2]])

            def stF(t):
                hd, kb, j, c0, base = geom(t)
                first = (kb == 4 * g + 3)
                last = (kb == 0)
                sc.pe([mmx(PB[obk][base:base + 64, c0:512], Vb[:, kb, (2 * p + hd) * 64:(2 * p + hd + 1) * 64],
                          sb_at[t % 2][:, c0:512], first, last)],
                      [b_at[t % 2], B(bVb, (kb, (2 * p + hd) // 4))], [bPB[obk]])

            for s in range(n + 2):
                if 0 <= s - 2 < n:
                    stE(s - 2)
                    stF(s - 2)
                if 0 <= s - 1 < n:
                    stC(s - 1)
                    stD(s - 1)
                if s < n:
                    stA(s)
                    stB(s)
            rmsnorm_pair(obk, 8 + p)
        if debug == "sb" and g == G - 1:
            return True
        return False

    def ln_load(i):
        sc.dma("pool", GBg, lnp_d[2 * i, :].partition_broadcast(128), [], bRA[0:8], "gbg")
        sc.dma("pool", GBb, lnp_d[2 * i + 1, :].partition_broadcast(128), [], bRA[8:16], "gbb")

    def layer_norm(i, transposes):
        ln_load(i)
        for tb in range(4):
            h = RH[:, tb, :]
            st = ln_st[:, 0:24].rearrange("p (j s) -> p j s", s=6)
            mv = ln_st[:, 24:26]
            l = ln_st[:, 26:27]
            rstd = ln_st[:, 27:28]
            nmr = ln_st[:, 28:29]
            for j in range(4):
                sc.op("dve", lambda e, j=j: e.bn_stats(out=st[:, j, :], in_=RH[:, tb, j * 512:(j + 1) * 512]),
                      [bRH[tb]], [b_lnst])
            sc.op("dve", lambda e: e.bn_aggr(out=mv, in_=ln_st[:, 0:24]), [b_lnst], [b_lnst])
            sc.op("act", lambda e: e.activation(out=l, in_=mv[:, 1:2], func=AF.Ln, bias=LN_EPS), [b_lnst], [b_lnst])
            sc.op("act", lambda e: e.activation(out=rstd, in_=l, func=AF.Exp, scale=-0.5), [b_lnst], [b_lnst])
            sc.op("dve", lambda e: e.scalar_tensor_tensor(out=nmr, in0=mv[:, 0:1], scalar=-1.0, in1=rstd,
                                                          op0=ALU.mult, op1=ALU.mult), [b_lnst], [b_lnst])
            if debug == "s2b":
                return True
            sc.op("act", lambda e: e.activation(out=h, in_=h, func=AF.Identity, scale=rstd, bias=nmr),
                  [bRH[tb], b_lnst], [bRH[tb]])
            sc.op("dve", lambda e: e.tensor_tensor(out=h, in0=h, in1=GBg, op=ALU.mult),
                  [bRH[tb]] + bRA[0:8], [bRH[tb]])
            sc.op("dve", lambda e: e.tensor_tensor(out=h, in0=h, in1=GBb, op=ALU.add),
                  [bRH[tb]] + bRA[8:16], [bRH[tb]])
            if debug == "s2c":
                return True
            if transposes:
                for q4 in range(4):
                    bk = 6 + (q4 % 2)
                    sc.pe([tp(PB[bk][:, i4 * 128:(i4 + 1) * 128],
                              RH[:, tb, (4 * q4 + i4) * 128:(4 * q4 + i4 + 1) * 128], ident32) for i4 in range(4)],
                          [bRH[tb], bC], [bPB[bk]])
                    evac_copy(RY[:, 4 * q4:4 * q4 + 4, tb * 128:(tb + 1) * 128],
                              PB[bk][:, :].rearrange("p (c t) -> p c t", t=128), [bPB[bk]],
                              bRY[4 * q4:4 * q4 + 4])

    def tok_proj(nm, t0, src, bsrc, nk, g, first=True):
        for cb in range(8):
            wv, ba, bb = wload2(nm, t0 + cb, nk, g)
            for tp2 in range(2):
                bk = next_bank(0, 4)
                for t2 in range(2):
                    tb = 2 * tp2 + t2
                    sc.pe([mm(PB[bk][:, t2 * 256:(t2 + 1) * 256], src[:, kc, tb * 128:(tb + 1) * 128], wv[:, kc, :],
                              kc == 0, kc == nk - 1) for kc in range(nk)],
                          reads=[ba, bb] + bsrc, writes=[bPB[bk]])
                hv = RH[:, 2 * tp2:2 * tp2 + 2, cb * 256:(cb + 1) * 256]
                pv = PB[bk][:, :].rearrange("p (t n) -> p t n", n=256)
                if first:
                    sc.op("dve", lambda e, hv=hv, pv=pv: e.scalar_tensor_tensor(
                        out=hv, in0=hv, scalar=ALPHA, in1=pv, op0=ALU.mult, op1=ALU.add),
                        [bPB[bk], bRH[2 * tp2], bRH[2 * tp2 + 1]], [bRH[2 * tp2], bRH[2 * tp2 + 1]])
                else:
                    sc.op("dve", lambda e, hv=hv, pv=pv: e.tensor_tensor(out=hv, in0=hv, in1=pv, op=ALU.add),
                          [bPB[bk], bRH[2 * tp2], bRH[2 * tp2 + 1]], [bRH[2 * tp2], bRH[2 * tp2 + 1]])

    def stage2(g):
        for tb in range(4):
            sc.dma("pool", RH[:, tb, :], x_d[g * 512 + tb * 128:g * 512 + (tb + 1) * 128, :], [], [bRH[tb]],
                   "h%d" % tb)
        if debug == "s2x":
            return True
        tok_proj("w_o", 0, RY, bRY, 16, g)
        if debug == "s2a":
            return True
        if layer_norm(0, True):
            return True
        if debug == "ln1" and g == 0:
            return True
        for hd in range(4):
            q = qmT[hd % 2]
            bq = b_qm[hd % 2]
            for wt in range(2):
                wv, ba, bb = wload2("w_q_mem", hd * 2 + wt, 16, g)
                for half in range(2):
                    cc = 2 * wt + half
                    bk = next_bank(0, 4)
                    sc.pe([mm(PB[bk][:, :], wv[:, kc, half * 128:(half + 1) * 128], RY[:, kc, :], kc == 0, kc == 15)
                           for kc in range(16)], reads=[ba, bb] + bRY, writes=[bPB[bk]])
                    evac_copy(q[:, cc, :], PB[bk][:, :], [bPB[bk]], [bq])
            for tb in range(4):
                sbk = 4 + tb % 2
                P = ca_P[tb % 2]
                bP = b_caP[tb % 2]
                sc.pe([mm(PB[sbk][:, 0:MEM], q[:, cc, tb * 128:(tb + 1) * 128], KmT[:, hd * 4 + cc, :], cc == 0, cc == 3)
                       for cc in range(4)], [bq, bKm], [bPB[sbk]])
                rmax = ln_st[:, 32 + 4 * (tb % 2):33 + 4 * (tb % 2)]
                nm = ln_st[:, 33 + 4 * (tb % 2):34 + 4 * (tb % 2)]
                rs = ln_st[:, 34 + 4 * (tb % 2):35 + 4 * (tb % 2)]
                rr = ln_st[:, 35 + 4 * (tb % 2):36 + 4 * (tb % 2)]
                bst = b_caP[tb % 2]
                sc.op("dve", lambda e: e.tensor_reduce(out=rmax, in_=PB[sbk][:, 0:MEM], axis=AX.X, op=ALU.max),
                      [bPB[sbk]], [bst])
                sc.op("dve", lambda e: e.tensor_scalar(out=nm, in0=rmax, scalar1=-QSCALE_MEM, scalar2=None,
                                                       op0=ALU.mult), [bst], [bst])
                sc.op("act", lambda e: e.activation(out=P, in_=PB[sbk][:, 0:MEM], func=AF.Exp, scale=QSCALE_MEM,
                                                    bias=nm), [bPB[sbk], bst], [bP])
                sc.op("dve", lambda e: e.tensor_reduce(out=rs, in_=P, axis=AX.X, op=ALU.add), [bP], [bP])
                sc.op("dve", lambda e: e.reciprocal(out=rr, in_=rs), [bP], [bP])
                sc.op("dve", lambda e: e.tensor_scalar(out=ca_Pn, in0=P, scalar1=rr, scalar2=None, op0=ALU.mult),
                      [bP], [b_caPn])
                ptps = PB[6 + tb % 2][:, 0:128].bitcast(BF16)
                sc.pe([tp(ptps[:, mb * 128:(mb + 1) * 128], ca_Pn[:, mb * 128:(mb + 1) * 128], identb)
                       for mb in range(2)], [b_caPn, bC], [bPB[6 + tb % 2]])
                evac_copy(ca_PT[:, :, tb * 128:(tb + 1) * 128], ptps.rearrange("p (m t) -> p m t", t=128),
                          [bPB[6 + tb % 2]], [b_caPT])
            for cc in range(4):
                c = hd * 4 + cc
                bk = next_bank(0, 4)
                sc.pe([mm(PB[bk][:, :], Vm[:, mb, c * 128:(c + 1) * 128], ca_PT[:, mb, :], mb == 0, mb == 1)
                       for mb in range(2)], [bVm, b_caPT], [bPB[bk]])
                evac_copy(RA[:, c, :], PB[bk][:, :], [bPB[bk]], [bRA[c]])
        tok_proj("w_o_mem", 0, RA, bRA, 16, g)
        layer_norm(1, True)
        if debug == "ln2" and g == 0:
            return True
        for qd in range(4):
            for j in range(11):
                fc = qd * 11 + j
                wv, ba, bb = wload2("w_gate_up", fc, 16, g)
                bg = next_bank(0, 4)
                sc.pe([mm(PB[bg][:, :], wv[:, kc, 0:128], RY[:, kc, :], kc == 0, kc == 15) for kc in range(16)],
                      reads=[ba] + bRY, writes=[bPB[bg]])
                bu = next_bank(0, 4)
                sc.pe([mm(PB[bu][:, :], wv[:, kc, 128:256], RY[:, kc, :], kc == 0, kc == 15) for kc in range(16)],
                      reads=[bb] + bRY, writes=[bPB[bu]])
                sc.op("act", lambda e: e.activation(out=sg, in_=PB[bg][:, :], func=AF.Silu), [bPB[bg]], [b_sg])
                sc.op("dve", lambda e: e.tensor_tensor(out=RA[:, j, :], in0=sg, in1=PB[bu][:, :], op=ALU.mult),
                      [b_sg, bPB[bu]], [bRA[j]])
            tok_proj("w_down", qd * 8, RA, bRA[0:11], 11, g, first=(qd == 0))
        layer_norm(2, False)
        for tb in range(4):
            sc.dma("sp", out_d[g * 512 + tb * 128:g * 512 + (tb + 1) * 128, :], RH[:, tb, :], [bRH[tb]], [],
                   "o%d" % tb)
        return False

    stop = False
    for g in range(G):
        stop = stage1(g)
        if stop:
            break
        sc.barrier()
        stop = stage2(g)
        if stop:
            break
        sc.barrier()
    if debug is not None:
        sc.barrier()
        dbg_views = {
            "inproj": [QaT, QbT],
            "swa": [RY],
            "sb": [RY],
            "ln1": [RY],
            "ln2": [RY],
        }
        if (debug.startswith("swa") and debug != "swa") or debug.startswith("s2"):
            sc.dma("sp", dbg_d[:, 0:NCF], CF[:, :], [bC], [], "dbg")
        elif debug in ("inproj",):
            sc.dma("sp", dbg_d[:, 0:4096], RH[:, 2:4, :].rearrange("p a b -> p (a b)"), bRH, [], "dbg")
        elif debug in ("swa", "sb", "ln1", "ln2", "full"):
            if debug == "swa":
                sc.dma("sp", dbg_d[:, 0:2048], RY[:, 0:8, :].rearrange("p a b -> p (a b)").bitcast(F32), bRY, [], "dbg")
            else:
                sc.dma("sp", dbg_d[:, 0:4096], RY[:, :, :].rearrange("p a b -> p (a b)").bitcast(F32), bRY, [], "dbg")
            if debug in ("ln1", "ln2"):
                pass
        sc.barrier()
    sc.finish()
    return nc


def _consts(sinks, g_swa, g_sb):
    cf = np.zeros((128, NCF), np.float32)
    cf[:, CF_ID:CF_ID + 128] = np.eye(128, dtype=np.float32)
    bd = np.zeros((128, 128), np.float32)
    bd[:64, :64] = 1.0
    bd[64:, 64:] = 1.0
    cf[:, CF_BD:CF_BD + 128] = bd
    q = np.arange(128)[:, None]
    s = np.arange(128)[None, :]
    prev = np.where(s > q, -(128.0 + q - s), -1e9)
    cur = np.where(s <= q, -(q - s).astype(np.float64), -1e9)
    cf[:, CF_ND:CF_ND + 128] = prev
    cf[:, CF_ND + 128:CF_ND + 256] = cur
    cf[:, CF_ND0:CF_ND0 + 128] = -1e9
    cf[:, CF_ND0 + 128:CF_ND0 + 256] = cur
    gcat = np.concatenate([g_swa.reshape(-1), g_sb.reshape(-1)]).astype(np.float32)
    cf[:, CF_G:CF_G + 16] = gcat.reshape(16, 128).T
    cf[:, CF_SK:CF_SK + 16] = np.broadcast_to(sinks.reshape(1, 16), (128, 16))
    cf[:, CF_M01:CF_M01 + 128] = np.where(q < s, 1.0, 0.0)
    cb = np.zeros((128, NCB), np.float32)
    cb[:, CB_ID:CB_ID + 128] = np.eye(128, dtype=np.float32)
    j = np.arange(128)[:, None]
    sidx = np.arange(128)[None, :]
    cb[:, CB_NT:CB_NT + 128] = np.where(j >= sidx, -1.0, 0.0)
    cb[:, CB_NSL:CB_NSL + 128] = np.where(j < sidx, -1.0, 0.0)
    cb[:, CB_MK:CB_MK + 128] = np.where(j >= sidx, NEG, 0.0)
    return cf, cb


def _tile(W, nk, col0s, r0=0):
    out = np.empty((len(col0s), 128, nk * 256), np.float32)
    for t, c0 in enumerate(col0s):
        out[t] = W[r0:r0 + nk * 128, c0:c0 + 256].reshape(nk, 128, 256).transpose(1, 0, 2).reshape(128, nk * 256)
    return out


def _w_in2(w_in):
    w = w_in
    qa = w[:, 0:1024]
    ka = w[:, 1024:1280]
    va = w[:, 1280:1536]
    qb = w[:, 1536:2560]
    kb = w[:, 2560:3584]
    vb = w[:, 3584:4608]
    kad = np.concatenate([np.concatenate([ka[:, k * 64:(k + 1) * 64]] * 2, axis=1) for k in range(4)], axis=1)
    return np.ascontiguousarray(np.concatenate([qa, kad, qb, kb, va, vb], axis=1))


_NC_CACHE = {}


def _prep_shared(inp):
    f = lambda a: np.ascontiguousarray(np.asarray(a, dtype=np.float32))
    cf, cb = _consts(f(inp["sinks"])[0], f(inp["g_swa"])[0], f(inp["g_sb"])[0])
    lnp = np.stack([f(inp[k])[0] for k in ("ln1_g", "ln1_b", "ln2_g", "ln2_b", "ln3_g", "ln3_b")])
    wgu = f(inp["w_gate_up"])[0]
    wgu2 = np.concatenate([wgu[:, :DFF].reshape(D, NFC, 128), wgu[:, DFF:].reshape(D, NFC, 128)], axis=2)
    wgu2 = wgu2.reshape(D, NFC * 256)
    wdn = f(inp["w_down"])[0]
    c8 = [cb * 256 for cb in range(8)]
    return {
        "w_in2": _tile(_w_in2(f(inp["w_in"])[0]), 16, [t * 256 for t in range(19)]),
        "w_o": _tile(f(inp["w_o"])[0], 16, c8),
        "w_q_mem": _tile(f(inp["w_q_mem"])[0], 16, c8),
        "w_kv_mem": _tile(f(inp["w_kv_mem"])[0], 16, [t * 256 for t in range(16)]),
        "w_o_mem": _tile(f(inp["w_o_mem"])[0], 16, c8),
        "w_gate_up": _tile(wgu2, 16, [t * 256 for t in range(NFC)]),
        "w_down": np.concatenate([_tile(wdn, 11, c8, r0=qd * 11 * 128) for qd in range(4)], axis=0),
        "cf32": cf, "cb16": cb, "lnp": np.ascontiguousarray(lnp),
    }


def kernel(**inputs):
    x = np.asarray(inputs["x"], dtype=np.float32)
    mem = np.asarray(inputs["mem"], dtype=np.float32)
    Bn, S, _ = x.shape
    shared = _prep_shared(inputs)
    if S not in _NC_CACHE:
        _NC_CACHE[S] = build(S)
    nc = _NC_CACHE[S]
    in_maps = []
    for b in range(Bn):
        m = dict(shared)
        m["x"] = np.ascontiguousarray(x[b])
        m["xT"] = np.ascontiguousarray(x[b].T)
        m["memT"] = np.ascontiguousarray(mem[b].T)
        in_maps.append(m)
    res = run_bass_kernel_spmd(nc, in_maps, core_ids=list(range(Bn)))
    return np.stack([r["out"] for r in res.results], axis=0).astype(np.float32)
```

```python
import math
import numpy as np
import concourse.bass as bass
import concourse.mybir as mybir
from concourse.bass_utils import run_bass_kernel_spmd

AF = mybir.ActivationFunctionType
ALU = mybir.AluOpType
AX = mybir.AxisListType
F32 = mybir.dt.float32
BF16 = mybir.dt.bfloat16

D = 2048
DFF = 5632
NFC = DFF // 128
ALPHA = 2.0 ** 0.25
LN_EPS = 1e-5
RMS_EPS = 1e-6
MEM = 256
NEG = -30000.0
SLOPES = [2.0 ** (-8.0 * (h + 1) / 16) for h in range(16)]
QSCALE_MEM = 1.0 / math.sqrt(512.0)

CF_ID, CF_BD, CF_ND, CF_ND0, CF_G, CF_SK, CF_M01, NCF = 0, 128, 256, 512, 768, 784, 800, 928
CB_ID, CB_NT, CB_NSL, CB_MK, NCB = 0, 128, 256, 384, 512

W_IN2 = 4864


class Buf:
    __slots__ = ("name", "w", "r")

    def __init__(self, name):
        self.name = name
        self.w = None
        self.r = {}


class _Eng:
    def __init__(self, name, h, sem):
        self.name, self.h, self.sem, self.cnt, self.waited = name, h, sem, 0, {}


class _Stream:
    def __init__(self, name, sem):
        self.name, self.sem, self.cnt = name, sem, 0


class Sched:
    def __init__(self, nc):
        self.nc = nc
        self.E = {}
        for n, h in (("pe", nc.tensor), ("act", nc.scalar), ("dve", nc.vector)):
            self.E[n] = _Eng(n, h, nc.alloc_semaphore("s_" + n))
        for n, h in (("pool", nc.gpsimd), ("sp", nc.sync)):
            self.E[n] = _Eng(n, h, None)
        self.streams = {}
        self.sems = {}
        for n in ("pe", "act", "dve"):
            self.sems[n] = self.E[n].sem

    def stream(self, name):
        if name not in self.streams:
            st = _Stream(name, self.nc.alloc_semaphore("d_" + name))
            self.streams[name] = st
            self.sems["d_" + name] = st.sem
        return self.streams[name]

    def _deps(self, reads, writes):
        deps = {}

        def add(kv):
            k, v = kv
            if deps.get(k, 0) < v:
                deps[k] = v
        for b in reads:
            if b.w is not None:
                add(b.w)
        for b in writes:
            if b.w is not None:
                add(b.w)
            for kv in b.r.items():
                add(kv)
        return deps

    def _wait(self, eng, deps):
        for k, v in deps.items():
            if k == eng.name and eng.name == "pe":
                continue
            if eng.waited.get(k, 0) < v:
                eng.h.wait_ge(self.sems[k], v)
                eng.waited[k] = v

    def _mark(self, key, val, reads, writes):
        for b in reads:
            if b.r.get(key, 0) < val:
                b.r[key] = val
        for b in writes:
            b.w = (key, val)
            b.r = {}

    def op(self, en, fn, reads=(), writes=()):
        eng = self.E[en]
        self._wait(eng, self._deps(reads, writes))
        ins = fn(eng.h)
        eng.cnt += 1
        ins.then_inc(eng.sem, 1)
        self._mark(en, eng.cnt, reads, writes)

    def pe(self, fns, reads=(), writes=()):
        eng = self.E["pe"]
        self._wait(eng, self._deps(reads, writes))
        ins = None
        for fn in fns:
            ins = fn(eng.h)
        eng.cnt += 1
        ins.then_inc(eng.sem, 1)
        self._mark("pe", eng.cnt, reads, writes)

    def dma(self, qn, out, in_, reads, writes, stream):
        q = self.E[qn]
        st = self.stream(stream)
        key = "d_" + stream
        deps = self._deps(reads, writes)
        if st.cnt > 0:
            deps[key] = max(deps.get(key, 0), st.cnt)
        self._wait(q, deps)
        q.h.dma_start(out=out, in_=in_).then_inc(st.sem, 16)
        st.cnt += 16
        self._mark(key, st.cnt, reads, writes)

    def barrier(self):
        tgt = {n: self.E[n].cnt for n in ("pe", "act", "dve")}
        for st in self.streams.values():
            tgt["d_" + st.name] = st.cnt
        for e in self.E.values():
            for k, v in tgt.items():
                if k == e.name or v == 0:
                    continue
                if e.waited.get(k, 0) < v:
                    e.h.wait_ge(self.sems[k], v)
                    e.waited[k] = v

    def finish(self):
        sp = self.E["sp"]
        for st in self.streams.values():
            if st.cnt and sp.waited.get("d_" + st.name, 0) < st.cnt:
                sp.h.wait_ge(st.sem, st.cnt)
                sp.waited["d_" + st.name] = st.cnt


def build(S=2048, debug=None):
    G = S // 512
    NB = S // 128
    nc = bass.Bass("TRN2", target_bir_lowering=False)
    sc = Sched(nc)

    def dram(name, shape, kind="ExternalInput"):
        return nc.dram_tensor(name, shape, F32, kind=kind).ap()

    xT_d = dram("xT", [D, S])
    x_d = dram("x", [S, D])
    memT_d = dram("memT", [D, MEM])
    win_d = dram("w_in2", [19, 128, 4096])
    wo_d = dram("w_o", [8, 128, 4096])
    wq_d = dram("w_q_mem", [8, 128, 4096])
    wkv_d = dram("w_kv_mem", [16, 128, 4096])
    wom_d = dram("w_o_mem", [8, 128, 4096])
    wgu_d = dram("w_gate_up", [NFC, 128, 4096])
    wdn_d = dram("w_down", [32, 128, 11 * 256])
    cf_d = dram("cf32", [128, NCF])
    cb_d = dram("cb16", [128, NCB])
    lnp_d = dram("lnp", [6, D])
    out_d = dram("out", [S, D], kind="ExternalOutput")
    dbg_d = None
    if debug is not None:
        dbg_d = dram("dbg", [128, 16 * 512], kind="ExternalOutput")

    sb = nc.alloc_sbuf_tensor
    KbT = sb("KbT", [128, 8, S], BF16)
    Vb = sb("Vb", [128, NB, 1024], BF16)
    KaT = sb("KaT", [128, 4, 640], BF16)
    Va = sb("Va", [128, 5, 256], BF16)
    KmT = sb("KmT", [128, 16, MEM], BF16)
    Vm = sb("Vm", [128, 2, D], BF16)
    CF = sb("CF", [128, NCF], F32)
    CB = sb("CB", [128, NCB], BF16)
    RH = sb("RH", [128, 4, D], F32)
    RHb = RH[:, :, :].rearrange("p a b -> p (a b)").bitcast(BF16)
    xT = RHb[:, 0:8192].rearrange("p (c t) -> p c t", t=512)
    QaT = RHb[:, 8192:12288].rearrange("p (c t) -> p c t", t=512)
    QbT = RHb[:, 12288:16384].rearrange("p (c t) -> p c t", t=512)
    memT = RHb[:, 0:4096].rearrange("p (c t) -> p c t", t=MEM)
    RY = sb("RY", [128, 16, 512], BF16)
    WS = [sb("WS%d" % i, [128, 4096], BF16) for i in range(3)]
    RU = sb("RU", [128, 15360], BF16)
    RUf = RU[:, :].bitcast(F32)
    sb_e = [RUf[:, i * 512:(i + 1) * 512] for i in range(3)]
    sb_xc = [RUf[:, 1536 + i * 512:1536 + (i + 1) * 512] for i in range(2)]
    sb_sp = [RU[:, 5120 + i * 512:5120 + (i + 1) * 512] for i in range(2)]
    sb_sp.append(RU[:, 14464:14976])
    sb_at = [RU[:, 6144 + i * 512:6144 + (i + 1) * 512] for i in range(2)]
    rms_sq = RUf[:, 3584:4096]
    rms_l = RUf[:, 4096:4608]
    rms_r = RUf[:, 4608:5120]
    swa_S = RUf[:, 5120:6144].rearrange("p (h k) -> p h k", k=256)
    swa_Pn = RU[:, 12288:13312].rearrange("p (h k) -> p h k", k=256)
    swa_PT = RU[:, 13312:14336]
    swa_st = RUf[:, 7168:7232]
    RA = RU[:, 0:8192].rearrange("p (c t) -> p c t", t=512)
    GBg = RUf[:, 0:2048]
    GBb = RUf[:, 2048:4096]
    qmT = [RU[:, 8192 + i * 2048:8192 + (i + 1) * 2048].rearrange("p (c t) -> p c t", t=512)
           for i in range(2)]
    ca_P = [RUf[:, 6144 + i * 256:6144 + (i + 1) * 256] for i in range(2)]
    ca_Pn = RU[:, 13312:13568]
    ca_PT = RU[:, 13568:14592].rearrange("p (m t) -> p m t", t=512)
    sg = RUf[:, 7296:7552].bitcast(BF16)
    ln_st = sb("ln_st", [128, 64], F32)
    PB = [nc.alloc_psum_tensor("PB%d" % i, [128, 512], F32) for i in range(8)]
    bPB = [Buf("PB%d" % i) for i in range(8)]

    bW = [(Buf("W%da" % i), Buf("W%db" % i)) for i in range(3)]
    bRH = [Buf("h%d" % i) for i in range(4)]
    bRY = [Buf("RY%d" % i) for i in range(16)]
    bRA = [Buf("RA%d" % i) for i in range(16)]
    bKbT = {}
    bVb = {}
    bKa = [[Buf("Ka%d_%d" % (k, s)) for s in range(5)] for k in range(4)]
    bVa = [Buf("Va%d" % s) for s in range(5)]
    bKm = Buf("KmT")
    bVm = Buf("Vm")
    bC = Buf("consts")
    bXT = bRH[0:2]
    bQa = bRH[2]
    bQb = bRH[3]

    def B(d, key):
        if key not in d:
            d[key] = Buf(str(key))
        return d[key]

    wslot = [0]

    def wload(src_ap, nk, ncols):
        i = wslot[0] % 3
        wslot[0] += 1
        view = WS[i][:, 0:nk * ncols].rearrange("p (c n) -> p c n", n=ncols)
        ba, bb = bW[i]
        sc.dma("pool", WS[i][:, 0:nk * ncols], src_ap, [], [ba, bb], "w%d" % i)
        return view, ba, bb

    evac_rr = [0]

    def evac_copy(out, in_, reads, writes, scale=None):
        evac_rr[0] += 1
        if evac_rr[0] % 2 == 0:
            if scale is None:
                sc.op("act", lambda e: e.activation(out=out, in_=in_, func=AF.Copy), reads, writes)
            else:
                sc.op("act", lambda e: e.activation(out=out, in_=in_, func=AF.Copy, scale=scale), reads, writes)
        else:
            if scale is None:
                sc.op("dve", lambda e: e.tensor_copy(out=out, in_=in_), reads, writes)
            else:
                sc.op("dve", lambda e: e.tensor_scalar(out=out, in0=in_, scalar1=scale, scalar2=None,
                                                        op0=ALU.mult), reads, writes)

    def mm(out, lhsT, rhs, start, stop):
        return lambda e: e.matmul(out, lhsT=lhsT, rhs=rhs, start=start, stop=stop)

    def mmx(out, lhsT, rhs, start, stop):
        return lambda e: e.matmul(out, lhsT=lhsT, rhs=rhs, start=start, stop=stop, skip_group_check=True)

    def tp(out, in_, ident):
        return lambda e: e.transpose(out, in_, ident)

    def dbg_dump(ap_list):
        pass

    sc.dma("sp", CF[:, :], cf_d, [], [bC], "cf")
    sc.dma("pool", CB[:, :], cb_d, [], [bC], "cb")
    ident32 = CF[:, CF_ID:CF_ID + 128]
    BDm = CF[:, CF_BD:CF_BD + 128]
    negdist = CF[:, CF_ND:CF_ND + 256]
    negdist0 = CF[:, CF_ND0:CF_ND0 + 256]
    gcol = CF[:, CF_G:CF_G + 16]
    sinks = CF[:, CF_SK:CF_SK + 16]
    mask01 = CF[:, CF_M01:CF_M01 + 128]
    identb = CB[:, CB_ID:CB_ID + 128]
    NegTri = CB[:, CB_NT:CB_NT + 128]
    NegSL = CB[:, CB_NSL:CB_NSL + 128]
    MaskT = CB[:, CB_MK:CB_MK + 128]
    sc.op("dve", lambda e: e.memset(KaT[:, :, 0:128], 0.0), [], [bKa[k][0] for k in range(4)])
    sc.op("dve", lambda e: e.memset(Va[:, 0, :], 0.0), [], [bVa[0]])

    sc.dma("pool", memT, memT_d.rearrange("(c p) m -> p c m", p=128), [], bXT, "xT")
    pbi = [0]

    def next_bank(lo=0, hi=8):
        pbi[0] += 1
        return lo + pbi[0] % (hi - lo)

    for wt in range(8):
        wv, ba, bb = wload(wkv_d[wt], 16, 256)
        for half in range(2):
            c = 2 * wt + half
            bk = next_bank(0, 4)
            sc.pe([mm(PB[bk][:, 0:MEM], wv[:, kc, half * 128:(half + 1) * 128], memT[:, kc, :],
                      kc == 0, kc == 15) for kc in range(16)],
                  reads=[ba, bb, bC] + bXT, writes=[bPB[bk]])
            evac_copy(KmT[:, c, :], PB[bk][:, 0:MEM], [bPB[bk]], [bKm])
    for ct in range(8):
        wv, ba, bb = wload(wkv_d[8 + ct], 16, 256)
        bk = next_bank(0, 4)
        for mb in range(2):
            sc.pe([mm(PB[bk][:, mb * 256:(mb + 1) * 256], memT[:, kc, mb * 128:(mb + 1) * 128], wv[:, kc, :],
                      kc == 0, kc == 15) for kc in range(16)],
                  reads=[ba, bb] + bXT, writes=[bPB[bk]])
        evac_copy(Vm[:, :, ct * 256:(ct + 1) * 256],
                  PB[bk][:, :].rearrange("p (m n) -> p m n", n=256), [bPB[bk]], [bVm])
    sc.barrier()

    def rmsnorm_pair(obank, c):
        ob = PB[obank]
        sc.op("act", lambda e: e.activation(out=rms_sq, in_=ob[:, :], func=AF.Square),
              [bPB[obank]], [b_rms_sq])
        ssb = 5 if obank != 5 else 7
        sc.pe([mm(PB[ssb][:, :], BDm, rms_sq, True, True)], [b_rms_sq, bC], [bPB[ssb]])
        sc.op("act", lambda e: e.activation(out=rms_l, in_=PB[ssb][:, :], func=AF.Ln, scale=1.0 / 64.0,
                                            bias=RMS_EPS), [bPB[ssb]], [b_rms_l])
        sc.op("act", lambda e: e.activation(out=rms_r, in_=rms_l, func=AF.Exp, scale=-0.5),
              [b_rms_l], [b_rms_r])
        sc.op("dve", lambda e: e.scalar_tensor_tensor(out=RY[:, c, :], in0=ob[:, :], scalar=gcol[:, c:c + 1],
                                                      in1=rms_r, op0=ALU.mult, op1=ALU.mult),
              [bPB[obank], b_rms_r, bC], [bRY[c]])

    b_rms_sq, b_rms_l, b_rms_r = Buf("rms_sq"), Buf("rms_l"), Buf("rms_r")
    b_e = [Buf("e%d" % i) for i in range(3)]
    b_xc = [Buf("xc%d" % i) for i in range(2)]
    b_sp = [Buf("sp%d" % i) for i in range(3)]
    b_at = [Buf("at%d" % i) for i in range(2)]
    b_swaS, b_swaPn, b_swaPT, b_swast = Buf("swaS"), Buf("swaPn"), Buf("swaPT"), Buf("swast")
    b_qm = [Buf("qm0"), Buf("qm1")]
    b_caP = [Buf("caP0"), Buf("caP1")]
    b_caPn, b_caPT, b_sg, b_lnst = Buf("caPn"), Buf("caPT"), Buf("sg"), Buf("lnst")
    b_gb = Buf("gb_dummy")

    def stage1(g):
        sc.dma("pool", xT, xT_d[:, g * 512:(g + 1) * 512].rearrange("(c p) t -> p c t", p=128),
               [], bXT, "xT")
        for wt in range(14):
            wv, ba, bb = wload(win_d[wt], 16, 256)
            for half in range(2):
                c = 2 * wt + half
                bk = next_bank(0, 4)
                sc.pe([mm(PB[bk][:, :], wv[:, kc, half * 128:(half + 1) * 128], xT[:, kc, :], kc == 0, kc == 15)
                       for kc in range(16)], reads=[ba, bb] + bXT, writes=[bPB[bk]])
                if c < 8:
                    evac_copy(QaT[:, c, :], PB[bk][:, :], [bPB[bk]], [bQa], scale=0.125)
                elif c < 12:
                    k = c - 8
                    evac_copy(KaT[:, k, 128:640], PB[bk][:, :], [bPB[bk]], [bKa[k][s] for s in range(1, 5)])
                elif c < 20:
                    evac_copy(QbT[:, c - 12, :], PB[bk][:, :], [bPB[bk]], [bQb], scale=0.125)
                else:
                    evac_copy(KbT[:, c - 20, g * 512:(g + 1) * 512], PB[bk][:, :], [bPB[bk]],
                              [B(bKbT, (c - 20, g))])
        for vt in range(5):
            wv, ba, bb = wload(win_d[14 + vt], 16, 256)
            for tp2 in range(2):
                bk = next_bank(0, 4)
                for t2 in range(2):
                    tb = 2 * tp2 + t2
                    sc.pe([mm(PB[bk][:, t2 * 256:(t2 + 1) * 256], xT[:, kc, tb * 128:(tb + 1) * 128], wv[:, kc, :],
                              kc == 0, kc == 15) for kc in range(16)],
                          reads=[ba, bb] + bXT, writes=[bPB[bk]])
                src = PB[bk][:, :].rearrange("p (t n) -> p t n", n=256)
                if vt == 0:
                    evac_copy(Va[:, 1 + 2 * tp2:3 + 2 * tp2, :], src, [bPB[bk]], [bVa[1 + 2 * tp2], bVa[2 + 2 * tp2]])
                else:
                    evac_copy(Vb[:, g * 4 + 2 * tp2:g * 4 + 2 * tp2 + 2, (vt - 1) * 256:vt * 256], src, [bPB[bk]],
                              [B(bVb, (g * 4 + 2 * tp2, vt - 1)), B(bVb, (g * 4 + 2 * tp2 + 1, vt - 1))])
        if debug == "inproj" and g == 0:
            return True
        for kg in range(4):
            ob = [4, 6]
            for qb in range(4):
                gb = 4 * g + qb
                nd = negdist0 if gb == 0 else negdist
                Sb = [0, 1]
                fns = []
                for hh in range(4):
                    h = 4 * kg + hh
                    base = (h % 2) * 64
                    fns.append(mm(PB[Sb[hh % 2]][:, (hh // 2) * 256:(hh // 2 + 1) * 256],
                                  QaT[base:base + 64, h // 2, qb * 128:(qb + 1) * 128],
                                  KaT[base:base + 64, kg, qb * 128:qb * 128 + 256], True, True))
                sc.pe(fns, reads=[bQa, bKa[kg][qb], bKa[kg][qb + 1]], writes=[bPB[0], bPB[1]])
                for hh in range(4):
                    h = 4 * kg + hh
                    sc.op("dve", lambda e, hh=hh, h=h: e.scalar_tensor_tensor(
                        out=swa_S[:, hh, :], in0=nd, scalar=SLOPES[h],
                        in1=PB[Sb[hh % 2]][:, (hh // 2) * 256:(hh // 2 + 1) * 256], op0=ALU.mult, op1=ALU.add),
                        [bPB[Sb[hh % 2]], bC], [b_swaS])
                if debug == "swa1":
                    return True
                rowmax = swa_st[:, 0:4]
                negm = swa_st[:, 4:8]
                dsk = swa_st[:, 8:12]
                rowsum = swa_st[:, 12:16]
                es = swa_st[:, 16:20]
                rden = swa_st[:, 20:24]
                sk = sinks[:, 4 * kg:4 * kg + 4]
                sc.op("dve", lambda e: e.tensor_reduce(out=rowmax, in_=swa_S, axis=AX.X, op=ALU.max),
                      [b_swaS], [b_swast])
                sc.op("dve", lambda e: e.tensor_tensor(out=rowmax, in0=rowmax, in1=sk, op=ALU.max),
                      [b_swast, bC], [b_swast])
                sc.op("dve", lambda e: e.tensor_scalar(out=negm, in0=rowmax, scalar1=-1.0, scalar2=None,
                                                       op0=ALU.mult), [b_swast], [b_swast])
                sc.op("dve", lambda e: e.tensor_tensor(out=dsk, in0=sk, in1=rowmax, op=ALU.subtract),
                      [b_swast, bC], [b_swast])
                for hh in range(4):
                    sc.op("act", lambda e, hh=hh: e.activation(out=swa_S[:, hh, :], in_=swa_S[:, hh, :], func=AF.Exp,
                                                               bias=negm[:, hh:hh + 1]),
                          [b_swaS, b_swast], [b_swaS])
                sc.op("dve", lambda e: e.tensor_reduce(out=rowsum, in_=swa_S, axis=AX.X, op=ALU.add),
                      [b_swaS], [b_swast])
                sc.op("act", lambda e: e.activation(out=es, in_=dsk, func=AF.Exp), [b_swast], [b_swast])
                sc.op("dve", lambda e: e.tensor_tensor(out=rden, in0=rowsum, in1=es, op=ALU.add),
                      [b_swast], [b_swast])
                sc.op("dve", lambda e: e.reciprocal(out=rden, in_=rden), [b_swast], [b_swast])
                for hh in range(4):
                    sc.op("dve", lambda e, hh=hh: e.tensor_scalar(out=swa_Pn[:, hh, :], in0=swa_S[:, hh, :],
                                                                  scalar1=rden[:, hh:hh + 1], scalar2=None,
                                                                  op0=ALU.mult),
                          [b_swaS, b_swast], [b_swaPn])
                if debug == "swa2":
                    return True
                ptps = PB[2][:, :].bitcast(BF16)
                sc.pe([tp(ptps[:, (hh * 2 + kb) * 128:(hh * 2 + kb + 1) * 128],
                          swa_Pn[:, hh, kb * 128:(kb + 1) * 128], identb)
                       for hh in range(4) for kb in range(2)], [b_swaPn, bC], [bPB[2]])
                evac_copy(swa_PT, ptps, [bPB[2]], [b_swaPT])
                for hh in range(4):
                    h = 4 * kg + hh
                    base = (h % 2) * 64
                    obk = ob[hh // 2]
                    sc.pe([mm(PB[obk][base:base + 64, qb * 128:(qb + 1) * 128],
                              Va[:, qb + kb, kg * 64:(kg + 1) * 64],
                              swa_PT[:, (hh * 2 + kb) * 128:(hh * 2 + kb + 1) * 128], kb == 0, kb == 1)
                           for kb in range(2)],
                          [b_swaPT, bVa[qb], bVa[qb + 1]], [bPB[obk]])
                if debug == "swa3":
                    return True
            if debug == "swa4":
                return True
            rmsnorm_pair(ob[0], 2 * kg)
            rmsnorm_pair(ob[1], 2 * kg + 1)
        if g + 1 < G:
            for k in range(4):
                sc.op("dve", lambda e, k=k: e.tensor_copy(out=KaT[:, k, 0:128], in_=KaT[:, k, 512:640]),
                      [bKa[k][4]], [bKa[k][0]])
            sc.op("dve", lambda e: e.tensor_copy(out=Va[:, 0, :], in_=Va[:, 4, :]), [bVa[4]], [bVa[0]])
        if debug == "swa" and g == G - 1:
            return True
        for p in range(8):
            obk = 4 if p % 2 == 0 else 6
            tiles = []
            for kb in range(4 * g + 3, -1, -1):
                for hd in range(2):
                    tiles.append((hd, kb))
            n = len(tiles)
            Cb = [2, 3]

            def geom(t):
                hd, kb = tiles[t]
                j = kb - 4 * g
                c0 = max(j, 0) * 128
                return hd, kb, j, c0, hd * 64

            def stA(t):
                hd, kb, j, c0, base = geom(t)
                zb = t % 2
                fns = [mm(PB[zb][:, c0:512], KbT[base:base + 64, p, kb * 128:(kb + 1) * 128],
                          QbT[base:base + 64, p, c0:512], True, True)]
                sc.pe(fns, [B(bKbT, (p, kb // 4)), bQb, bC], [bPB[zb]])

            def stB(t):
                hd, kb, j, c0, base = geom(t)
                zb = t % 2
                sc.op("act", lambda e: e.activation(out=sb_e[t % 3][:, c0:512], in_=PB[zb][:, c0:512], func=AF.Exp),
                      [bPB[zb]], [b_e[t % 3]])
                if j >= 0:
                    sc.op("dve", lambda e: e.tensor_tensor(out=sb_e[t % 3][:, c0:c0 + 128], in0=sb_e[t % 3][:, c0:c0 + 128],
                                                           in1=mask01, op=ALU.mult), [b_e[t % 3], bC], [b_e[t % 3]])
                sc.op("act", lambda e: e.activation(out=sb_sp[t % 3][:, c0:512], in_=sb_e[t % 3][:, c0:512],
                                                    func=AF.Ln, bias=1.0), [b_e[t % 3]], [b_sp[t % 3]])

            def stC(t):
                hd, kb, j, c0, base = geom(t)
                first = (kb == 4 * g + 3)
                sc.pe([mmx(PB[Cb[hd]][:, c0:512], NegTri, sb_sp[t % 3][:, c0:512], first, True)],
                      [b_sp[t % 3], bC], [bPB[Cb[hd]]])

            def stD(t):
                hd, kb, j, c0, base = geom(t)
                sc.op("act", lambda e: e.activation(out=sb_xc[t % 2][:, c0:512], in_=PB[Cb[hd]][:, c0:512],
                                                    func=AF.Exp), [bPB[Cb[hd]]], [b_xc[t % 2]])

            def stE(t):
                hd, kb, j, c0, base = geom(t)
                last = (kb == 0)
                sc.pe([mmx(PB[Cb[hd]][:, c0:512], NegSL, sb_sp[t % 3][:, c0:512], False, True)],
                      [b_sp[t % 3], bC], [bPB[Cb[hd]]])

            def stE2(t):
                hd, kb, j, c0, base = geom(t)
                sc.op("dve", lambda e: e.tensor_tensor(out=sb_at[t % 2][:, c0:512], in0=sb_e[t % 3][:, c0:512],
                                                       in1=sb_xc[t % 2][:, c0:512], op=ALU.mult),
                      [b_e[t % 3], b_xc[t % 2]], [b_at[t % 2]])

            def stF(t):
                hd, kb, j, c0, base = geom(t)
                first = (kb == 4 * g + 3)
                last = (kb == 0)
                sc.pe([mmx(PB[obk][base:base + 64, c0:512], Vb[:, kb, (2 * p + hd) * 64:(2 * p + hd + 1) * 64],
                          sb_at[t % 2][:, c0:512], first, last)],
                      [b_at[t % 2], B(bVb, (kb, (2 * p + hd) // 4))], [bPB[obk]])

            stA(0)
            for s in range(n + 2):
                if 0 <= s - 1 < n:
                    stC(s - 1)
                if s + 1 < n:
                    stA(s + 1)
                if 0 <= s - 2 < n:
                    stE(s - 2)
                    stE2(s - 2)
                    stF(s - 2)
                if s < n:
                    stB(s)
                if 0 <= s - 1 < n:
                    stD(s - 1)
            rmsnorm_pair(obk, 8 + p)
        if debug == "sb" and g == G - 1:
            return True
        return False

    def ln_load(i):
        sc.dma("pool", GBg, lnp_d[2 * i, :].partition_broadcast(128), [], bRA[0:8], "gbg")
        sc.dma("pool", GBb, lnp_d[2 * i + 1, :].partition_broadcast(128), [], bRA[8:16], "gbb")

    def layer_norm(i, transposes):
        ln_load(i)
        for tb in range(4):
            h = RH[:, tb, :]
            st = ln_st[:, 0:24].rearrange("p (j s) -> p j s", s=6)
            mv = ln_st[:, 24:26]
            l = ln_st[:, 26:27]
            rstd = ln_st[:, 27:28]
            nmr = ln_st[:, 28:29]
            for j in range(4):
                sc.op("dve", lambda e, j=j: e.bn_stats(out=st[:, j, :], in_=RH[:, tb, j * 512:(j + 1) * 512]),
                      [bRH[tb]], [b_lnst])
            sc.op("dve", lambda e: e.bn_aggr(out=mv, in_=ln_st[:, 0:24]), [b_lnst], [b_lnst])
            sc.op("act", lambda e: e.activation(out=l, in_=mv[:, 1:2], func=AF.Ln, bias=LN_EPS), [b_lnst], [b_lnst])
            sc.op("act", lambda e: e.activation(out=rstd, in_=l, func=AF.Exp, scale=-0.5), [b_lnst], [b_lnst])
            sc.op("dve", lambda e: e.scalar_tensor_tensor(out=nmr, in0=mv[:, 0:1], scalar=-1.0, in1=rstd,
                                                          op0=ALU.mult, op1=ALU.mult), [b_lnst], [b_lnst])
            if debug == "s2b":
                return True
            sc.op("act", lambda e: e.activation(out=h, in_=h, func=AF.Identity, scale=rstd, bias=nmr),
                  [bRH[tb], b_lnst], [bRH[tb]])
            sc.op("dve", lambda e: e.tensor_tensor(out=h, in0=h, in1=GBg, op=ALU.mult),
                  [bRH[tb]] + bRA[0:8], [bRH[tb]])
            sc.op("dve", lambda e: e.tensor_tensor(out=h, in0=h, in1=GBb, op=ALU.add),
                  [bRH[tb]] + bRA[8:16], [bRH[tb]])
            if debug == "s2c":
                return True
            if transposes:
                for q4 in range(4):
                    bk = 6 + (q4 % 2)
                    sc.pe([tp(PB[bk][:, i4 * 128:(i4 + 1) * 128],
                              RH[:, tb, (4 * q4 + i4) * 128:(4 * q4 + i4 + 1) * 128], ident32) for i4 in range(4)],
                          [bRH[tb], bC], [bPB[bk]])
                    evac_copy(RY[:, 4 * q4:4 * q4 + 4, tb * 128:(tb + 1) * 128],
                              PB[bk][:, :].rearrange("p (c t) -> p c t", t=128), [bPB[bk]],
                              bRY[4 * q4:4 * q4 + 4])

    def tok_proj(w_ap, t0, src, bsrc, nk, first=True):
        for cb in range(8):
            wv, ba, bb = wload(w_ap[t0 + cb], nk, 256)
            for tp2 in range(2):
                bk = next_bank(0, 4)
                for t2 in range(2):
                    tb = 2 * tp2 + t2
                    sc.pe([mm(PB[bk][:, t2 * 256:(t2 + 1) * 256], src[:, kc, tb * 128:(tb + 1) * 128], wv[:, kc, :],
                              kc == 0, kc == nk - 1) for kc in range(nk)],
                          reads=[ba, bb] + bsrc, writes=[bPB[bk]])
                hv = RH[:, 2 * tp2:2 * tp2 + 2, cb * 256:(cb + 1) * 256]
                pv = PB[bk][:, :].rearrange("p (t n) -> p t n", n=256)
                if first:
                    sc.op("dve", lambda e, hv=hv, pv=pv: e.scalar_tensor_tensor(
                        out=hv, in0=hv, scalar=ALPHA, in1=pv, op0=ALU.mult, op1=ALU.add),
                        [bPB[bk], bRH[2 * tp2], bRH[2 * tp2 + 1]], [bRH[2 * tp2], bRH[2 * tp2 + 1]])
                else:
                    sc.op("dve", lambda e, hv=hv, pv=pv: e.tensor_tensor(out=hv, in0=hv, in1=pv, op=ALU.add),
                          [bPB[bk], bRH[2 * tp2], bRH[2 * tp2 + 1]], [bRH[2 * tp2], bRH[2 * tp2 + 1]])

    def stage2(g):
        for tb in range(4):
            sc.dma("pool", RH[:, tb, :], x_d[g * 512 + tb * 128:g * 512 + (tb + 1) * 128, :], [], [bRH[tb]],
                   "h%d" % tb)
        if debug == "s2x":
            return True
        tok_proj(wo_d, 0, RY, bRY, 16)
        if debug == "s2a":
            return True
        if layer_norm(0, True):
            return True
        if debug == "ln1" and g == 0:
            return True
        for hd in range(4):
            q = qmT[hd % 2]
            bq = b_qm[hd % 2]
            for wt in range(2):
                wv, ba, bb = wload(wq_d[hd * 2 + wt], 16, 256)
                for half in range(2):
                    cc = 2 * wt + half
                    bk = next_bank(0, 4)
                    sc.pe([mm(PB[bk][:, :], wv[:, kc, half * 128:(half + 1) * 128], RY[:, kc, :], kc == 0, kc == 15)
                           for kc in range(16)], reads=[ba, bb] + bRY, writes=[bPB[bk]])
                    evac_copy(q[:, cc, :], PB[bk][:, :], [bPB[bk]], [bq])
            for tb in range(4):
                sbk = 4 + tb % 2
                P = ca_P[tb % 2]
                bP = b_caP[tb % 2]
                sc.pe([mm(PB[sbk][:, 0:MEM], q[:, cc, tb * 128:(tb + 1) * 128], KmT[:, hd * 4 + cc, :], cc == 0, cc == 3)
                       for cc in range(4)], [bq, bKm], [bPB[sbk]])
                rmax = ln_st[:, 32 + 4 * (tb % 2):33 + 4 * (tb % 2)]
                nm = ln_st[:, 33 + 4 * (tb % 2):34 + 4 * (tb % 2)]
                rs = ln_st[:, 34 + 4 * (tb % 2):35 + 4 * (tb % 2)]
                rr = ln_st[:, 35 + 4 * (tb % 2):36 + 4 * (tb % 2)]
                bst = b_caP[tb % 2]
                sc.op("dve", lambda e: e.tensor_reduce(out=rmax, in_=PB[sbk][:, 0:MEM], axis=AX.X, op=ALU.max),
                      [bPB[sbk]], [bst])
                sc.op("dve", lambda e: e.tensor_scalar(out=nm, in0=rmax, scalar1=-QSCALE_MEM, scalar2=None,
                                                       op0=ALU.mult), [bst], [bst])
                sc.op("act", lambda e: e.activation(out=P, in_=PB[sbk][:, 0:MEM], func=AF.Exp, scale=QSCALE_MEM,
                                                    bias=nm), [bPB[sbk], bst], [bP])
                sc.op("dve", lambda e: e.tensor_reduce(out=rs, in_=P, axis=AX.X, op=ALU.add), [bP], [bP])
                sc.op("dve", lambda e: e.reciprocal(out=rr, in_=rs), [bP], [bP])
                sc.op("dve", lambda e: e.tensor_scalar(out=ca_Pn, in0=P, scalar1=rr, scalar2=None, op0=ALU.mult),
                      [bP], [b_caPn])
                ptps = PB[6 + tb % 2][:, 0:128].bitcast(BF16)
                sc.pe([tp(ptps[:, mb * 128:(mb + 1) * 128], ca_Pn[:, mb * 128:(mb + 1) * 128], identb)
                       for mb in range(2)], [b_caPn, bC], [bPB[6 + tb % 2]])
                evac_copy(ca_PT[:, :, tb * 128:(tb + 1) * 128], ptps.rearrange("p (m t) -> p m t", t=128),
                          [bPB[6 + tb % 2]], [b_caPT])
            for cc in range(4):
                c = hd * 4 + cc
                bk = next_bank(0, 4)
                sc.pe([mm(PB[bk][:, :], Vm[:, mb, c * 128:(c + 1) * 128], ca_PT[:, mb, :], mb == 0, mb == 1)
                       for mb in range(2)], [bVm, b_caPT], [bPB[bk]])
                evac_copy(RA[:, c, :], PB[bk][:, :], [bPB[bk]], [bRA[c]])
        tok_proj(wom_d, 0, RA, bRA, 16)
        layer_norm(1, True)
        if debug == "ln2" and g == 0:
            return True
        for qd in range(4):
            for j in range(11):
                fc = qd * 11 + j
                wv, ba, bb = wload(wgu_d[fc], 16, 256)
                bg = next_bank(0, 4)
                sc.pe([mm(PB[bg][:, :], wv[:, kc, 0:128], RY[:, kc, :], kc == 0, kc == 15) for kc in range(16)],
                      reads=[ba] + bRY, writes=[bPB[bg]])
                bu = next_bank(0, 4)
                sc.pe([mm(PB[bu][:, :], wv[:, kc, 128:256], RY[:, kc, :], kc == 0, kc == 15) for kc in range(16)],
                      reads=[bb] + bRY, writes=[bPB[bu]])
                sc.op("act", lambda e: e.activation(out=sg, in_=PB[bg][:, :], func=AF.Silu), [bPB[bg]], [b_sg])
                sc.op("dve", lambda e: e.tensor_tensor(out=RA[:, j, :], in0=sg, in1=PB[bu][:, :], op=ALU.mult),
                      [b_sg, bPB[bu]], [bRA[j]])
            tok_proj(wdn_d, qd * 8, RA, bRA[0:11], 11, first=(qd == 0))
        layer_norm(2, False)
        for tb in range(4):
            sc.dma("sp", out_d[g * 512 + tb * 128:g * 512 + (tb + 1) * 128, :], RH[:, tb, :], [bRH[tb]], [],
                   "o%d" % tb)
        return False

    stop = False
    for g in range(G):
        stop = stage1(g)
        if stop:
            break
        sc.barrier()
        stop = stage2(g)
        if stop:
            break
        sc.barrier()
    if debug is not None:
        sc.barrier()
        dbg_views = {
            "inproj": [QaT, QbT],
            "swa": [RY],
            "sb": [RY],
            "ln1": [RY],
            "ln2": [RY],
        }
        if (debug.startswith("swa") and debug != "swa") or debug.startswith("s2"):
            sc.dma("sp", dbg_d[:, 0:NCF], CF[:, :], [bC], [], "dbg")
        elif debug in ("inproj",):
            sc.dma("sp", dbg_d[:, 0:4096], RH[:, 2:4, :].rearrange("p a b -> p (a b)"), bRH, [], "dbg")
        elif debug in ("swa", "sb", "ln1", "ln2", "full"):
            if debug == "swa":
                sc.dma("sp", dbg_d[:, 0:2048], RY[:, 0:8, :].rearrange("p a b -> p (a b)").bitcast(F32), bRY, [], "dbg")
            else:
                sc.dma("sp", dbg_d[:, 0:4096], RY[:, :, :].rearrange("p a b -> p (a b)").bitcast(F32), bRY, [], "dbg")
            if debug in ("ln1", "ln2"):
                pass
        sc.barrier()
    sc.finish()
    return nc


def _consts(sinks, g_swa, g_sb):
    cf = np.zeros((128, NCF), np.float32)
    cf[:, CF_ID:CF_ID + 128] = np.eye(128, dtype=np.float32)
    bd = np.zeros((128, 128), np.float32)
    bd[:64, :64] = 1.0
    bd[64:, 64:] = 1.0
    cf[:, CF_BD:CF_BD + 128] = bd
    q = np.arange(128)[:, None]
    s = np.arange(128)[None, :]
    prev = np.where(s > q, -(128.0 + q - s), -1e9)
    cur = np.where(s <= q, -(q - s).astype(np.float64), -1e9)
    cf[:, CF_ND:CF_ND + 128] = prev
    cf[:, CF_ND + 128:CF_ND + 256] = cur
    cf[:, CF_ND0:CF_ND0 + 128] = -1e9
    cf[:, CF_ND0 + 128:CF_ND0 + 256] = cur
    gcat = np.concatenate([g_swa.reshape(-1), g_sb.reshape(-1)]).astype(np.float32)
    cf[:, CF_G:CF_G + 16] = gcat.reshape(16, 128).T
    cf[:, CF_SK:CF_SK + 16] = np.broadcast_to(sinks.reshape(1, 16), (128, 16))
    cf[:, CF_M01:CF_M01 + 128] = np.where(q < s, 1.0, 0.0)
    cb = np.zeros((128, NCB), np.float32)
    cb[:, CB_ID:CB_ID + 128] = np.eye(128, dtype=np.float32)
    j = np.arange(128)[:, None]
    sidx = np.arange(128)[None, :]
    cb[:, CB_NT:CB_NT + 128] = np.where(j >= sidx, -1.0, 0.0)
    cb[:, CB_NSL:CB_NSL + 128] = np.where(j < sidx, -1.0, 0.0)
    cb[:, CB_MK:CB_MK + 128] = np.where(j >= sidx, NEG, 0.0)
    return cf, cb


def _tile(W, nk, col0s, r0=0):
    out = np.empty((len(col0s), 128, nk * 256), np.float32)
    for t, c0 in enumerate(col0s):
        out[t] = W[r0:r0 + nk * 128, c0:c0 + 256].reshape(nk, 128, 256).transpose(1, 0, 2).reshape(128, nk * 256)
    return out


def _w_in2(w_in):
    w = w_in
    qa = w[:, 0:1024]
    ka = w[:, 1024:1280]
    va = w[:, 1280:1536]
    qb = w[:, 1536:2560]
    kb = w[:, 2560:3584]
    vb = w[:, 3584:4608]
    kad = np.concatenate([np.concatenate([ka[:, k * 64:(k + 1) * 64]] * 2, axis=1) for k in range(4)], axis=1)
    return np.ascontiguousarray(np.concatenate([qa, kad, qb, kb, va, vb], axis=1))


_NC_CACHE = {}


def _prep_shared(inp):
    f = lambda a: np.ascontiguousarray(np.asarray(a, dtype=np.float32))
    cf, cb = _consts(f(inp["sinks"])[0], f(inp["g_swa"])[0], f(inp["g_sb"])[0])
    lnp = np.stack([f(inp[k])[0] for k in ("ln1_g", "ln1_b", "ln2_g", "ln2_b", "ln3_g", "ln3_b")])
    wgu = f(inp["w_gate_up"])[0]
    wgu2 = np.concatenate([wgu[:, :DFF].reshape(D, NFC, 128), wgu[:, DFF:].reshape(D, NFC, 128)], axis=2)
    wgu2 = wgu2.reshape(D, NFC * 256)
    wdn = f(inp["w_down"])[0]
    c8 = [cb * 256 for cb in range(8)]
    return {
        "w_in2": _tile(_w_in2(f(inp["w_in"])[0]), 16, [t * 256 for t in range(19)]),
        "w_o": _tile(f(inp["w_o"])[0], 16, c8),
        "w_q_mem": _tile(f(inp["w_q_mem"])[0], 16, c8),
        "w_kv_mem": _tile(f(inp["w_kv_mem"])[0], 16, [t * 256 for t in range(16)]),
        "w_o_mem": _tile(f(inp["w_o_mem"])[0], 16, c8),
        "w_gate_up": _tile(wgu2, 16, [t * 256 for t in range(NFC)]),
        "w_down": np.concatenate([_tile(wdn, 11, c8, r0=qd * 11 * 128) for qd in range(4)], axis=0),
        "cf32": cf, "cb16": cb, "lnp": np.ascontiguousarray(lnp),
    }


def kernel(**inputs):
    x = np.asarray(inputs["x"], dtype=np.float32)
    mem = np.asarray(inputs["mem"], dtype=np.float32)
    Bn, S, _ = x.shape
    shared = _prep_shared(inputs)
    if S not in _NC_CACHE:
        _NC_CACHE[S] = build(S)
    nc = _NC_CACHE[S]
    in_maps = []
    for b in range(Bn):
        m = dict(shared)
        m["x"] = np.ascontiguousarray(x[b])
        m["xT"] = np.ascontiguousarray(x[b].T)
        m["memT"] = np.ascontiguousarray(mem[b].T)
        in_maps.append(m)
    res = run_bass_kernel_spmd(nc, in_maps, core_ids=list(range(Bn)))
    return np.stack([r["out"] for r in res.results], axis=0).astype(np.float32)
```

```python
import math
import numpy as np
import concourse.bass as bass
import concourse.mybir as mybir
from concourse.bass_utils import run_bass_kernel_spmd

AF = mybir.ActivationFunctionType
ALU = mybir.AluOpType
AX = mybir.AxisListType
F32 = mybir.dt.float32
BF16 = mybir.dt.bfloat16

D = 2048
DFF = 5632
NFC = DFF // 128
ALPHA = 2.0 ** 0.25
LN_EPS = 1e-5
RMS_EPS = 1e-6
MEM = 256
NEG = -30000.0
SLOPES = [2.0 ** (-8.0 * (h + 1) / 16) for h in range(16)]
QSCALE_MEM = 1.0 / math.sqrt(512.0)

CF_ID, CF_BD, CF_ND, CF_ND0, CF_G, CF_SK, CF_M01, NCF = 0, 128, 256, 512, 768, 784, 800, 928
CB_ID, CB_NT, CB_NSL, CB_MK, NCB = 0, 128, 256, 384, 512

W_IN2 = 4864


class Buf:
    __slots__ = ("name", "w", "r")

    def __init__(self, name):
        self.name = name
        self.w = None
        self.r = {}


class _Eng:
    def __init__(self, name, h, sem):
        self.name, self.h, self.sem, self.cnt, self.waited = name, h, sem, 0, {}


class _Stream:
    def __init__(self, name, sem):
        self.name, self.sem, self.cnt = name, sem, 0


class Sched:
    def __init__(self, nc):
        self.nc = nc
        self.E = {}
        for n, h in (("pe", nc.tensor), ("act", nc.scalar), ("dve", nc.vector)):
            self.E[n] = _Eng(n, h, nc.alloc_semaphore("s_" + n))
        for n, h in (("pool", nc.gpsimd), ("sp", nc.sync)):
            self.E[n] = _Eng(n, h, None)
        self.streams = {}
        self.sems = {}
        for n in ("pe", "act", "dve"):
            self.sems[n] = self.E[n].sem

    def stream(self, name):
        if name not in self.streams:
            st = _Stream(name, self.nc.alloc_semaphore("d_" + name))
            self.streams[name] = st
            self.sems["d_" + name] = st.sem
        return self.streams[name]

    def _deps(self, reads, writes):
        deps = {}

        def add(kv):
            k, v = kv
            if deps.get(k, 0) < v:
                deps[k] = v
        for b in reads:
            if b.w is not None:
                add(b.w)
        for b in writes:
            if b.w is not None:
                add(b.w)
            for kv in b.r.items():
                add(kv)
        return deps

    def _wait(self, eng, deps):
        for k, v in deps.items():
            if k == eng.name and eng.name == "pe":
                continue
            if eng.waited.get(k, 0) < v:
                eng.h.wait_ge(self.sems[k], v)
                eng.waited[k] = v

    def _mark(self, key, val, reads, writes):
        for b in reads:
            if b.r.get(key, 0) < val:
                b.r[key] = val
        for b in writes:
            b.w = (key, val)
            b.r = {}

    def op(self, en, fn, reads=(), writes=()):
        eng = self.E[en]
        self._wait(eng, self._deps(reads, writes))
        ins = fn(eng.h)
        eng.cnt += 1
        ins.then_inc(eng.sem, 1)
        self._mark(en, eng.cnt, reads, writes)

    def pe(self, fns, reads=(), writes=()):
        eng = self.E["pe"]
        self._wait(eng, self._deps(reads, writes))
        ins = None
        for fn in fns:
            ins = fn(eng.h)
        eng.cnt += 1
        ins.then_inc(eng.sem, 1)
        self._mark("pe", eng.cnt, reads, writes)

    def dma(self, qn, out, in_, reads, writes, stream):
        q = self.E[qn]
        st = self.stream(stream)
        key = "d_" + stream
        deps = self._deps(reads, writes)
        if st.cnt > 0:
            deps[key] = max(deps.get(key, 0), st.cnt)
        self._wait(q, deps)
        q.h.dma_start(out=out, in_=in_).then_inc(st.sem, 16)
        st.cnt += 16
        self._mark(key, st.cnt, reads, writes)

    def barrier(self):
        tgt = {n: self.E[n].cnt for n in ("pe", "act", "dve")}
        for st in self.streams.values():
            tgt["d_" + st.name] = st.cnt
        for e in self.E.values():
            for k, v in tgt.items():
                if k == e.name or v == 0:
                    continue
                if e.waited.get(k, 0) < v:
                    e.h.wait_ge(self.sems[k], v)
                    e.waited[k] = v

    def finish(self):
        sp = self.E["sp"]
        for st in self.streams.values():
            if st.cnt and sp.waited.get("d_" + st.name, 0) < st.cnt:
                sp.h.wait_ge(st.sem, st.cnt)
                sp.waited["d_" + st.name] = st.cnt


def build(S=2048, debug=None):
    G = S // 512
    NB = S // 128
    nc = bass.Bass("TRN2", target_bir_lowering=False)
    sc = Sched(nc)

    def dram(name, shape, kind="ExternalInput"):
        return nc.dram_tensor(name, shape, F32, kind=kind).ap()

    xT_d = dram("xT", [D, S])
    x_d = dram("x", [S, D])
    memT_d = dram("memT", [D, MEM])
    win_d = dram("w_in2", [19, 128, 4096])
    wo_d = dram("w_o", [8, 128, 4096])
    wq_d = dram("w_q_mem", [8, 128, 4096])
    wkv_d = dram("w_kv_mem", [16, 128, 4096])
    wom_d = dram("w_o_mem", [8, 128, 4096])
    wgu_d = dram("w_gate_up", [NFC, 128, 4096])
    wdn_d = dram("w_down", [32, 128, 11 * 256])
    cf_d = dram("cf32", [128, NCF])
    cb_d = dram("cb16", [128, NCB])
    lnp_d = dram("lnp", [6, D])
    out_d = dram("out", [S, D], kind="ExternalOutput")
    dbg_d = None
    if debug is not None:
        dbg_d = dram("dbg", [128, 16 * 512], kind="ExternalOutput")

    sb = nc.alloc_sbuf_tensor
    KbT = sb("KbT", [128, 8, S], BF16)
    Vb = sb("Vb", [128, NB, 1024], BF16)
    KaT = sb("KaT", [128, 4, 640], BF16)
    Va = sb("Va", [128, 5, 256], BF16)
    KmT = sb("KmT", [128, 16, MEM], BF16)
    Vm = sb("Vm", [128, 2, D], BF16)
    CF = sb("CF", [128, NCF], F32)
    CB = sb("CB", [128, NCB], BF16)
    RH = sb("RH", [128, 4, D], F32)
    RHb = RH[:, :, :].rearrange("p a b -> p (a b)").bitcast(BF16)
    xT = RHb[:, 0:8192].rearrange("p (c t) -> p c t", t=512)
    QaT = RHb[:, 8192:12288].rearrange("p (c t) -> p c t", t=512)
    QbT = RHb[:, 12288:16384].rearrange("p (c t) -> p c t", t=512)
    memT = RHb[:, 0:4096].rearrange("p (c t) -> p c t", t=MEM)
    RY = sb("RY", [128, 16, 512], BF16)
    WS = [sb("WS%d" % i, [128, 4096], BF16) for i in range(3)]
    RU = sb("RU", [128, 15360], BF16)
    RUf = RU[:, :].bitcast(F32)
    sb_e = [RUf[:, i * 512:(i + 1) * 512] for i in range(3)]
    sb_xc = [RUf[:, 1536 + i * 512:1536 + (i + 1) * 512] for i in range(2)]
    sb_sp = [RU[:, 5120 + i * 512:5120 + (i + 1) * 512] for i in range(2)]
    sb_sp.append(RU[:, 14464:14976])
    sb_at = [RU[:, 6144 + i * 512:6144 + (i + 1) * 512] for i in range(2)]
    rms_sq = RUf[:, 3584:4096]
    rms_l = RUf[:, 4096:4608]
    rms_r = RUf[:, 4608:5120]
    swa_S = RUf[:, 5120:6144].rearrange("p (h k) -> p h k", k=256)
    swa_Pn = RU[:, 12288:13312].rearrange("p (h k) -> p h k", k=256)
    swa_PT = RU[:, 13312:14336]
    swa_st = RUf[:, 7168:7232]
    RA = RU[:, 0:8192].rearrange("p (c t) -> p c t", t=512)
    GBg = RUf[:, 0:2048]
    GBb = RUf[:, 2048:4096]
    qmT = [RU[:, 8192 + i * 2048:8192 + (i + 1) * 2048].rearrange("p (c t) -> p c t", t=512)
           for i in range(2)]
    ca_P = [RUf[:, 6144 + i * 256:6144 + (i + 1) * 256] for i in range(2)]
    ca_Pn = RU[:, 13312:13568]
    ca_PT = RU[:, 13568:14592].rearrange("p (m t) -> p m t", t=512)
    sg = RUf[:, 7296:7552].bitcast(BF16)
    ln_st = sb("ln_st", [128, 64], F32)
    swa_S2 = sb("swa_S2", [128, 4, 256], F32)
    swa_Pn2 = sb("swa_Pn2", [128, 4, 256], BF16)
    swa_PT2 = sb("swa_PT2", [128, 1024], BF16)
    swa_st2 = sb("swa_st2", [128, 64], F32)
    PB = [nc.alloc_psum_tensor("PB%d" % i, [128, 512], F32) for i in range(8)]
    bPB = [Buf("PB%d" % i) for i in range(8)]

    bW = [(Buf("W%da" % i), Buf("W%db" % i)) for i in range(3)]
    bRH = [Buf("h%d" % i) for i in range(4)]
    bRY = [Buf("RY%d" % i) for i in range(16)]
    bRA = [Buf("RA%d" % i) for i in range(16)]
    bKbT = {}
    bVb = {}
    bKa = [[Buf("Ka%d_%d" % (k, s)) for s in range(5)] for k in range(4)]
    bVa = [Buf("Va%d" % s) for s in range(5)]
    bKm = Buf("KmT")
    bVm = Buf("Vm")
    bC = Buf("consts")
    bXT = bRH[0:2]
    bQa = bRH[2]
    bQb = bRH[3]

    def B(d, key):
        if key not in d:
            d[key] = Buf(str(key))
        return d[key]

    wslot = [0]

    def wload(src_ap, nk, ncols):
        i = wslot[0] % 3
        wslot[0] += 1
        view = WS[i][:, 0:nk * ncols].rearrange("p (c n) -> p c n", n=ncols)
        ba, bb = bW[i]
        sc.dma("pool", WS[i][:, 0:nk * ncols], src_ap, [], [ba, bb], "w%d" % i)
        return view, ba, bb

    evac_rr = [0]

    def evac_copy(out, in_, reads, writes, scale=None):
        evac_rr[0] += 1
        if evac_rr[0] % 2 == 0:
            if scale is None:
                sc.op("act", lambda e: e.activation(out=out, in_=in_, func=AF.Copy), reads, writes)
            else:
                sc.op("act", lambda e: e.activation(out=out, in_=in_, func=AF.Copy, scale=scale), reads, writes)
        else:
            if scale is None:
                sc.op("dve", lambda e: e.tensor_copy(out=out, in_=in_), reads, writes)
            else:
                sc.op("dve", lambda e: e.tensor_scalar(out=out, in0=in_, scalar1=scale, scalar2=None,
                                                        op0=ALU.mult), reads, writes)

    def mm(out, lhsT, rhs, start, stop):
        return lambda e: e.matmul(out, lhsT=lhsT, rhs=rhs, start=start, stop=stop)

    def mmx(out, lhsT, rhs, start, stop):
        return lambda e: e.matmul(out, lhsT=lhsT, rhs=rhs, start=start, stop=stop, skip_group_check=True)

    def tp(out, in_, ident):
        return lambda e: e.transpose(out, in_, ident)

    def dbg_dump(ap_list):
        pass

    sc.dma("sp", CF[:, :], cf_d, [], [bC], "cf")
    sc.dma("pool", CB[:, :], cb_d, [], [bC], "cb")
    ident32 = CF[:, CF_ID:CF_ID + 128]
    BDm = CF[:, CF_BD:CF_BD + 128]
    negdist = CF[:, CF_ND:CF_ND + 256]
    negdist0 = CF[:, CF_ND0:CF_ND0 + 256]
    gcol = CF[:, CF_G:CF_G + 16]
    sinks = CF[:, CF_SK:CF_SK + 16]
    mask01 = CF[:, CF_M01:CF_M01 + 128]
    identb = CB[:, CB_ID:CB_ID + 128]
    NegTri = CB[:, CB_NT:CB_NT + 128]
    NegSL = CB[:, CB_NSL:CB_NSL + 128]
    MaskT = CB[:, CB_MK:CB_MK + 128]
    sc.op("dve", lambda e: e.memset(KaT[:, :, 0:128], 0.0), [], [bKa[k][0] for k in range(4)])
    sc.op("dve", lambda e: e.memset(Va[:, 0, :], 0.0), [], [bVa[0]])

    sc.dma("pool", memT, memT_d.rearrange("(c p) m -> p c m", p=128), [], bXT, "xT")
    pbi = [0]

    def next_bank(lo=0, hi=8):
        pbi[0] += 1
        return lo + pbi[0] % (hi - lo)

    for wt in range(8):
        wv, ba, bb = wload(wkv_d[wt], 16, 256)
        for half in range(2):
            c = 2 * wt + half
            bk = next_bank(0, 4)
            sc.pe([mm(PB[bk][:, 0:MEM], wv[:, kc, half * 128:(half + 1) * 128], memT[:, kc, :],
                      kc == 0, kc == 15) for kc in range(16)],
                  reads=[ba, bb, bC] + bXT, writes=[bPB[bk]])
            evac_copy(KmT[:, c, :], PB[bk][:, 0:MEM], [bPB[bk]], [bKm])
    for ct in range(8):
        wv, ba, bb = wload(wkv_d[8 + ct], 16, 256)
        bk = next_bank(0, 4)
        for mb in range(2):
            sc.pe([mm(PB[bk][:, mb * 256:(mb + 1) * 256], memT[:, kc, mb * 128:(mb + 1) * 128], wv[:, kc, :],
                      kc == 0, kc == 15) for kc in range(16)],
                  reads=[ba, bb] + bXT, writes=[bPB[bk]])
        evac_copy(Vm[:, :, ct * 256:(ct + 1) * 256],
                  PB[bk][:, :].rearrange("p (m n) -> p m n", n=256), [bPB[bk]], [bVm])
    sc.barrier()

    def rmsnorm_pair(obank, c):
        ob = PB[obank]
        sc.op("act", lambda e: e.activation(out=rms_sq, in_=ob[:, :], func=AF.Square),
              [bPB[obank]], [b_rms_sq])
        ssb = 5 if obank != 5 else 7
        sc.pe([mm(PB[ssb][:, :], BDm, rms_sq, True, True)], [b_rms_sq, bC], [bPB[ssb]])
        sc.op("act", lambda e: e.activation(out=rms_l, in_=PB[ssb][:, :], func=AF.Ln, scale=1.0 / 64.0,
                                            bias=RMS_EPS), [bPB[ssb]], [b_rms_l])
        sc.op("act", lambda e: e.activation(out=rms_r, in_=rms_l, func=AF.Exp, scale=-0.5),
              [b_rms_l], [b_rms_r])
        sc.op("dve", lambda e: e.scalar_tensor_tensor(out=RY[:, c, :], in0=ob[:, :], scalar=gcol[:, c:c + 1],
                                                      in1=rms_r, op0=ALU.mult, op1=ALU.mult),
              [bPB[obank], b_rms_r, bC], [bRY[c]])

    b_rms_sq, b_rms_l, b_rms_r = Buf("rms_sq"), Buf("rms_l"), Buf("rms_r")
    b_e = [Buf("e%d" % i) for i in range(3)]
    b_xc = [Buf("xc%d" % i) for i in range(2)]
    b_sp = [Buf("sp%d" % i) for i in range(3)]
    b_at = [Buf("at%d" % i) for i in range(2)]
    b_swaS, b_swaPn, b_swaPT, b_swast = Buf("swaS"), Buf("swaPn"), Buf("swaPT"), Buf("swast")
    b_swaS2, b_swaPn2, b_swaPT2, b_swast2 = Buf("swaS2"), Buf("swaPn2"), Buf("swaPT2"), Buf("swast2")
    b_qm = [Buf("qm0"), Buf("qm1")]
    b_caP = [Buf("caP0"), Buf("caP1")]
    b_caPn, b_caPT, b_sg, b_lnst = Buf("caPn"), Buf("caPT"), Buf("sg"), Buf("lnst")
    b_gb = Buf("gb_dummy")

    def stage1(g):
        sc.dma("pool", xT, xT_d[:, g * 512:(g + 1) * 512].rearrange("(c p) t -> p c t", p=128),
               [], bXT, "xT")
        for wt in range(14):
            wv, ba, bb = wload(win_d[wt], 16, 256)
            for half in range(2):
                c = 2 * wt + half
                bk = next_bank(0, 4)
                sc.pe([mm(PB[bk][:, :], wv[:, kc, half * 128:(half + 1) * 128], xT[:, kc, :], kc == 0, kc == 15)
                       for kc in range(16)], reads=[ba, bb] + bXT, writes=[bPB[bk]])
                if c < 8:
                    evac_copy(QaT[:, c, :], PB[bk][:, :], [bPB[bk]], [bQa], scale=0.125)
                elif c < 12:
                    k = c - 8
                    evac_copy(KaT[:, k, 128:640], PB[bk][:, :], [bPB[bk]], [bKa[k][s] for s in range(1, 5)])
                elif c < 20:
                    evac_copy(QbT[:, c - 12, :], PB[bk][:, :], [bPB[bk]], [bQb], scale=0.125)
                else:
                    evac_copy(KbT[:, c - 20, g * 512:(g + 1) * 512], PB[bk][:, :], [bPB[bk]],
                              [B(bKbT, (c - 20, g))])
        for vt in range(5):
            wv, ba, bb = wload(win_d[14 + vt], 16, 256)
            for tp2 in range(2):
                bk = next_bank(0, 4)
                for t2 in range(2):
                    tb = 2 * tp2 + t2
                    sc.pe([mm(PB[bk][:, t2 * 256:(t2 + 1) * 256], xT[:, kc, tb * 128:(tb + 1) * 128], wv[:, kc, :],
                              kc == 0, kc == 15) for kc in range(16)],
                          reads=[ba, bb] + bXT, writes=[bPB[bk]])
                src = PB[bk][:, :].rearrange("p (t n) -> p t n", n=256)
                if vt == 0:
                    evac_copy(Va[:, 1 + 2 * tp2:3 + 2 * tp2, :], src, [bPB[bk]], [bVa[1 + 2 * tp2], bVa[2 + 2 * tp2]])
                else:
                    evac_copy(Vb[:, g * 4 + 2 * tp2:g * 4 + 2 * tp2 + 2, (vt - 1) * 256:vt * 256], src, [bPB[bk]],
                              [B(bVb, (g * 4 + 2 * tp2, vt - 1)), B(bVb, (g * 4 + 2 * tp2 + 1, vt - 1))])
        if debug == "inproj" and g == 0:
            return True
        def swa_unit(kg, qb, R):
            S_, Pn_, PT_, st_ = R["S"], R["Pn"], R["PT"], R["st"]
            bS, bPn, bPT, bst = R["bS"], R["bPn"], R["bPT"], R["bst"]
            Sb, ptb, ob = R["Sb"], R["ptb"], R["ob"]
            gb = 4 * g + qb
            nd = negdist0 if gb == 0 else negdist
            fns = []
            for hh in range(4):
                h = 4 * kg + hh
                base = (h % 2) * 64
                fns.append(mm(PB[Sb[hh % 2]][:, (hh // 2) * 256:(hh // 2 + 1) * 256],
                              QaT[base:base + 64, h // 2, qb * 128:(qb + 1) * 128],
                              KaT[base:base + 64, kg, qb * 128:qb * 128 + 256], True, True))
            sc.pe(fns, reads=[bQa, bKa[kg][qb], bKa[kg][qb + 1]], writes=[bPB[Sb[0]], bPB[Sb[1]]])
            yield
            for hh in range(4):
                h = 4 * kg + hh
                sc.op("dve", lambda e, hh=hh, h=h: e.scalar_tensor_tensor(
                    out=S_[:, hh, :], in0=nd, scalar=SLOPES[h],
                    in1=PB[Sb[hh % 2]][:, (hh // 2) * 256:(hh // 2 + 1) * 256], op0=ALU.mult, op1=ALU.add),
                    [bPB[Sb[hh % 2]], bC], [bS])
                yield
            rowmax = st_[:, 0:4]
            negm = st_[:, 4:8]
            dsk = st_[:, 8:12]
            rowsum = st_[:, 12:16]
            es = st_[:, 16:20]
            rden = st_[:, 20:24]
            sk = sinks[:, 4 * kg:4 * kg + 4]
            sc.op("dve", lambda e: e.tensor_reduce(out=rowmax, in_=S_, axis=AX.X, op=ALU.max), [bS], [bst])
            yield
            sc.op("dve", lambda e: e.tensor_tensor(out=rowmax, in0=rowmax, in1=sk, op=ALU.max), [bst, bC], [bst])
            yield
            sc.op("dve", lambda e: e.tensor_scalar(out=negm, in0=rowmax, scalar1=-1.0, scalar2=None,
                                                   op0=ALU.mult), [bst], [bst])
            yield
            sc.op("dve", lambda e: e.tensor_tensor(out=dsk, in0=sk, in1=rowmax, op=ALU.subtract),
                  [bst, bC], [bst])
            yield
            for hh in range(4):
                sc.op("act", lambda e, hh=hh: e.activation(out=S_[:, hh, :], in_=S_[:, hh, :], func=AF.Exp,
                                                           bias=negm[:, hh:hh + 1]), [bS, bst], [bS])
                yield
            sc.op("dve", lambda e: e.tensor_reduce(out=rowsum, in_=S_, axis=AX.X, op=ALU.add), [bS], [bst])
            yield
            sc.op("act", lambda e: e.activation(out=es, in_=dsk, func=AF.Exp), [bst], [bst])
            yield
            sc.op("dve", lambda e: e.tensor_tensor(out=rden, in0=rowsum, in1=es, op=ALU.add), [bst], [bst])
            yield
            sc.op("dve", lambda e: e.reciprocal(out=rden, in_=rden), [bst], [bst])
            yield
            for hh in range(4):
                sc.op("dve", lambda e, hh=hh: e.tensor_scalar(out=Pn_[:, hh, :], in0=S_[:, hh, :],
                                                              scalar1=rden[:, hh:hh + 1], scalar2=None,
                                                              op0=ALU.mult), [bS, bst], [bPn])
                yield
            ptps = PB[ptb][:, :].bitcast(BF16)
            sc.pe([tp(ptps[:, (hh * 2 + kb) * 128:(hh * 2 + kb + 1) * 128],
                      Pn_[:, hh, kb * 128:(kb + 1) * 128], identb)
                   for hh in range(4) for kb in range(2)], [bPn, bC], [bPB[ptb]])
            yield
            evac_copy(PT_, ptps, [bPB[ptb]], [bPT])
            yield
            for hh in range(4):
                h = 4 * kg + hh
                base = (h % 2) * 64
                obk = ob[hh // 2]
                sc.pe([mm(PB[obk][base:base + 64, qb * 128:(qb + 1) * 128],
                          Va[:, qb + kb, kg * 64:(kg + 1) * 64],
                          PT_[:, (hh * 2 + kb) * 128:(hh * 2 + kb + 1) * 128], kb == 0, kb == 1)
                       for kb in range(2)],
                      [bPT, bVa[qb], bVa[qb + 1]], [bPB[obk]])
                yield

        for kg in range(4):
            ob = [4, 6]
            RS = [dict(S=swa_S, Pn=swa_Pn, PT=swa_PT, st=swa_st, bS=b_swaS, bPn=b_swaPn, bPT=b_swaPT,
                       bst=b_swast, Sb=[0, 1], ptb=2, ob=ob),
                  dict(S=swa_S2[:, :, :], Pn=swa_Pn2[:, :, :], PT=swa_PT2[:, :], st=swa_st2[:, :], bS=b_swaS2, bPn=b_swaPn2, bPT=b_swaPT2,
                       bst=b_swast2, Sb=[3, 5], ptb=7, ob=ob)]
            for qb0 in (0, 2):
                gens = [swa_unit(kg, qb0, RS[0]), swa_unit(kg, qb0 + 1, RS[1])]
                while gens:
                    for gen in list(gens):
                        try:
                            next(gen)
                        except StopIteration:
                            gens.remove(gen)
            rmsnorm_pair(ob[0], 2 * kg)
            rmsnorm_pair(ob[1], 2 * kg + 1)
        if g + 1 < G:
            for k in range(4):
                sc.op("dve", lambda e, k=k: e.tensor_copy(out=KaT[:, k, 0:128], in_=KaT[:, k, 512:640]),
                      [bKa[k][4]], [bKa[k][0]])
            sc.op("dve", lambda e: e.tensor_copy(out=Va[:, 0, :], in_=Va[:, 4, :]), [bVa[4]], [bVa[0]])
        if debug == "swa" and g == G - 1:
            return True
        for p in range(8):
            obk = 4 if p % 2 == 0 else 6
            tiles = []
            for kb in range(4 * g + 3, -1, -1):
                for hd in range(2):
                    tiles.append((hd, kb))
            n = len(tiles)
            Cb = [2, 3]

            def geom(t):
                hd, kb = tiles[t]
                j = kb - 4 * g
                c0 = max(j, 0) * 128
                return hd, kb, j, c0, hd * 64

            def stA(t):
                hd, kb, j, c0, base = geom(t)
                zb = t % 2
                fns = [mm(PB[zb][:, c0:512], KbT[base:base + 64, p, kb * 128:(kb + 1) * 128],
                          QbT[base:base + 64, p, c0:512], True, True)]
                sc.pe(fns, [B(bKbT, (p, kb // 4)), bQb, bC], [bPB[zb]])

            def stB(t):
                hd, kb, j, c0, base = geom(t)
                zb = t % 2
                sc.op("act", lambda e: e.activation(out=sb_e[t % 3][:, c0:512], in_=PB[zb][:, c0:512], func=AF.Exp),
                      [bPB[zb]], [b_e[t % 3]])
                if j >= 0:
                    sc.op("dve", lambda e: e.tensor_tensor(out=sb_e[t % 3][:, c0:c0 + 128], in0=sb_e[t % 3][:, c0:c0 + 128],
                                                           in1=mask01, op=ALU.mult), [b_e[t % 3], bC], [b_e[t % 3]])
                sc.op("act", lambda e: e.activation(out=sb_sp[t % 3][:, c0:512], in_=sb_e[t % 3][:, c0:512],
                                                    func=AF.Ln, bias=1.0), [b_e[t % 3]], [b_sp[t % 3]])

            def stC(t):
                hd, kb, j, c0, base = geom(t)
                first = (kb == 4 * g + 3)
                sc.pe([mmx(PB[Cb[hd]][:, c0:512], NegTri, sb_sp[t % 3][:, c0:512], first, True)],
                      [b_sp[t % 3], bC], [bPB[Cb[hd]]])

            def stD(t):
                hd, kb, j, c0, base = geom(t)
                sc.op("act", lambda e: e.activation(out=sb_xc[t % 2][:, c0:512], in_=PB[Cb[hd]][:, c0:512],
                                                    func=AF.Exp), [bPB[Cb[hd]]], [b_xc[t % 2]])

            def stE(t):
                hd, kb, j, c0, base = geom(t)
                last = (kb == 0)
                sc.pe([mmx(PB[Cb[hd]][:, c0:512], NegSL, sb_sp[t % 3][:, c0:512], False, True)],
                      [b_sp[t % 3], bC], [bPB[Cb[hd]]])

            def stE2(t):
                hd, kb, j, c0, base = geom(t)
                sc.op("dve", lambda e: e.tensor_tensor(out=sb_at[t % 2][:, c0:512], in0=sb_e[t % 3][:, c0:512],
                                                       in1=sb_xc[t % 2][:, c0:512], op=ALU.mult),
                      [b_e[t % 3], b_xc[t % 2]], [b_at[t % 2]])

            def stF(t):
                hd, kb, j, c0, base = geom(t)
                first = (kb == 4 * g + 3)
                last = (kb == 0)
                sc.pe([mmx(PB[obk][base:base + 64, c0:512], Vb[:, kb, (2 * p + hd) * 64:(2 * p + hd + 1) * 64],
                          sb_at[t % 2][:, c0:512], first, last)],
                      [b_at[t % 2], B(bVb, (kb, (2 * p + hd) // 4))], [bPB[obk]])

            stA(0)
            for s in range(n + 2):
                if 0 <= s - 1 < n:
                    stC(s - 1)
                if s + 1 < n:
                    stA(s + 1)
                if 0 <= s - 2 < n:
                    stE(s - 2)
                    stE2(s - 2)
                    stF(s - 2)
                if s < n:
                    stB(s)
                if 0 <= s - 1 < n:
                    stD(s - 1)
            rmsnorm_pair(obk, 8 + p)
        if debug == "sb" and g == G - 1:
            return True
        return False

    def ln_load(i):
        sc.dma("pool", GBg, lnp_d[2 * i, :].partition_broadcast(128), [], bRA[0:8], "gbg")
        sc.dma("pool", GBb, lnp_d[2 * i + 1, :].partition_broadcast(128), [], bRA[8:16], "gbb")

    def layer_norm(i, transposes):
        ln_load(i)
        for tb in range(4):
            h = RH[:, tb, :]
            st = ln_st[:, 0:24].rearrange("p (j s) -> p j s", s=6)
            mv = ln_st[:, 24:26]
            l = ln_st[:, 26:27]
            rstd = ln_st[:, 27:28]
            nmr = ln_st[:, 28:29]
            for j in range(4):
                sc.op("dve", lambda e, j=j: e.bn_stats(out=st[:, j, :], in_=RH[:, tb, j * 512:(j + 1) * 512]),
                      [bRH[tb]], [b_lnst])
            sc.op("dve", lambda e: e.bn_aggr(out=mv, in_=ln_st[:, 0:24]), [b_lnst], [b_lnst])
            sc.op("act", lambda e: e.activation(out=l, in_=mv[:, 1:2], func=AF.Ln, bias=LN_EPS), [b_lnst], [b_lnst])
            sc.op("act", lambda e: e.activation(out=rstd, in_=l, func=AF.Exp, scale=-0.5), [b_lnst], [b_lnst])
            sc.op("dve", lambda e: e.scalar_tensor_tensor(out=nmr, in0=mv[:, 0:1], scalar=-1.0, in1=rstd,
                                                          op0=ALU.mult, op1=ALU.mult), [b_lnst], [b_lnst])
            if debug == "s2b":
                return True
            sc.op("act", lambda e: e.activation(out=h, in_=h, func=AF.Identity, scale=rstd, bias=nmr),
                  [bRH[tb], b_lnst], [bRH[tb]])
            sc.op("dve", lambda e: e.tensor_tensor(out=h, in0=h, in1=GBg, op=ALU.mult),
                  [bRH[tb]] + bRA[0:8], [bRH[tb]])
            sc.op("dve", lambda e: e.tensor_tensor(out=h, in0=h, in1=GBb, op=ALU.add),
                  [bRH[tb]] + bRA[8:16], [bRH[tb]])
            if debug == "s2c":
                return True
            if transposes:
                for q4 in range(4):
                    bk = 6 + (q4 % 2)
                    sc.pe([tp(PB[bk][:, i4 * 128:(i4 + 1) * 128],
                              RH[:, tb, (4 * q4 + i4) * 128:(4 * q4 + i4 + 1) * 128], ident32) for i4 in range(4)],
                          [bRH[tb], bC], [bPB[bk]])
                    evac_copy(RY[:, 4 * q4:4 * q4 + 4, tb * 128:(tb + 1) * 128],
                              PB[bk][:, :].rearrange("p (c t) -> p c t", t=128), [bPB[bk]],
                              bRY[4 * q4:4 * q4 + 4])

    def tok_proj(w_ap, t0, src, bsrc, nk, first=True):
        for cb in range(8):
            wv, ba, bb = wload(w_ap[t0 + cb], nk, 256)
            for tp2 in range(2):
                bk = next_bank(0, 4)
                for t2 in range(2):
                    tb = 2 * tp2 + t2
                    sc.pe([mm(PB[bk][:, t2 * 256:(t2 + 1) * 256], src[:, kc, tb * 128:(tb + 1) * 128], wv[:, kc, :],
                              kc == 0, kc == nk - 1) for kc in range(nk)],
                          reads=[ba, bb] + bsrc, writes=[bPB[bk]])
                hv = RH[:, 2 * tp2:2 * tp2 + 2, cb * 256:(cb + 1) * 256]
                pv = PB[bk][:, :].rearrange("p (t n) -> p t n", n=256)
                if first:
                    sc.op("dve", lambda e, hv=hv, pv=pv: e.scalar_tensor_tensor(
                        out=hv, in0=hv, scalar=ALPHA, in1=pv, op0=ALU.mult, op1=ALU.add),
                        [bPB[bk], bRH[2 * tp2], bRH[2 * tp2 + 1]], [bRH[2 * tp2], bRH[2 * tp2 + 1]])
                else:
                    sc.op("dve", lambda e, hv=hv, pv=pv: e.tensor_tensor(out=hv, in0=hv, in1=pv, op=ALU.add),
                          [bPB[bk], bRH[2 * tp2], bRH[2 * tp2 + 1]], [bRH[2 * tp2], bRH[2 * tp2 + 1]])

    def stage2(g):
        for tb in range(4):
            sc.dma("pool", RH[:, tb, :], x_d[g * 512 + tb * 128:g * 512 + (tb + 1) * 128, :], [], [bRH[tb]],
                   "h%d" % tb)
        if debug == "s2x":
            return True
        tok_proj(wo_d, 0, RY, bRY, 16)
        if debug == "s2a":
            return True
        if layer_norm(0, True):
            return True
        if debug == "ln1" and g == 0:
            return True
        for hd in range(4):
            q = qmT[hd % 2]
            bq = b_qm[hd % 2]
            for wt in range(2):
                wv, ba, bb = wload(wq_d[hd * 2 + wt], 16, 256)
                for half in range(2):
                    cc = 2 * wt + half
                    bk = next_bank(0, 4)
                    sc.pe([mm(PB[bk][:, :], wv[:, kc, half * 128:(half + 1) * 128], RY[:, kc, :], kc == 0, kc == 15)
                           for kc in range(16)], reads=[ba, bb] + bRY, writes=[bPB[bk]])
                    evac_copy(q[:, cc, :], PB[bk][:, :], [bPB[bk]], [bq])
            for tb in range(4):
                sbk = 4 + tb % 2
                P = ca_P[tb % 2]
                bP = b_caP[tb % 2]
                sc.pe([mm(PB[sbk][:, 0:MEM], q[:, cc, tb * 128:(tb + 1) * 128], KmT[:, hd * 4 + cc, :], cc == 0, cc == 3)
                       for cc in range(4)], [bq, bKm], [bPB[sbk]])
                rmax = ln_st[:, 32 + 4 * (tb % 2):33 + 4 * (tb % 2)]
                nm = ln_st[:, 33 + 4 * (tb % 2):34 + 4 * (tb % 2)]
                rs = ln_st[:, 34 + 4 * (tb % 2):35 + 4 * (tb % 2)]
                rr = ln_st[:, 35 + 4 * (tb % 2):36 + 4 * (tb % 2)]
                bst = b_caP[tb % 2]
                sc.op("dve", lambda e: e.tensor_reduce(out=rmax, in_=PB[sbk][:, 0:MEM], axis=AX.X, op=ALU.max),
                      [bPB[sbk]], [bst])
                sc.op("dve", lambda e: e.tensor_scalar(out=nm, in0=rmax, scalar1=-QSCALE_MEM, scalar2=None,
                                                       op0=ALU.mult), [bst], [bst])
                sc.op("act", lambda e: e.activation(out=P, in_=PB[sbk][:, 0:MEM], func=AF.Exp, scale=QSCALE_MEM,
                                                    bias=nm), [bPB[sbk], bst], [bP])
                sc.op("dve", lambda e: e.tensor_reduce(out=rs, in_=P, axis=AX.X, op=ALU.add), [bP], [bP])
                sc.op("dve", lambda e: e.reciprocal(out=rr, in_=rs), [bP], [bP])
                sc.op("dve", lambda e: e.tensor_scalar(out=ca_Pn, in0=P, scalar1=rr, scalar2=None, op0=ALU.mult),
                      [bP], [b_caPn])
                ptps = PB[6 + tb % 2][:, 0:128].bitcast(BF16)
                sc.pe([tp(ptps[:, mb * 128:(mb + 1) * 128], ca_Pn[:, mb * 128:(mb + 1) * 128], identb)
                       for mb in range(2)], [b_caPn, bC], [bPB[6 + tb % 2]])
                evac_copy(ca_PT[:, :, tb * 128:(tb + 1) * 128], ptps.rearrange("p (m t) -> p m t", t=128),
                          [bPB[6 + tb % 2]], [b_caPT])
            for cc in range(4):
                c = hd * 4 + cc
                bk = next_bank(0, 4)
                sc.pe([mm(PB[bk][:, :], Vm[:, mb, c * 128:(c + 1) * 128], ca_PT[:, mb, :], mb == 0, mb == 1)
                       for mb in range(2)], [bVm, b_caPT], [bPB[bk]])
                evac_copy(RA[:, c, :], PB[bk][:, :], [bPB[bk]], [bRA[c]])
        tok_proj(wom_d, 0, RA, bRA, 16)
        layer_norm(1, True)
        if debug == "ln2" and g == 0:
            return True
        for qd in range(4):
            for j in range(11):
                fc = qd * 11 + j
                wv, ba, bb = wload(wgu_d[fc], 16, 256)
                bg = next_bank(0, 4)
                sc.pe([mm(PB[bg][:, :], wv[:, kc, 0:128], RY[:, kc, :], kc == 0, kc == 15) for kc in range(16)],
                      reads=[ba] + bRY, writes=[bPB[bg]])
                bu = next_bank(0, 4)
                sc.pe([mm(PB[bu][:, :], wv[:, kc, 128:256], RY[:, kc, :], kc == 0, kc == 15) for kc in range(16)],
                      reads=[bb] + bRY, writes=[bPB[bu]])
                sc.op("act", lambda e: e.activation(out=sg, in_=PB[bg][:, :], func=AF.Silu), [bPB[bg]], [b_sg])
                sc.op("dve", lambda e: e.tensor_tensor(out=RA[:, j, :], in0=sg, in1=PB[bu][:, :], op=ALU.mult),
                      [b_sg, bPB[bu]], [bRA[j]])
            tok_proj(wdn_d, qd * 8, RA, bRA[0:11], 11, first=(qd == 0))
        layer_norm(2, False)
        for tb in range(4):
            sc.dma("sp", out_d[g * 512 + tb * 128:g * 512 + (tb + 1) * 128, :], RH[:, tb, :], [bRH[tb]], [],
                   "o%d" % tb)
        return False

    stop = False
    for g in range(G):
        stop = stage1(g)
        if stop:
            break
        sc.barrier()
        stop = stage2(g)
        if stop:
            break
        sc.barrier()
    if debug is not None:
        sc.barrier()
        dbg_views = {
            "inproj": [QaT, QbT],
            "swa": [RY],
            "sb": [RY],
            "ln1": [RY],
            "ln2": [RY],
        }
        if (debug.startswith("swa") and debug != "swa") or debug.startswith("s2"):
            sc.dma("sp", dbg_d[:, 0:NCF], CF[:, :], [bC], [], "dbg")
        elif debug in ("inproj",):
            sc.dma("sp", dbg_d[:, 0:4096], RH[:, 2:4, :].rearrange("p a b -> p (a b)"), bRH, [], "dbg")
        elif debug in ("swa", "sb", "ln1", "ln2", "full"):
            if debug == "swa":
                sc.dma("sp", dbg_d[:, 0:2048], RY[:, 0:8, :].rearrange("p a b -> p (a b)").bitcast(F32), bRY, [], "dbg")
            else:
                sc.dma("sp", dbg_d[:, 0:4096], RY[:, :, :].rearrange("p a b -> p (a b)").bitcast(F32), bRY, [], "dbg")
            if debug in ("ln1", "ln2"):
                pass
        sc.barrier()
    sc.finish()
    return nc


def _consts(sinks, g_swa, g_sb):
    cf = np.zeros((128, NCF), np.float32)
    cf[:, CF_ID:CF_ID + 128] = np.eye(128, dtype=np.float32)
    bd = np.zeros((128, 128), np.float32)
    bd[:64, :64] = 1.0
    bd[64:, 64:] = 1.0
    cf[:, CF_BD:CF_BD + 128] = bd
    q = np.arange(128)[:, None]
    s = np.arange(128)[None, :]
    prev = np.where(s > q, -(128.0 + q - s), -1e9)
    cur = np.where(s <= q, -(q - s).astype(np.float64), -1e9)
    cf[:, CF_ND:CF_ND + 128] = prev
    cf[:, CF_ND + 128:CF_ND + 256] = cur
    cf[:, CF_ND0:CF_ND0 + 128] = -1e9
    cf[:, CF_ND0 + 128:CF_ND0 + 256] = cur
    gcat = np.concatenate([g_swa.reshape(-1), g_sb.reshape(-1)]).astype(np.float32)
    cf[:, CF_G:CF_G + 16] = gcat.reshape(16, 128).T
    cf[:, CF_SK:CF_SK + 16] = np.broadcast_to(sinks.reshape(1, 16), (128, 16))
    cf[:, CF_M01:CF_M01 + 128] = np.where(q < s, 1.0, 0.0)
    cb = np.zeros((128, NCB), np.float32)
    cb[:, CB_ID:CB_ID + 128] = np.eye(128, dtype=np.float32)
    j = np.arange(128)[:, None]
    sidx = np.arange(128)[None, :]
    cb[:, CB_NT:CB_NT + 128] = np.where(j >= sidx, -1.0, 0.0)
    cb[:, CB_NSL:CB_NSL + 128] = np.where(j < sidx, -1.0, 0.0)
    cb[:, CB_MK:CB_MK + 128] = np.where(j >= sidx, NEG, 0.0)
    return cf, cb


def _tile(W, nk, col0s, r0=0):
    out = np.empty((len(col0s), 128, nk * 256), np.float32)
    for t, c0 in enumerate(col0s):
        out[t] = W[r0:r0 + nk * 128, c0:c0 + 256].reshape(nk, 128, 256).transpose(1, 0, 2).reshape(128, nk * 256)
    return out


def _w_in2(w_in):
    w = w_in
    qa = w[:, 0:1024]
    ka = w[:, 1024:1280]
    va = w[:, 1280:1536]
    qb = w[:, 1536:2560]
    kb = w[:, 2560:3584]
    vb = w[:, 3584:4608]
    kad = np.concatenate([np.concatenate([ka[:, k * 64:(k + 1) * 64]] * 2, axis=1) for k in range(4)], axis=1)
    return np.ascontiguousarray(np.concatenate([qa, kad, qb, kb, va, vb], axis=1))


_NC_CACHE = {}


def _prep_shared(inp):
    f = lambda a: np.ascontiguousarray(np.asarray(a, dtype=np.float32))
    cf, cb = _consts(f(inp["sinks"])[0], f(inp["g_swa"])[0], f(inp["g_sb"])[0])
    lnp = np.stack([f(inp[k])[0] for k in ("ln1_g", "ln1_b", "ln2_g", "ln2_b", "ln3_g", "ln3_b")])
    wgu = f(inp["w_gate_up"])[0]
    wgu2 = np.concatenate([wgu[:, :DFF].reshape(D, NFC, 128), wgu[:, DFF:].reshape(D, NFC, 128)], axis=2)
    wgu2 = wgu2.reshape(D, NFC * 256)
    wdn = f(inp["w_down"])[0]
    c8 = [cb * 256 for cb in range(8)]
    return {
        "w_in2": _tile(_w_in2(f(inp["w_in"])[0]), 16, [t * 256 for t in range(19)]),
        "w_o": _tile(f(inp["w_o"])[0], 16, c8),
        "w_q_mem": _tile(f(inp["w_q_mem"])[0], 16, c8),
        "w_kv_mem": _tile(f(inp["w_kv_mem"])[0], 16, [t * 256 for t in range(16)]),
        "w_o_mem": _tile(f(inp["w_o_mem"])[0], 16, c8),
        "w_gate_up": _tile(wgu2, 16, [t * 256 for t in range(NFC)]),
        "w_down": np.concatenate([_tile(wdn, 11, c8, r0=qd * 11 * 128) for qd in range(4)], axis=0),
        "cf32": cf, "cb16": cb, "lnp": np.ascontiguousarray(lnp),
    }


def kernel(**inputs):
    x = np.asarray(inputs["x"], dtype=np.float32)
    mem = np.asarray(inputs["mem"], dtype=np.float32)
    Bn, S, _ = x.shape
    shared = _prep_shared(inputs)
    if S not in _NC_CACHE:
        _NC_CACHE[S] = build(S)
    nc = _NC_CACHE[S]
    in_maps = []
    for b in range(Bn):
        m = dict(shared)
        m["x"] = np.ascontiguousarray(x[b])
        m["xT"] = np.ascontiguousarray(x[b].T)
        m["memT"] = np.ascontiguousarray(mem[b].T)
        in_maps.append(m)
    res = run_bass_kernel_spmd(nc, in_maps, core_ids=list(range(Bn)))
    return np.stack([r["out"] for r in res.results], axis=0).astype(np.float32)
```

```python
import math
import numpy as np
import concourse.bass as bass
import concourse.mybir as mybir
from concourse.bass_utils import run_bass_kernel_spmd

AF = mybir.ActivationFunctionType
ALU = mybir.AluOpType
AX = mybir.AxisListType
F32 = mybir.dt.float32
BF16 = mybir.dt.bfloat16

D = 2048
DFF = 5632
NFC = DFF // 128
ALPHA = 2.0 ** 0.25
LN_EPS = 1e-5
RMS_EPS = 1e-6
MEM = 256
NEG = -30000.0
SLOPES = [2.0 ** (-8.0 * (h + 1) / 16) for h in range(16)]
QSCALE_MEM = 1.0 / math.sqrt(512.0)

CF_ID, CF_BD, CF_ND, CF_ND0, CF_G, CF_SK, CF_M01, NCF = 0, 128, 256, 512, 768, 784, 800, 928
CB_ID, CB_NT, CB_NSL, CB_MK, NCB = 0, 128, 256, 384, 512

W_IN2 = 4864


class Buf:
    __slots__ = ("name", "w", "r")

    def __init__(self, name):
        self.name = name
        self.w = None
        self.r = {}


class _Eng:
    def __init__(self, name, h, sem):
        self.name, self.h, self.sem, self.cnt, self.waited = name, h, sem, 0, {}


class _Stream:
    def __init__(self, name, sem):
        self.name, self.sem, self.cnt = name, sem, 0


class Sched:
    def __init__(self, nc):
        self.nc = nc
        self.E = {}
        for n, h in (("pe", nc.tensor), ("act", nc.scalar), ("dve", nc.vector)):
            self.E[n] = _Eng(n, h, nc.alloc_semaphore("s_" + n))
        for n, h in (("pool", nc.gpsimd), ("sp", nc.sync)):
            self.E[n] = _Eng(n, h, None)
        self.streams = {}
        self.sems = {}
        for n in ("pe", "act", "dve"):
            self.sems[n] = self.E[n].sem

    def stream(self, name):
        if name not in self.streams:
            st = _Stream(name, self.nc.alloc_semaphore("d_" + name))
            self.streams[name] = st
            self.sems["d_" + name] = st.sem
        return self.streams[name]

    def _deps(self, reads, writes):
        deps = {}

        def add(kv):
            k, v = kv
            if deps.get(k, 0) < v:
                deps[k] = v
        for b in reads:
            if b.w is not None:
                add(b.w)
        for b in writes:
            if b.w is not None:
                add(b.w)
            for kv in b.r.items():
                add(kv)
        return deps

    def _wait(self, eng, deps):
        for k, v in deps.items():
            if k == eng.name and eng.name == "pe":
                continue
            if eng.waited.get(k, 0) < v:
                eng.h.wait_ge(self.sems[k], v)
                eng.waited[k] = v

    def _mark(self, key, val, reads, writes):
        for b in reads:
            if b.r.get(key, 0) < val:
                b.r[key] = val
        for b in writes:
            b.w = (key, val)
            b.r = {}

    def op(self, en, fn, reads=(), writes=()):
        eng = self.E[en]
        self._wait(eng, self._deps(reads, writes))
        ins = fn(eng.h)
        eng.cnt += 1
        ins.then_inc(eng.sem, 1)
        self._mark(en, eng.cnt, reads, writes)

    def pe(self, fns, reads=(), writes=()):
        eng = self.E["pe"]
        self._wait(eng, self._deps(reads, writes))
        ins = None
        for fn in fns:
            ins = fn(eng.h)
        eng.cnt += 1
        ins.then_inc(eng.sem, 1)
        self._mark("pe", eng.cnt, reads, writes)

    def dma(self, qn, out, in_, reads, writes, stream):
        q = self.E[qn]
        st = self.stream(stream)
        key = "d_" + stream
        deps = self._deps(reads, writes)
        if st.cnt > 0:
            deps[key] = max(deps.get(key, 0), st.cnt)
        self._wait(q, deps)
        q.h.dma_start(out=out, in_=in_).then_inc(st.sem, 16)
        st.cnt += 16
        self._mark(key, st.cnt, reads, writes)

    def barrier(self):
        tgt = {n: self.E[n].cnt for n in ("pe", "act", "dve")}
        for st in self.streams.values():
            tgt["d_" + st.name] = st.cnt
        for e in self.E.values():
            for k, v in tgt.items():
                if k == e.name or v == 0:
                    continue
                if e.waited.get(k, 0) < v:
                    e.h.wait_ge(self.sems[k], v)
                    e.waited[k] = v

    def finish(self):
        sp = self.E["sp"]
        for st in self.streams.values():
            if st.cnt and sp.waited.get("d_" + st.name, 0) < st.cnt:
                sp.h.wait_ge(st.sem, st.cnt)
                sp.waited["d_" + st.name] = st.cnt


def build(S=2048, debug=None):
    G = S // 512
    NB = S // 128
    nc = bass.Bass("TRN2", target_bir_lowering=False)
    sc = Sched(nc)

    def dram(name, shape, kind="ExternalInput"):
        return nc.dram_tensor(name, shape, F32, kind=kind).ap()

    xT_d = dram("xT", [D, S])
    x_d = dram("x", [S, D])
    memT_d = dram("memT", [D, MEM])
    win_d = dram("w_in2", [19, 128, 4096])
    wo_d = dram("w_o", [8, 128, 4096])
    wq_d = dram("w_q_mem", [8, 128, 4096])
    wkv_d = dram("w_kv_mem", [16, 128, 4096])
    wom_d = dram("w_o_mem", [8, 128, 4096])
    wgu_d = dram("w_gate_up", [NFC, 128, 4096])
    wdn_d = dram("w_down", [32, 128, 11 * 256])
    cf_d = dram("cf32", [128, NCF])
    cb_d = dram("cb16", [128, NCB])
    lnp_d = dram("lnp", [6, D])
    out_d = dram("out", [S, D], kind="ExternalOutput")
    dbg_d = None
    if debug is not None:
        dbg_d = dram("dbg", [128, 16 * 512], kind="ExternalOutput")

    sb = nc.alloc_sbuf_tensor
    KbT = sb("KbT", [128, 8, S], BF16)
    Vb = sb("Vb", [128, NB, 1024], BF16)
    KaT = sb("KaT", [128, 4, 640], BF16)
    Va = sb("Va", [128, 5, 256], BF16)
    KmT = sb("KmT", [128, 16, MEM], BF16)
    Vm = sb("Vm", [128, 2, D], BF16)
    CF = sb("CF", [128, NCF], F32)
    CB = sb("CB", [128, NCB], BF16)
    RH = sb("RH", [128, 4, D], F32)
    RHb = RH[:, :, :].rearrange("p a b -> p (a b)").bitcast(BF16)
    xT = RHb[:, 0:8192].rearrange("p (c t) -> p c t", t=512)
    QaT = RHb[:, 8192:12288].rearrange("p (c t) -> p c t", t=512)
    QbT = RHb[:, 12288:16384].rearrange("p (c t) -> p c t", t=512)
    memT = RHb[:, 0:4096].rearrange("p (c t) -> p c t", t=MEM)
    RY = sb("RY", [128, 16, 512], BF16)
    WS = [sb("WS%d" % i, [128, 4096], BF16) for i in range(3)]
    RU = sb("RU", [128, 15360], BF16)
    RUf = RU[:, :].bitcast(F32)
    sb_e = [RUf[:, i * 512:(i + 1) * 512] for i in range(3)]
    sb_xc = [RUf[:, 1536 + i * 512:1536 + (i + 1) * 512] for i in range(2)]
    sb_sp = [RU[:, 5120 + i * 512:5120 + (i + 1) * 512] for i in range(2)]
    sb_sp.append(RU[:, 14464:14976])
    sb_at = [RU[:, 6144 + i * 512:6144 + (i + 1) * 512] for i in range(2)]
    rms_sq = RUf[:, 3584:4096]
    rms_l = RUf[:, 4096:4608]
    rms_r = RUf[:, 4608:5120]
    swa_S = RUf[:, 5120:6144].rearrange("p (h k) -> p h k", k=256)
    swa_Pn = RU[:, 12288:13312].rearrange("p (h k) -> p h k", k=256)
    swa_PT = RU[:, 13312:14336]
    swa_st = RUf[:, 7168:7232]
    RA = RU[:, 0:8192].rearrange("p (c t) -> p c t", t=512)
    GBg = RUf[:, 0:2048]
    GBb = RUf[:, 2048:4096]
    qmT = [RU[:, 8192 + i * 2048:8192 + (i + 1) * 2048].rearrange("p (c t) -> p c t", t=512)
           for i in range(2)]
    ca_P = [RUf[:, 6144 + i * 256:6144 + (i + 1) * 256] for i in range(2)]
    ca_Pn = RU[:, 13312:13568]
    ca_PT = RU[:, 13568:14592].rearrange("p (m t) -> p m t", t=512)
    sg = RUf[:, 7296:7552].bitcast(BF16)
    ln_st = sb("ln_st", [128, 64], F32)
    swa_S2 = sb("swa_S2", [128, 4, 256], F32)
    swa_Pn2 = sb("swa_Pn2", [128, 4, 256], BF16)
    swa_PT2 = sb("swa_PT2", [128, 1024], BF16)
    swa_st2 = sb("swa_st2", [128, 64], F32)
    ln_st2 = sb("ln_st2", [128, 64], F32)
    PB = [nc.alloc_psum_tensor("PB%d" % i, [128, 512], F32) for i in range(8)]
    bPB = [Buf("PB%d" % i) for i in range(8)]

    bW = [(Buf("W%da" % i), Buf("W%db" % i)) for i in range(3)]
    bRH = [Buf("h%d" % i) for i in range(4)]
    bRY = [Buf("RY%d" % i) for i in range(16)]
    bRA = [Buf("RA%d" % i) for i in range(16)]
    bKbT = {}
    bVb = {}
    bKa = [[Buf("Ka%d_%d" % (k, s)) for s in range(5)] for k in range(4)]
    bVa = [Buf("Va%d" % s) for s in range(5)]
    bKm = Buf("KmT")
    bVm = Buf("Vm")
    bC = Buf("consts")
    bXT = bRH[0:2]
    bQa = bRH[2]
    bQb = bRH[3]

    def B(d, key):
        if key not in d:
            d[key] = Buf(str(key))
        return d[key]

    wslot = [0]

    def wload(src_ap, nk, ncols):
        i = wslot[0] % 3
        wslot[0] += 1
        view = WS[i][:, 0:nk * ncols].rearrange("p (c n) -> p c n", n=ncols)
        ba, bb = bW[i]
        sc.dma("pool", WS[i][:, 0:nk * ncols], src_ap, [], [ba, bb], "w%d" % i)
        return view, ba, bb

    evac_rr = [0]

    def evac_copy(out, in_, reads, writes, scale=None):
        evac_rr[0] += 1
        if evac_rr[0] % 2 == 0:
            if scale is None:
                sc.op("act", lambda e: e.activation(out=out, in_=in_, func=AF.Copy), reads, writes)
            else:
                sc.op("act", lambda e: e.activation(out=out, in_=in_, func=AF.Copy, scale=scale), reads, writes)
        else:
            if scale is None:
                sc.op("dve", lambda e: e.tensor_copy(out=out, in_=in_), reads, writes)
            else:
                sc.op("dve", lambda e: e.tensor_scalar(out=out, in0=in_, scalar1=scale, scalar2=None,
                                                        op0=ALU.mult), reads, writes)

    def mm(out, lhsT, rhs, start, stop):
        return lambda e: e.matmul(out, lhsT=lhsT, rhs=rhs, start=start, stop=stop)

    def mmx(out, lhsT, rhs, start, stop):
        return lambda e: e.matmul(out, lhsT=lhsT, rhs=rhs, start=start, stop=stop, skip_group_check=True)

    def tp(out, in_, ident):
        return lambda e: e.transpose(out, in_, ident)

    def dbg_dump(ap_list):
        pass

    sc.dma("sp", CF[:, :], cf_d, [], [bC], "cf")
    sc.dma("pool", CB[:, :], cb_d, [], [bC], "cb")
    ident32 = CF[:, CF_ID:CF_ID + 128]
    BDm = CF[:, CF_BD:CF_BD + 128]
    negdist = CF[:, CF_ND:CF_ND + 256]
    negdist0 = CF[:, CF_ND0:CF_ND0 + 256]
    gcol = CF[:, CF_G:CF_G + 16]
    sinks = CF[:, CF_SK:CF_SK + 16]
    mask01 = CF[:, CF_M01:CF_M01 + 128]
    identb = CB[:, CB_ID:CB_ID + 128]
    NegTri = CB[:, CB_NT:CB_NT + 128]
    NegSL = CB[:, CB_NSL:CB_NSL + 128]
    MaskT = CB[:, CB_MK:CB_MK + 128]
    sc.op("dve", lambda e: e.memset(KaT[:, :, 0:128], 0.0), [], [bKa[k][0] for k in range(4)])
    sc.op("dve", lambda e: e.memset(Va[:, 0, :], 0.0), [], [bVa[0]])

    sc.dma("pool", memT, memT_d.rearrange("(c p) m -> p c m", p=128), [], bXT, "xT")
    pbi = [0]

    def next_bank(lo=0, hi=8):
        pbi[0] += 1
        return lo + pbi[0] % (hi - lo)

    for wt in range(8):
        wv, ba, bb = wload(wkv_d[wt], 16, 256)
        for half in range(2):
            c = 2 * wt + half
            bk = next_bank(0, 4)
            sc.pe([mm(PB[bk][:, 0:MEM], wv[:, kc, half * 128:(half + 1) * 128], memT[:, kc, :],
                      kc == 0, kc == 15) for kc in range(16)],
                  reads=[ba, bb, bC] + bXT, writes=[bPB[bk]])
            evac_copy(KmT[:, c, :], PB[bk][:, 0:MEM], [bPB[bk]], [bKm])
    for ct in range(8):
        wv, ba, bb = wload(wkv_d[8 + ct], 16, 256)
        bk = next_bank(0, 4)
        for mb in range(2):
            sc.pe([mm(PB[bk][:, mb * 256:(mb + 1) * 256], memT[:, kc, mb * 128:(mb + 1) * 128], wv[:, kc, :],
                      kc == 0, kc == 15) for kc in range(16)],
                  reads=[ba, bb] + bXT, writes=[bPB[bk]])
        evac_copy(Vm[:, :, ct * 256:(ct + 1) * 256],
                  PB[bk][:, :].rearrange("p (m n) -> p m n", n=256), [bPB[bk]], [bVm])
    sc.barrier()

    def rmsnorm_pair(obank, c):
        ob = PB[obank]
        sc.op("act", lambda e: e.activation(out=rms_sq, in_=ob[:, :], func=AF.Square),
              [bPB[obank]], [b_rms_sq])
        ssb = 5 if obank != 5 else 7
        sc.pe([mm(PB[ssb][:, :], BDm, rms_sq, True, True)], [b_rms_sq, bC], [bPB[ssb]])
        sc.op("act", lambda e: e.activation(out=rms_l, in_=PB[ssb][:, :], func=AF.Ln, scale=1.0 / 64.0,
                                            bias=RMS_EPS), [bPB[ssb]], [b_rms_l])
        sc.op("act", lambda e: e.activation(out=rms_r, in_=rms_l, func=AF.Exp, scale=-0.5),
              [b_rms_l], [b_rms_r])
        sc.op("dve", lambda e: e.scalar_tensor_tensor(out=RY[:, c, :], in0=ob[:, :], scalar=gcol[:, c:c + 1],
                                                      in1=rms_r, op0=ALU.mult, op1=ALU.mult),
              [bPB[obank], b_rms_r, bC], [bRY[c]])

    b_rms_sq, b_rms_l, b_rms_r = Buf("rms_sq"), Buf("rms_l"), Buf("rms_r")
    b_e = [Buf("e%d" % i) for i in range(3)]
    b_xc = [Buf("xc%d" % i) for i in range(2)]
    b_sp = [Buf("sp%d" % i) for i in range(3)]
    b_at = [Buf("at%d" % i) for i in range(2)]
    b_swaS, b_swaPn, b_swaPT, b_swast = Buf("swaS"), Buf("swaPn"), Buf("swaPT"), Buf("swast")
    b_swaS2, b_swaPn2, b_swaPT2, b_swast2 = Buf("swaS2"), Buf("swaPn2"), Buf("swaPT2"), Buf("swast2")
    b_qm = [Buf("qm0"), Buf("qm1")]
    b_caP = [Buf("caP0"), Buf("caP1")]
    b_caPn, b_caPT, b_sg, b_lnst = Buf("caPn"), Buf("caPT"), Buf("sg"), Buf("lnst")
    b_gb = Buf("gb_dummy")
    b_lnst2 = Buf("lnst2")

    def stage1(g):
        sc.dma("pool", xT, xT_d[:, g * 512:(g + 1) * 512].rearrange("(c p) t -> p c t", p=128),
               [], bXT, "xT")
        for wt in range(14):
            wv, ba, bb = wload(win_d[wt], 16, 256)
            for half in range(2):
                c = 2 * wt + half
                bk = next_bank(0, 4)
                sc.pe([mm(PB[bk][:, :], wv[:, kc, half * 128:(half + 1) * 128], xT[:, kc, :], kc == 0, kc == 15)
                       for kc in range(16)], reads=[ba, bb] + bXT, writes=[bPB[bk]])
                if c < 8:
                    evac_copy(QaT[:, c, :], PB[bk][:, :], [bPB[bk]], [bQa], scale=0.125)
                elif c < 12:
                    k = c - 8
                    evac_copy(KaT[:, k, 128:640], PB[bk][:, :], [bPB[bk]], [bKa[k][s] for s in range(1, 5)])
                elif c < 20:
                    evac_copy(QbT[:, c - 12, :], PB[bk][:, :], [bPB[bk]], [bQb], scale=0.125)
                else:
                    evac_copy(KbT[:, c - 20, g * 512:(g + 1) * 512], PB[bk][:, :], [bPB[bk]],
                              [B(bKbT, (c - 20, g))])
        for vt in range(5):
            wv, ba, bb = wload(win_d[14 + vt], 16, 256)
            for tp2 in range(2):
                bk = next_bank(0, 4)
                for t2 in range(2):
                    tb = 2 * tp2 + t2
                    sc.pe([mm(PB[bk][:, t2 * 256:(t2 + 1) * 256], xT[:, kc, tb * 128:(tb + 1) * 128], wv[:, kc, :],
                              kc == 0, kc == 15) for kc in range(16)],
                          reads=[ba, bb] + bXT, writes=[bPB[bk]])
                src = PB[bk][:, :].rearrange("p (t n) -> p t n", n=256)
                if vt == 0:
                    evac_copy(Va[:, 1 + 2 * tp2:3 + 2 * tp2, :], src, [bPB[bk]], [bVa[1 + 2 * tp2], bVa[2 + 2 * tp2]])
                else:
                    evac_copy(Vb[:, g * 4 + 2 * tp2:g * 4 + 2 * tp2 + 2, (vt - 1) * 256:vt * 256], src, [bPB[bk]],
                              [B(bVb, (g * 4 + 2 * tp2, vt - 1)), B(bVb, (g * 4 + 2 * tp2 + 1, vt - 1))])
        if debug == "inproj" and g == 0:
            return True
        def swa_unit(kg, qb, R):
            S_, Pn_, PT_, st_ = R["S"], R["Pn"], R["PT"], R["st"]
            bS, bPn, bPT, bst = R["bS"], R["bPn"], R["bPT"], R["bst"]
            Sb, ptb, ob = R["Sb"], R["ptb"], R["ob"]
            gb = 4 * g + qb
            nd = negdist0 if gb == 0 else negdist
            fns = []
            for hh in range(4):
                h = 4 * kg + hh
                base = (h % 2) * 64
                fns.append(mm(PB[Sb[hh % 2]][:, (hh // 2) * 256:(hh // 2 + 1) * 256],
                              QaT[base:base + 64, h // 2, qb * 128:(qb + 1) * 128],
                              KaT[base:base + 64, kg, qb * 128:qb * 128 + 256], True, True))
            sc.pe(fns, reads=[bQa, bKa[kg][qb], bKa[kg][qb + 1]], writes=[bPB[Sb[0]], bPB[Sb[1]]])
            yield
            for hh in range(4):
                h = 4 * kg + hh
                sc.op("dve", lambda e, hh=hh, h=h: e.scalar_tensor_tensor(
                    out=S_[:, hh, :], in0=nd, scalar=SLOPES[h],
                    in1=PB[Sb[hh % 2]][:, (hh // 2) * 256:(hh // 2 + 1) * 256], op0=ALU.mult, op1=ALU.add),
                    [bPB[Sb[hh % 2]], bC], [bS])
                yield
            rowmax = st_[:, 0:4]
            negm = st_[:, 4:8]
            dsk = st_[:, 8:12]
            rowsum = st_[:, 12:16]
            es = st_[:, 16:20]
            rden = st_[:, 20:24]
            sk = sinks[:, 4 * kg:4 * kg + 4]
            sc.op("dve", lambda e: e.tensor_reduce(out=rowmax, in_=S_, axis=AX.X, op=ALU.max), [bS], [bst])
            yield
            sc.op("dve", lambda e: e.tensor_tensor(out=rowmax, in0=rowmax, in1=sk, op=ALU.max), [bst, bC], [bst])
            yield
            sc.op("dve", lambda e: e.tensor_scalar(out=negm, in0=rowmax, scalar1=-1.0, scalar2=None,
                                                   op0=ALU.mult), [bst], [bst])
            yield
            sc.op("dve", lambda e: e.tensor_tensor(out=dsk, in0=sk, in1=rowmax, op=ALU.subtract),
                  [bst, bC], [bst])
            yield
            for hh in range(4):
                sc.op("act", lambda e, hh=hh: e.activation(out=S_[:, hh, :], in_=S_[:, hh, :], func=AF.Exp,
                                                           bias=negm[:, hh:hh + 1]), [bS, bst], [bS])
                yield
            sc.op("dve", lambda e: e.tensor_reduce(out=rowsum, in_=S_, axis=AX.X, op=ALU.add), [bS], [bst])
            yield
            sc.op("act", lambda e: e.activation(out=es, in_=dsk, func=AF.Exp), [bst], [bst])
            yield
            sc.op("dve", lambda e: e.tensor_tensor(out=rden, in0=rowsum, in1=es, op=ALU.add), [bst], [bst])
            yield
            sc.op("dve", lambda e: e.reciprocal(out=rden, in_=rden), [bst], [bst])
            yield
            for hh in range(4):
                sc.op("dve", lambda e, hh=hh: e.tensor_scalar(out=Pn_[:, hh, :], in0=S_[:, hh, :],
                                                              scalar1=rden[:, hh:hh + 1], scalar2=None,
                                                              op0=ALU.mult), [bS, bst], [bPn])
                yield
            ptps = PB[ptb][:, :].bitcast(BF16)
            sc.pe([tp(ptps[:, (hh * 2 + kb) * 128:(hh * 2 + kb + 1) * 128],
                      Pn_[:, hh, kb * 128:(kb + 1) * 128], identb)
                   for hh in range(4) for kb in range(2)], [bPn, bC], [bPB[ptb]])
            yield
            evac_copy(PT_, ptps, [bPB[ptb]], [bPT])
            yield
            for hh in range(4):
                h = 4 * kg + hh
                base = (h % 2) * 64
                obk = ob[hh // 2]
                sc.pe([mm(PB[obk][base:base + 64, qb * 128:(qb + 1) * 128],
                          Va[:, qb + kb, kg * 64:(kg + 1) * 64],
                          PT_[:, (hh * 2 + kb) * 128:(hh * 2 + kb + 1) * 128], kb == 0, kb == 1)
                       for kb in range(2)],
                      [bPT, bVa[qb], bVa[qb + 1]], [bPB[obk]])
                yield

        for kg in range(4):
            ob = [4, 6]
            RS = [dict(S=swa_S, Pn=swa_Pn, PT=swa_PT, st=swa_st, bS=b_swaS, bPn=b_swaPn, bPT=b_swaPT,
                       bst=b_swast, Sb=[0, 1], ptb=2, ob=ob),
                  dict(S=swa_S2[:, :, :], Pn=swa_Pn2[:, :, :], PT=swa_PT2[:, :], st=swa_st2[:, :], bS=b_swaS2, bPn=b_swaPn2, bPT=b_swaPT2,
                       bst=b_swast2, Sb=[3, 5], ptb=7, ob=ob)]
            for qb0 in (0, 2):
                gens = [swa_unit(kg, qb0, RS[0]), swa_unit(kg, qb0 + 1, RS[1])]
                while gens:
                    for gen in list(gens):
                        try:
                            next(gen)
                        except StopIteration:
                            gens.remove(gen)
            rmsnorm_pair(ob[0], 2 * kg)
            rmsnorm_pair(ob[1], 2 * kg + 1)
        if g + 1 < G:
            for k in range(4):
                sc.op("dve", lambda e, k=k: e.tensor_copy(out=KaT[:, k, 0:128], in_=KaT[:, k, 512:640]),
                      [bKa[k][4]], [bKa[k][0]])
            sc.op("dve", lambda e: e.tensor_copy(out=Va[:, 0, :], in_=Va[:, 4, :]), [bVa[4]], [bVa[0]])
        if debug == "swa" and g == G - 1:
            return True
        for p in range(8):
            obk = 4 if p % 2 == 0 else 6
            tiles = []
            for kb in range(4 * g + 3, -1, -1):
                for hd in range(2):
                    tiles.append((hd, kb))
            n = len(tiles)
            Cb = [2, 3]

            def geom(t):
                hd, kb = tiles[t]
                j = kb - 4 * g
                c0 = max(j, 0) * 128
                return hd, kb, j, c0, hd * 64

            def stA(t):
                hd, kb, j, c0, base = geom(t)
                zb = t % 2
                fns = [mm(PB[zb][:, c0:512], KbT[base:base + 64, p, kb * 128:(kb + 1) * 128],
                          QbT[base:base + 64, p, c0:512], True, True)]
                sc.pe(fns, [B(bKbT, (p, kb // 4)), bQb, bC], [bPB[zb]])

            def stB(t):
                hd, kb, j, c0, base = geom(t)
                zb = t % 2
                sc.op("act", lambda e: e.activation(out=sb_e[t % 3][:, c0:512], in_=PB[zb][:, c0:512], func=AF.Exp),
                      [bPB[zb]], [b_e[t % 3]])
                if j >= 0:
                    sc.op("dve", lambda e: e.tensor_tensor(out=sb_e[t % 3][:, c0:c0 + 128], in0=sb_e[t % 3][:, c0:c0 + 128],
                                                           in1=mask01, op=ALU.mult), [b_e[t % 3], bC], [b_e[t % 3]])
                sc.op("act", lambda e: e.activation(out=sb_sp[t % 3][:, c0:512], in_=sb_e[t % 3][:, c0:512],
                                                    func=AF.Ln, bias=1.0), [b_e[t % 3]], [b_sp[t % 3]])

            def stC(t):
                hd, kb, j, c0, base = geom(t)
                first = (kb == 4 * g + 3)
                sc.pe([mmx(PB[Cb[hd]][:, c0:512], NegTri, sb_sp[t % 3][:, c0:512], first, True)],
                      [b_sp[t % 3], bC], [bPB[Cb[hd]]])

            def stD(t):
                hd, kb, j, c0, base = geom(t)
                sc.op("act", lambda e: e.activation(out=sb_xc[t % 2][:, c0:512], in_=PB[Cb[hd]][:, c0:512],
                                                    func=AF.Exp), [bPB[Cb[hd]]], [b_xc[t % 2]])

            def stE(t):
                hd, kb, j, c0, base = geom(t)
                last = (kb == 0)
                sc.pe([mmx(PB[Cb[hd]][:, c0:512], NegSL, sb_sp[t % 3][:, c0:512], False, True)],
                      [b_sp[t % 3], bC], [bPB[Cb[hd]]])

            def stE2(t):
                hd, kb, j, c0, base = geom(t)
                sc.op("dve", lambda e: e.tensor_tensor(out=sb_at[t % 2][:, c0:512], in0=sb_e[t % 3][:, c0:512],
                                                       in1=sb_xc[t % 2][:, c0:512], op=ALU.mult),
                      [b_e[t % 3], b_xc[t % 2]], [b_at[t % 2]])

            def stF(t):
                hd, kb, j, c0, base = geom(t)
                first = (kb == 4 * g + 3)
                last = (kb == 0)
                sc.pe([mmx(PB[obk][base:base + 64, c0:512], Vb[:, kb, (2 * p + hd) * 64:(2 * p + hd + 1) * 64],
                          sb_at[t % 2][:, c0:512], first, last)],
                      [b_at[t % 2], B(bVb, (kb, (2 * p + hd) // 4))], [bPB[obk]])

            stA(0)
            for s in range(n + 2):
                if 0 <= s - 1 < n:
                    stC(s - 1)
                if s + 1 < n:
                    stA(s + 1)
                if 0 <= s - 2 < n:
                    stE(s - 2)
                    stE2(s - 2)
                    stF(s - 2)
                if s < n:
                    stB(s)
                if 0 <= s - 1 < n:
                    stD(s - 1)
            rmsnorm_pair(obk, 8 + p)
        if debug == "sb" and g == G - 1:
            return True
        return False

    def ln_load(i):
        sc.dma("pool", GBg, lnp_d[2 * i, :].partition_broadcast(128), [], bRA[0:8], "gbg")
        sc.dma("pool", GBb, lnp_d[2 * i + 1, :].partition_broadcast(128), [], bRA[8:16], "gbb")

    def ln_unit(tb, transposes, lst, blst, banks):
        h = RH[:, tb, :]
        st = lst[:, 0:24].rearrange("p (j s) -> p j s", s=6)
        mv = lst[:, 24:26]
        l = lst[:, 26:27]
        rstd = lst[:, 27:28]
        nmr = lst[:, 28:29]
        for j in range(4):
            sc.op("dve", lambda e, j=j: e.bn_stats(out=st[:, j, :], in_=RH[:, tb, j * 512:(j + 1) * 512]),
                  [bRH[tb]], [blst])
            yield
        sc.op("dve", lambda e: e.bn_aggr(out=mv, in_=lst[:, 0:24]), [blst], [blst])
        yield
        sc.op("act", lambda e: e.activation(out=l, in_=mv[:, 1:2], func=AF.Ln, bias=LN_EPS), [blst], [blst])
        yield
        sc.op("act", lambda e: e.activation(out=rstd, in_=l, func=AF.Exp, scale=-0.5), [blst], [blst])
        yield
        sc.op("dve", lambda e: e.scalar_tensor_tensor(out=nmr, in0=mv[:, 0:1], scalar=-1.0, in1=rstd,
                                                      op0=ALU.mult, op1=ALU.mult), [blst], [blst])
        yield
        sc.op("act", lambda e: e.activation(out=h, in_=h, func=AF.Identity, scale=rstd, bias=nmr),
              [bRH[tb], blst], [bRH[tb]])
        yield
        sc.op("dve", lambda e: e.tensor_tensor(out=h, in0=h, in1=GBg, op=ALU.mult),
              [bRH[tb]] + bRA[0:8], [bRH[tb]])
        yield
        sc.op("dve", lambda e: e.tensor_tensor(out=h, in0=h, in1=GBb, op=ALU.add),
              [bRH[tb]] + bRA[8:16], [bRH[tb]])
        yield
        if transposes:
            for q4 in range(4):
                bk = banks[q4 % 2]
                sc.pe([tp(PB[bk][:, i4 * 128:(i4 + 1) * 128],
                          RH[:, tb, (4 * q4 + i4) * 128:(4 * q4 + i4 + 1) * 128], ident32) for i4 in range(4)],
                      [bRH[tb], bC], [bPB[bk]])
                yield
                evac_copy(RY[:, 4 * q4:4 * q4 + 4, tb * 128:(tb + 1) * 128],
                          PB[bk][:, :].rearrange("p (c t) -> p c t", t=128), [bPB[bk]],
                          bRY[4 * q4:4 * q4 + 4])
                yield

    def layer_norm(i, transposes):
        ln_load(i)
        for tb0 in (0, 2):
            gens = [ln_unit(tb0, transposes, ln_st[:, :], b_lnst, (6, 7)),
                    ln_unit(tb0 + 1, transposes, ln_st2[:, :], b_lnst2, (4, 5))]
            while gens:
                for gen in list(gens):
                    try:
                        next(gen)
                    except StopIteration:
                        gens.remove(gen)

    def tok_proj(w_ap, t0, src, bsrc, nk, first=True):
        for cb in range(8):
            wv, ba, bb = wload(w_ap[t0 + cb], nk, 256)
            for tp2 in range(2):
                bk = next_bank(0, 4)
                for t2 in range(2):
                    tb = 2 * tp2 + t2
                    sc.pe([mm(PB[bk][:, t2 * 256:(t2 + 1) * 256], src[:, kc, tb * 128:(tb + 1) * 128], wv[:, kc, :],
                              kc == 0, kc == nk - 1) for kc in range(nk)],
                          reads=[ba, bb] + bsrc, writes=[bPB[bk]])
                hv = RH[:, 2 * tp2:2 * tp2 + 2, cb * 256:(cb + 1) * 256]
                pv = PB[bk][:, :].rearrange("p (t n) -> p t n", n=256)
                if first:
                    sc.op("dve", lambda e, hv=hv, pv=pv: e.scalar_tensor_tensor(
                        out=hv, in0=hv, scalar=ALPHA, in1=pv, op0=ALU.mult, op1=ALU.add),
                        [bPB[bk], bRH[2 * tp2], bRH[2 * tp2 + 1]], [bRH[2 * tp2], bRH[2 * tp2 + 1]])
                else:
                    sc.op("dve", lambda e, hv=hv, pv=pv: e.tensor_tensor(out=hv, in0=hv, in1=pv, op=ALU.add),
                          [bPB[bk], bRH[2 * tp2], bRH[2 * tp2 + 1]], [bRH[2 * tp2], bRH[2 * tp2 + 1]])

    def stage2(g):
        for tb in range(4):
            sc.dma("pool", RH[:, tb, :], x_d[g * 512 + tb * 128:g * 512 + (tb + 1) * 128, :], [], [bRH[tb]],
                   "h%d" % tb)
        if debug == "s2x":
            return True
        tok_proj(wo_d, 0, RY, bRY, 16)
        if debug == "s2a":
            return True
        if layer_norm(0, True):
            return True
        if debug == "ln1" and g == 0:
            return True
        for hd in range(4):
            q = qmT[hd % 2]
            bq = b_qm[hd % 2]
            for wt in range(2):
                wv, ba, bb = wload(wq_d[hd * 2 + wt], 16, 256)
                for half in range(2):
                    cc = 2 * wt + half
                    bk = next_bank(0, 4)
                    sc.pe([mm(PB[bk][:, :], wv[:, kc, half * 128:(half + 1) * 128], RY[:, kc, :], kc == 0, kc == 15)
                           for kc in range(16)], reads=[ba, bb] + bRY, writes=[bPB[bk]])
                    evac_copy(q[:, cc, :], PB[bk][:, :], [bPB[bk]], [bq])
            for tb in range(4):
                sbk = 4 + tb % 2
                P = ca_P[tb % 2]
                bP = b_caP[tb % 2]
                sc.pe([mm(PB[sbk][:, 0:MEM], q[:, cc, tb * 128:(tb + 1) * 128], KmT[:, hd * 4 + cc, :], cc == 0, cc == 3)
                       for cc in range(4)], [bq, bKm], [bPB[sbk]])
                rmax = ln_st[:, 32 + 4 * (tb % 2):33 + 4 * (tb % 2)]
                nm = ln_st[:, 33 + 4 * (tb % 2):34 + 4 * (tb % 2)]
                rs = ln_st[:, 34 + 4 * (tb % 2):35 + 4 * (tb % 2)]
                rr = ln_st[:, 35 + 4 * (tb % 2):36 + 4 * (tb % 2)]
                bst = b_caP[tb % 2]
                sc.op("dve", lambda e: e.tensor_reduce(out=rmax, in_=PB[sbk][:, 0:MEM], axis=AX.X, op=ALU.max),
                      [bPB[sbk]], [bst])
                sc.op("dve", lambda e: e.tensor_scalar(out=nm, in0=rmax, scalar1=-QSCALE_MEM, scalar2=None,
                                                       op0=ALU.mult), [bst], [bst])
                sc.op("act", lambda e: e.activation(out=P, in_=PB[sbk][:, 0:MEM], func=AF.Exp, scale=QSCALE_MEM,
                                                    bias=nm), [bPB[sbk], bst], [bP])
                sc.op("dve", lambda e: e.tensor_reduce(out=rs, in_=P, axis=AX.X, op=ALU.add), [bP], [bP])
                sc.op("dve", lambda e: e.reciprocal(out=rr, in_=rs), [bP], [bP])
                sc.op("dve", lambda e: e.tensor_scalar(out=ca_Pn, in0=P, scalar1=rr, scalar2=None, op0=ALU.mult),
                      [bP], [b_caPn])
                ptps = PB[6 + tb % 2][:, 0:128].bitcast(BF16)
                sc.pe([tp(ptps[:, mb * 128:(mb + 1) * 128], ca_Pn[:, mb * 128:(mb + 1) * 128], identb)
                       for mb in range(2)], [b_caPn, bC], [bPB[6 + tb % 2]])
                evac_copy(ca_PT[:, :, tb * 128:(tb + 1) * 128], ptps.rearrange("p (m t) -> p m t", t=128),
                          [bPB[6 + tb % 2]], [b_caPT])
            for cc in range(4):
                c = hd * 4 + cc
                bk = next_bank(0, 4)
                sc.pe([mm(PB[bk][:, :], Vm[:, mb, c * 128:(c + 1) * 128], ca_PT[:, mb, :], mb == 0, mb == 1)
                       for mb in range(2)], [bVm, b_caPT], [bPB[bk]])
                evac_copy(RA[:, c, :], PB[bk][:, :], [bPB[bk]], [bRA[c]])
        tok_proj(wom_d, 0, RA, bRA, 16)
        layer_norm(1, True)
        if debug == "ln2" and g == 0:
            return True
        for qd in range(4):
            for j in range(11):
                fc = qd * 11 + j
                wv, ba, bb = wload(wgu_d[fc], 16, 256)
                bg = next_bank(0, 4)
                sc.pe([mm(PB[bg][:, :], wv[:, kc, 0:128], RY[:, kc, :], kc == 0, kc == 15) for kc in range(16)],
                      reads=[ba] + bRY, writes=[bPB[bg]])
                bu = next_bank(0, 4)
                sc.pe([mm(PB[bu][:, :], wv[:, kc, 128:256], RY[:, kc, :], kc == 0, kc == 15) for kc in range(16)],
                      reads=[bb] + bRY, writes=[bPB[bu]])
                sc.op("act", lambda e: e.activation(out=sg, in_=PB[bg][:, :], func=AF.Silu), [bPB[bg]], [b_sg])
                sc.op("dve", lambda e: e.tensor_tensor(out=RA[:, j, :], in0=sg, in1=PB[bu][:, :], op=ALU.mult),
                      [b_sg, bPB[bu]], [bRA[j]])
            tok_proj(wdn_d, qd * 8, RA, bRA[0:11], 11, first=(qd == 0))
        layer_norm(2, False)
        for tb in range(4):
            sc.dma("sp", out_d[g * 512 + tb * 128:g * 512 + (tb + 1) * 128, :], RH[:, tb, :], [bRH[tb]], [],
                   "o%d" % tb)
        return False

    stop = False
    for g in range(G):
        stop = stage1(g)
        if stop:
            break
        sc.barrier()
        stop = stage2(g)
        if stop:
            break
        sc.barrier()
    if debug is not None:
        sc.barrier()
        dbg_views = {
            "inproj": [QaT, QbT],
            "swa": [RY],
            "sb": [RY],
            "ln1": [RY],
            "ln2": [RY],
        }
        if (debug.startswith("swa") and debug != "swa") or debug.startswith("s2"):
            sc.dma("sp", dbg_d[:, 0:NCF], CF[:, :], [bC], [], "dbg")
        elif debug in ("inproj",):
            sc.dma("sp", dbg_d[:, 0:4096], RH[:, 2:4, :].rearrange("p a b -> p (a b)"), bRH, [], "dbg")
        elif debug in ("swa", "sb", "ln1", "ln2", "full"):
            if debug == "swa":
                sc.dma("sp", dbg_d[:, 0:2048], RY[:, 0:8, :].rearrange("p a b -> p (a b)").bitcast(F32), bRY, [], "dbg")
            else:
                sc.dma("sp", dbg_d[:, 0:4096], RY[:, :, :].rearrange("p a b -> p (a b)").bitcast(F32), bRY, [], "dbg")
            if debug in ("ln1", "ln2"):
                pass
        sc.barrier()
    sc.finish()
    return nc


def _consts(sinks, g_swa, g_sb):
    cf = np.zeros((128, NCF), np.float32)
    cf[:, CF_ID:CF_ID + 128] = np.eye(128, dtype=np.float32)
    bd = np.zeros((128, 128), np.float32)
    bd[:64, :64] = 1.0
    bd[64:, 64:] = 1.0
    cf[:, CF_BD:CF_BD + 128] = bd
    q = np.arange(128)[:, None]
    s = np.arange(128)[None, :]
    prev = np.where(s > q, -(128.0 + q - s), -1e9)
    cur = np.where(s <= q, -(q - s).astype(np.float64), -1e9)
    cf[:, CF_ND:CF_ND + 128] = prev
    cf[:, CF_ND + 128:CF_ND + 256] = cur
    cf[:, CF_ND0:CF_ND0 + 128] = -1e9
    cf[:, CF_ND0 + 128:CF_ND0 + 256] = cur
    gcat = np.concatenate([g_swa.reshape(-1), g_sb.reshape(-1)]).astype(np.float32)
    cf[:, CF_G:CF_G + 16] = gcat.reshape(16, 128).T
    cf[:, CF_SK:CF_SK + 16] = np.broadcast_to(sinks.reshape(1, 16), (128, 16))
    cf[:, CF_M01:CF_M01 + 128] = np.where(q < s, 1.0, 0.0)
    cb = np.zeros((128, NCB), np.float32)
    cb[:, CB_ID:CB_ID + 128] = np.eye(128, dtype=np.float32)
    j = np.arange(128)[:, None]
    sidx = np.arange(128)[None, :]
    cb[:, CB_NT:CB_NT + 128] = np.where(j >= sidx, -1.0, 0.0)
    cb[:, CB_NSL:CB_NSL + 128] = np.where(j < sidx, -1.0, 0.0)
    cb[:, CB_MK:CB_MK + 128] = np.where(j >= sidx, NEG, 0.0)
    return cf, cb


def _tile(W, nk, col0s, r0=0):
    out = np.empty((len(col0s), 128, nk * 256), np.float32)
    for t, c0 in enumerate(col0s):
        out[t] = W[r0:r0 + nk * 128, c0:c0 + 256].reshape(nk, 128, 256).transpose(1, 0, 2).reshape(128, nk * 256)
    return out


def _w_in2(w_in):
    w = w_in
    qa = w[:, 0:1024]
    ka = w[:, 1024:1280]
    va = w[:, 1280:1536]
    qb = w[:, 1536:2560]
    kb = w[:, 2560:3584]
    vb = w[:, 3584:4608]
    kad = np.concatenate([np.concatenate([ka[:, k * 64:(k + 1) * 64]] * 2, axis=1) for k in range(4)], axis=1)
    return np.ascontiguousarray(np.concatenate([qa, kad, qb, kb, va, vb], axis=1))


_NC_CACHE = {}


def _prep_shared(inp):
    f = lambda a: np.ascontiguousarray(np.asarray(a, dtype=np.float32))
    cf, cb = _consts(f(inp["sinks"])[0], f(inp["g_swa"])[0], f(inp["g_sb"])[0])
    lnp = np.stack([f(inp[k])[0] for k in ("ln1_g", "ln1_b", "ln2_g", "ln2_b", "ln3_g", "ln3_b")])
    wgu = f(inp["w_gate_up"])[0]
    wgu2 = np.concatenate([wgu[:, :DFF].reshape(D, NFC, 128), wgu[:, DFF:].reshape(D, NFC, 128)], axis=2)
    wgu2 = wgu2.reshape(D, NFC * 256)
    wdn = f(inp["w_down"])[0]
    c8 = [cb * 256 for cb in range(8)]
    return {
        "w_in2": _tile(_w_in2(f(inp["w_in"])[0]), 16, [t * 256 for t in range(19)]),
        "w_o": _tile(f(inp["w_o"])[0], 16, c8),
        "w_q_mem": _tile(f(inp["w_q_mem"])[0], 16, c8),
        "w_kv_mem": _tile(f(inp["w_kv_mem"])[0], 16, [t * 256 for t in range(16)]),
        "w_o_mem": _tile(f(inp["w_o_mem"])[0], 16, c8),
        "w_gate_up": _tile(wgu2, 16, [t * 256 for t in range(NFC)]),
        "w_down": np.concatenate([_tile(wdn, 11, c8, r0=qd * 11 * 128) for qd in range(4)], axis=0),
        "cf32": cf, "cb16": cb, "lnp": np.ascontiguousarray(lnp),
    }


def kernel(**inputs):
    x = np.asarray(inputs["x"], dtype=np.float32)
    mem = np.asarray(inputs["mem"], dtype=np.float32)
    Bn, S, _ = x.shape
    shared = _prep_shared(inputs)
    if S not in _NC_CACHE:
        _NC_CACHE[S] = build(S)
    nc = _NC_CACHE[S]
    in_maps = []
    for b in range(Bn):
        m = dict(shared)
        m["x"] = np.ascontiguousarray(x[b])
        m["xT"] = np.ascontiguousarray(x[b].T)
        m["memT"] = np.ascontiguousarray(mem[b].T)
        in_maps.append(m)
    res = run_bass_kernel_spmd(nc, in_maps, core_ids=list(range(Bn)))
    return np.stack([r["out"] for r in res.results], axis=0).astype(np.float32)
```

```python
import math
import numpy as np
import concourse.bass as bass
import concourse.mybir as mybir
from concourse.bass_utils import run_bass_kernel_spmd

AF = mybir.ActivationFunctionType
ALU = mybir.AluOpType
AX = mybir.AxisListType
F32 = mybir.dt.float32
BF16 = mybir.dt.bfloat16

D = 2048
DFF = 5632
NFC = DFF // 128
ALPHA = 2.0 ** 0.25
LN_EPS = 1e-5
RMS_EPS = 1e-6
MEM = 256
NEG = -30000.0
SLOPES = [2.0 ** (-8.0 * (h + 1) / 16) for h in range(16)]
QSCALE_MEM = 1.0 / math.sqrt(512.0)

CF_ID, CF_BD, CF_ND, CF_ND0, CF_G, CF_SK, CF_M01, NCF = 0, 128, 256, 512, 768, 784, 800, 928
CB_ID, CB_NT, CB_NSL, CB_MK, NCB = 0, 128, 256, 384, 512

W_IN2 = 4864


class Buf:
    __slots__ = ("name", "w", "r")

    def __init__(self, name):
        self.name = name
        self.w = None
        self.r = {}


class _Eng:
    def __init__(self, name, h, sem):
        self.name, self.h, self.sem, self.cnt, self.waited = name, h, sem, 0, {}


class _Stream:
    def __init__(self, name, sem):
        self.name, self.sem, self.cnt = name, sem, 0


class Sched:
    def __init__(self, nc):
        self.nc = nc
        self.E = {}
        for n, h in (("pe", nc.tensor), ("act", nc.scalar), ("dve", nc.vector)):
            self.E[n] = _Eng(n, h, nc.alloc_semaphore("s_" + n))
        for n, h in (("pool", nc.gpsimd), ("sp", nc.sync)):
            self.E[n] = _Eng(n, h, None)
        self.streams = {}
        self.sems = {}
        for n in ("pe", "act", "dve"):
            self.sems[n] = self.E[n].sem

    def stream(self, name):
        if name not in self.streams:
            st = _Stream(name, self.nc.alloc_semaphore("d_" + name))
            self.streams[name] = st
            self.sems["d_" + name] = st.sem
        return self.streams[name]

    def _deps(self, reads, writes):
        deps = {}

        def add(kv):
            k, v = kv
            if deps.get(k, 0) < v:
                deps[k] = v
        for b in reads:
            if b.w is not None:
                add(b.w)
        for b in writes:
            if b.w is not None:
                add(b.w)
            for kv in b.r.items():
                add(kv)
        return deps

    def _wait(self, eng, deps):
        for k, v in deps.items():
            if k == eng.name and eng.name == "pe":
                continue
            if eng.waited.get(k, 0) < v:
                eng.h.wait_ge(self.sems[k], v)
                eng.waited[k] = v

    def _mark(self, key, val, reads, writes):
        for b in reads:
            if b.r.get(key, 0) < val:
                b.r[key] = val
        for b in writes:
            b.w = (key, val)
            b.r = {}

    def op(self, en, fn, reads=(), writes=()):
        eng = self.E[en]
        self._wait(eng, self._deps(reads, writes))
        ins = fn(eng.h)
        eng.cnt += 1
        ins.then_inc(eng.sem, 1)
        self._mark(en, eng.cnt, reads, writes)

    def pe(self, fns, reads=(), writes=()):
        eng = self.E["pe"]
        self._wait(eng, self._deps(reads, writes))
        ins = None
        for fn in fns:
            ins = fn(eng.h)
        eng.cnt += 1
        ins.then_inc(eng.sem, 1)
        self._mark("pe", eng.cnt, reads, writes)

    def dma(self, qn, out, in_, reads, writes, stream):
        q = self.E[qn]
        st = self.stream(stream)
        key = "d_" + stream
        deps = self._deps(reads, writes)
        if st.cnt > 0:
            deps[key] = max(deps.get(key, 0), st.cnt)
        self._wait(q, deps)
        q.h.dma_start(out=out, in_=in_).then_inc(st.sem, 16)
        st.cnt += 16
        self._mark(key, st.cnt, reads, writes)

    def barrier(self):
        tgt = {n: self.E[n].cnt for n in ("pe", "act", "dve")}
        for st in self.streams.values():
            tgt["d_" + st.name] = st.cnt
        for e in self.E.values():
            for k, v in tgt.items():
                if k == e.name or v == 0:
                    continue
                if e.waited.get(k, 0) < v:
                    e.h.wait_ge(self.sems[k], v)
                    e.waited[k] = v

    def finish(self):
        sp = self.E["sp"]
        for st in self.streams.values():
            if st.cnt and sp.waited.get("d_" + st.name, 0) < st.cnt:
                sp.h.wait_ge(st.sem, st.cnt)
                sp.waited["d_" + st.name] = st.cnt


def build(S=2048, debug=None):
    G = S // 512
    NB = S // 128
    nc = bass.Bass("TRN2", target_bir_lowering=False)
    sc = Sched(nc)

    def dram(name, shape, kind="ExternalInput"):
        return nc.dram_tensor(name, shape, F32, kind=kind).ap()

    xT_d = dram("xT", [D, S])
    x_d = dram("x", [S, D])
    memT_d = dram("memT", [D, MEM])
    win_d = dram("w_in2", [19, 128, 4096])
    wo_d = dram("w_o", [8, 128, 4096])
    wq_d = dram("w_q_mem", [8, 128, 4096])
    wkv_d = dram("w_kv_mem", [16, 128, 4096])
    wom_d = dram("w_o_mem", [8, 128, 4096])
    wgu_d = dram("w_gate_up", [NFC, 128, 4096])
    wdn_d = dram("w_down", [32, 128, 11 * 256])
    cf_d = dram("cf32", [128, NCF])
    cb_d = dram("cb16", [128, NCB])
    lnp_d = dram("lnp", [6, D])
    out_d = dram("out", [S, D], kind="ExternalOutput")
    dbg_d = None
    if debug is not None:
        dbg_d = dram("dbg", [128, 16 * 512], kind="ExternalOutput")

    sb = nc.alloc_sbuf_tensor
    KbT = sb("KbT", [128, 8, S], BF16)
    Vb = sb("Vb", [128, NB, 1024], BF16)
    KaT = sb("KaT", [128, 4, 640], BF16)
    Va = sb("Va", [128, 5, 256], BF16)
    KmT = sb("KmT", [128, 16, MEM], BF16)
    Vm = sb("Vm", [128, 2, D], BF16)
    CF = sb("CF", [128, NCF], F32)
    CB = sb("CB", [128, NCB], BF16)
    RH = sb("RH", [128, 4, D], F32)
    RHb = RH[:, :, :].rearrange("p a b -> p (a b)").bitcast(BF16)
    xT = RHb[:, 0:8192].rearrange("p (c t) -> p c t", t=512)
    QaT = RHb[:, 8192:12288].rearrange("p (c t) -> p c t", t=512)
    QbT = RHb[:, 12288:16384].rearrange("p (c t) -> p c t", t=512)
    memT = RHb[:, 0:4096].rearrange("p (c t) -> p c t", t=MEM)
    RY = sb("RY", [128, 16, 512], BF16)
    WS = [sb("WS%d" % i, [128, 4096], BF16) for i in range(3)]
    RU = sb("RU", [128, 15360], BF16)
    RUf = RU[:, :].bitcast(F32)
    sb_e = [RUf[:, i * 512:(i + 1) * 512] for i in range(3)]
    sb_xc = [RUf[:, 1536 + i * 512:1536 + (i + 1) * 512] for i in range(2)]
    sb_sp = [RU[:, 5120 + i * 512:5120 + (i + 1) * 512] for i in range(2)]
    sb_sp.append(RU[:, 14464:14976])
    sb_at = [RU[:, 6144 + i * 512:6144 + (i + 1) * 512] for i in range(2)]
    rms_sq = RUf[:, 3584:4096]
    rms_l = RUf[:, 4096:4608]
    rms_r = RUf[:, 4608:5120]
    swa_S = RUf[:, 5120:6144].rearrange("p (h k) -> p h k", k=256)
    swa_Pn = RU[:, 12288:13312].rearrange("p (h k) -> p h k", k=256)
    swa_PT = RU[:, 13312:14336]
    swa_st = RUf[:, 7168:7232]
    RA = RU[:, 0:8192].rearrange("p (c t) -> p c t", t=512)
    GBg = RUf[:, 0:2048]
    GBb = RUf[:, 2048:4096]
    qmT = [RU[:, 8192 + i * 2048:8192 + (i + 1) * 2048].rearrange("p (c t) -> p c t", t=512)
           for i in range(2)]
    ca_P = [RUf[:, 6144 + i * 256:6144 + (i + 1) * 256] for i in range(2)]
    ca_Pn = RU[:, 13312:13568]
    ca_PT = RU[:, 13568:14592].rearrange("p (m t) -> p m t", t=512)
    sg = RUf[:, 7296:7552].bitcast(BF16)
    ln_st = sb("ln_st", [128, 64], F32)
    swa_S2 = sb("swa_S2", [128, 4, 256], F32)
    swa_Pn2 = sb("swa_Pn2", [128, 4, 256], BF16)
    swa_PT2 = sb("swa_PT2", [128, 1024], BF16)
    swa_st2 = sb("swa_st2", [128, 64], F32)
    ln_st2 = sb("ln_st2", [128, 64], F32)
    ca_Pn2 = sb("ca_Pn2", [128, 256], BF16)
    PB = [nc.alloc_psum_tensor("PB%d" % i, [128, 512], F32) for i in range(8)]
    bPB = [Buf("PB%d" % i) for i in range(8)]

    bW = [(Buf("W%da" % i), Buf("W%db" % i)) for i in range(3)]
    bRH = [Buf("h%d" % i) for i in range(4)]
    bRY = [Buf("RY%d" % i) for i in range(16)]
    bRA = [Buf("RA%d" % i) for i in range(16)]
    bKbT = {}
    bVb = {}
    bKa = [[Buf("Ka%d_%d" % (k, s)) for s in range(5)] for k in range(4)]
    bVa = [Buf("Va%d" % s) for s in range(5)]
    bKm = Buf("KmT")
    bVm = Buf("Vm")
    bC = Buf("consts")
    bXT = bRH[0:2]
    bQa = bRH[2]
    bQb = bRH[3]

    def B(d, key):
        if key not in d:
            d[key] = Buf(str(key))
        return d[key]

    wslot = [0]

    def wload(src_ap, nk, ncols):
        i = wslot[0] % 3
        wslot[0] += 1
        view = WS[i][:, 0:nk * ncols].rearrange("p (c n) -> p c n", n=ncols)
        ba, bb = bW[i]
        sc.dma("pool", WS[i][:, 0:nk * ncols], src_ap, [], [ba, bb], "w%d" % i)
        return view, ba, bb

    evac_rr = [0]

    def evac_copy(out, in_, reads, writes, scale=None):
        evac_rr[0] += 1
        if evac_rr[0] % 2 == 0:
            if scale is None:
                sc.op("act", lambda e: e.activation(out=out, in_=in_, func=AF.Copy), reads, writes)
            else:
                sc.op("act", lambda e: e.activation(out=out, in_=in_, func=AF.Copy, scale=scale), reads, writes)
        else:
            if scale is None:
                sc.op("dve", lambda e: e.tensor_copy(out=out, in_=in_), reads, writes)
            else:
                sc.op("dve", lambda e: e.tensor_scalar(out=out, in0=in_, scalar1=scale, scalar2=None,
                                                        op0=ALU.mult), reads, writes)

    def mm(out, lhsT, rhs, start, stop):
        return lambda e: e.matmul(out, lhsT=lhsT, rhs=rhs, start=start, stop=stop)

    def mmx(out, lhsT, rhs, start, stop):
        return lambda e: e.matmul(out, lhsT=lhsT, rhs=rhs, start=start, stop=stop, skip_group_check=True)

    def tp(out, in_, ident):
        return lambda e: e.transpose(out, in_, ident)

    def dbg_dump(ap_list):
        pass

    sc.dma("sp", CF[:, :], cf_d, [], [bC], "cf")
    sc.dma("pool", CB[:, :], cb_d, [], [bC], "cb")
    ident32 = CF[:, CF_ID:CF_ID + 128]
    BDm = CF[:, CF_BD:CF_BD + 128]
    negdist = CF[:, CF_ND:CF_ND + 256]
    negdist0 = CF[:, CF_ND0:CF_ND0 + 256]
    gcol = CF[:, CF_G:CF_G + 16]
    sinks = CF[:, CF_SK:CF_SK + 16]
    mask01 = CF[:, CF_M01:CF_M01 + 128]
    identb = CB[:, CB_ID:CB_ID + 128]
    NegTri = CB[:, CB_NT:CB_NT + 128]
    NegSL = CB[:, CB_NSL:CB_NSL + 128]
    MaskT = CB[:, CB_MK:CB_MK + 128]
    sc.op("dve", lambda e: e.memset(KaT[:, :, 0:128], 0.0), [], [bKa[k][0] for k in range(4)])
    sc.op("dve", lambda e: e.memset(Va[:, 0, :], 0.0), [], [bVa[0]])

    sc.dma("pool", memT, memT_d.rearrange("(c p) m -> p c m", p=128), [], bXT, "xT")
    pbi = [0]

    def next_bank(lo=0, hi=8):
        pbi[0] += 1
        return lo + pbi[0] % (hi - lo)

    for wt in range(8):
        wv, ba, bb = wload(wkv_d[wt], 16, 256)
        for half in range(2):
            c = 2 * wt + half
            bk = next_bank(0, 4)
            sc.pe([mm(PB[bk][:, 0:MEM], wv[:, kc, half * 128:(half + 1) * 128], memT[:, kc, :],
                      kc == 0, kc == 15) for kc in range(16)],
                  reads=[ba, bb, bC] + bXT, writes=[bPB[bk]])
            evac_copy(KmT[:, c, :], PB[bk][:, 0:MEM], [bPB[bk]], [bKm])
    for ct in range(8):
        wv, ba, bb = wload(wkv_d[8 + ct], 16, 256)
        bk = next_bank(0, 4)
        for mb in range(2):
            sc.pe([mm(PB[bk][:, mb * 256:(mb + 1) * 256], memT[:, kc, mb * 128:(mb + 1) * 128], wv[:, kc, :],
                      kc == 0, kc == 15) for kc in range(16)],
                  reads=[ba, bb] + bXT, writes=[bPB[bk]])
        evac_copy(Vm[:, :, ct * 256:(ct + 1) * 256],
                  PB[bk][:, :].rearrange("p (m n) -> p m n", n=256), [bPB[bk]], [bVm])
    sc.barrier()

    def rmsnorm_pair(obank, c):
        ob = PB[obank]
        sc.op("act", lambda e: e.activation(out=rms_sq, in_=ob[:, :], func=AF.Square),
              [bPB[obank]], [b_rms_sq])
        ssb = 5 if obank != 5 else 7
        sc.pe([mm(PB[ssb][:, :], BDm, rms_sq, True, True)], [b_rms_sq, bC], [bPB[ssb]])
        sc.op("act", lambda e: e.activation(out=rms_l, in_=PB[ssb][:, :], func=AF.Ln, scale=1.0 / 64.0,
                                            bias=RMS_EPS), [bPB[ssb]], [b_rms_l])
        sc.op("act", lambda e: e.activation(out=rms_r, in_=rms_l, func=AF.Exp, scale=-0.5),
              [b_rms_l], [b_rms_r])
        sc.op("dve", lambda e: e.scalar_tensor_tensor(out=RY[:, c, :], in0=ob[:, :], scalar=gcol[:, c:c + 1],
                                                      in1=rms_r, op0=ALU.mult, op1=ALU.mult),
              [bPB[obank], b_rms_r, bC], [bRY[c]])

    b_rms_sq, b_rms_l, b_rms_r = Buf("rms_sq"), Buf("rms_l"), Buf("rms_r")
    b_e = [Buf("e%d" % i) for i in range(3)]
    b_xc = [Buf("xc%d" % i) for i in range(2)]
    b_sp = [Buf("sp%d" % i) for i in range(3)]
    b_at = [Buf("at%d" % i) for i in range(2)]
    b_swaS, b_swaPn, b_swaPT, b_swast = Buf("swaS"), Buf("swaPn"), Buf("swaPT"), Buf("swast")
    b_swaS2, b_swaPn2, b_swaPT2, b_swast2 = Buf("swaS2"), Buf("swaPn2"), Buf("swaPT2"), Buf("swast2")
    b_qm = [Buf("qm0"), Buf("qm1")]
    b_caP = [Buf("caP0"), Buf("caP1")]
    b_caPn, b_caPT, b_sg, b_lnst = Buf("caPn"), Buf("caPT"), Buf("sg"), Buf("lnst")
    b_gb = Buf("gb_dummy")
    b_lnst2 = Buf("lnst2")
    b_caPn2 = Buf("caPn2")

    def stage1(g):
        sc.dma("pool", xT, xT_d[:, g * 512:(g + 1) * 512].rearrange("(c p) t -> p c t", p=128),
               [], bXT, "xT")
        for wt in range(14):
            wv, ba, bb = wload(win_d[wt], 16, 256)
            for half in range(2):
                c = 2 * wt + half
                bk = next_bank(0, 4)
                sc.pe([mm(PB[bk][:, :], wv[:, kc, half * 128:(half + 1) * 128], xT[:, kc, :], kc == 0, kc == 15)
                       for kc in range(16)], reads=[ba, bb] + bXT, writes=[bPB[bk]])
                if c < 8:
                    evac_copy(QaT[:, c, :], PB[bk][:, :], [bPB[bk]], [bQa], scale=0.125)
                elif c < 12:
                    k = c - 8
                    evac_copy(KaT[:, k, 128:640], PB[bk][:, :], [bPB[bk]], [bKa[k][s] for s in range(1, 5)])
                elif c < 20:
                    evac_copy(QbT[:, c - 12, :], PB[bk][:, :], [bPB[bk]], [bQb], scale=0.125)
                else:
                    evac_copy(KbT[:, c - 20, g * 512:(g + 1) * 512], PB[bk][:, :], [bPB[bk]],
                              [B(bKbT, (c - 20, g))])
        for vt in range(5):
            wv, ba, bb = wload(win_d[14 + vt], 16, 256)
            for tp2 in range(2):
                bk = next_bank(0, 4)
                for t2 in range(2):
                    tb = 2 * tp2 + t2
                    sc.pe([mm(PB[bk][:, t2 * 256:(t2 + 1) * 256], xT[:, kc, tb * 128:(tb + 1) * 128], wv[:, kc, :],
                              kc == 0, kc == 15) for kc in range(16)],
                          reads=[ba, bb] + bXT, writes=[bPB[bk]])
                src = PB[bk][:, :].rearrange("p (t n) -> p t n", n=256)
                if vt == 0:
                    evac_copy(Va[:, 1 + 2 * tp2:3 + 2 * tp2, :], src, [bPB[bk]], [bVa[1 + 2 * tp2], bVa[2 + 2 * tp2]])
                else:
                    evac_copy(Vb[:, g * 4 + 2 * tp2:g * 4 + 2 * tp2 + 2, (vt - 1) * 256:vt * 256], src, [bPB[bk]],
                              [B(bVb, (g * 4 + 2 * tp2, vt - 1)), B(bVb, (g * 4 + 2 * tp2 + 1, vt - 1))])
        if debug == "inproj" and g == 0:
            return True
        def swa_unit(kg, qb, R):
            S_, Pn_, PT_, st_ = R["S"], R["Pn"], R["PT"], R["st"]
            bS, bPn, bPT, bst = R["bS"], R["bPn"], R["bPT"], R["bst"]
            Sb, ptb, ob = R["Sb"], R["ptb"], R["ob"]
            gb = 4 * g + qb
            nd = negdist0 if gb == 0 else negdist
            fns = []
            for hh in range(4):
                h = 4 * kg + hh
                base = (h % 2) * 64
                fns.append(mm(PB[Sb[hh % 2]][:, (hh // 2) * 256:(hh // 2 + 1) * 256],
                              QaT[base:base + 64, h // 2, qb * 128:(qb + 1) * 128],
                              KaT[base:base + 64, kg, qb * 128:qb * 128 + 256], True, True))
            sc.pe(fns, reads=[bQa, bKa[kg][qb], bKa[kg][qb + 1]], writes=[bPB[Sb[0]], bPB[Sb[1]]])
            yield
            for hh in range(4):
                h = 4 * kg + hh
                sc.op("dve", lambda e, hh=hh, h=h: e.scalar_tensor_tensor(
                    out=S_[:, hh, :], in0=nd, scalar=SLOPES[h],
                    in1=PB[Sb[hh % 2]][:, (hh // 2) * 256:(hh // 2 + 1) * 256], op0=ALU.mult, op1=ALU.add),
                    [bPB[Sb[hh % 2]], bC], [bS])
                yield
            rowmax = st_[:, 0:4]
            negm = st_[:, 4:8]
            dsk = st_[:, 8:12]
            rowsum = st_[:, 12:16]
            es = st_[:, 16:20]
            rden = st_[:, 20:24]
            sk = sinks[:, 4 * kg:4 * kg + 4]
            sc.op("dve", lambda e: e.tensor_reduce(out=rowmax, in_=S_, axis=AX.X, op=ALU.max), [bS], [bst])
            yield
            sc.op("dve", lambda e: e.tensor_tensor(out=rowmax, in0=rowmax, in1=sk, op=ALU.max), [bst, bC], [bst])
            yield
            sc.op("dve", lambda e: e.tensor_scalar(out=negm, in0=rowmax, scalar1=-1.0, scalar2=None,
                                                   op0=ALU.mult), [bst], [bst])
            yield
            sc.op("dve", lambda e: e.tensor_tensor(out=dsk, in0=sk, in1=rowmax, op=ALU.subtract),
                  [bst, bC], [bst])
            yield
            for hh in range(4):
                sc.op("act", lambda e, hh=hh: e.activation(out=S_[:, hh, :], in_=S_[:, hh, :], func=AF.Exp,
                                                           bias=negm[:, hh:hh + 1]), [bS, bst], [bS])
                yield
            sc.op("dve", lambda e: e.tensor_reduce(out=rowsum, in_=S_, axis=AX.X, op=ALU.add), [bS], [bst])
            yield
            sc.op("act", lambda e: e.activation(out=es, in_=dsk, func=AF.Exp), [bst], [bst])
            yield
            sc.op("dve", lambda e: e.tensor_tensor(out=rden, in0=rowsum, in1=es, op=ALU.add), [bst], [bst])
            yield
            sc.op("dve", lambda e: e.reciprocal(out=rden, in_=rden), [bst], [bst])
            yield
            for hh in range(4):
                sc.op("dve", lambda e, hh=hh: e.tensor_scalar(out=Pn_[:, hh, :], in0=S_[:, hh, :],
                                                              scalar1=rden[:, hh:hh + 1], scalar2=None,
                                                              op0=ALU.mult), [bS, bst], [bPn])
                yield
            ptps = PB[ptb][:, :].bitcast(BF16)
            sc.pe([tp(ptps[:, (hh * 2 + kb) * 128:(hh * 2 + kb + 1) * 128],
                      Pn_[:, hh, kb * 128:(kb + 1) * 128], identb)
                   for hh in range(4) for kb in range(2)], [bPn, bC], [bPB[ptb]])
            yield
            evac_copy(PT_, ptps, [bPB[ptb]], [bPT])
            yield
            for hh in range(4):
                h = 4 * kg + hh
                base = (h % 2) * 64
                obk = ob[hh // 2]
                sc.pe([mm(PB[obk][base:base + 64, qb * 128:(qb + 1) * 128],
                          Va[:, qb + kb, kg * 64:(kg + 1) * 64],
                          PT_[:, (hh * 2 + kb) * 128:(hh * 2 + kb + 1) * 128], kb == 0, kb == 1)
                       for kb in range(2)],
                      [bPT, bVa[qb], bVa[qb + 1]], [bPB[obk]])
                yield

        for kg in range(4):
            ob = [4, 6]
            RS = [dict(S=swa_S, Pn=swa_Pn, PT=swa_PT, st=swa_st, bS=b_swaS, bPn=b_swaPn, bPT=b_swaPT,
                       bst=b_swast, Sb=[0, 1], ptb=2, ob=ob),
                  dict(S=swa_S2[:, :, :], Pn=swa_Pn2[:, :, :], PT=swa_PT2[:, :], st=swa_st2[:, :], bS=b_swaS2, bPn=b_swaPn2, bPT=b_swaPT2,
                       bst=b_swast2, Sb=[3, 5], ptb=7, ob=ob)]
            for qb0 in (0, 2):
                gens = [swa_unit(kg, qb0, RS[0]), swa_unit(kg, qb0 + 1, RS[1])]
                while gens:
                    for gen in list(gens):
                        try:
                            next(gen)
                        except StopIteration:
                            gens.remove(gen)
            rmsnorm_pair(ob[0], 2 * kg)
            rmsnorm_pair(ob[1], 2 * kg + 1)
        if g + 1 < G:
            for k in range(4):
                sc.op("dve", lambda e, k=k: e.tensor_copy(out=KaT[:, k, 0:128], in_=KaT[:, k, 512:640]),
                      [bKa[k][4]], [bKa[k][0]])
            sc.op("dve", lambda e: e.tensor_copy(out=Va[:, 0, :], in_=Va[:, 4, :]), [bVa[4]], [bVa[0]])
        if debug == "swa" and g == G - 1:
            return True
        for p in range(8):
            obk = 4 if p % 2 == 0 else 6
            tiles = []
            for kb in range(4 * g + 3, -1, -1):
                for hd in range(2):
                    tiles.append((hd, kb))
            n = len(tiles)
            Cb = [2, 3]

            def geom(t):
                hd, kb = tiles[t]
                j = kb - 4 * g
                c0 = max(j, 0) * 128
                return hd, kb, j, c0, hd * 64

            def stA(t):
                hd, kb, j, c0, base = geom(t)
                zb = t % 2
                fns = [mm(PB[zb][:, c0:512], KbT[base:base + 64, p, kb * 128:(kb + 1) * 128],
                          QbT[base:base + 64, p, c0:512], True, True)]
                sc.pe(fns, [B(bKbT, (p, kb // 4)), bQb, bC], [bPB[zb]])

            def stB(t):
                hd, kb, j, c0, base = geom(t)
                zb = t % 2
                sc.op("act", lambda e: e.activation(out=sb_e[t % 3][:, c0:512], in_=PB[zb][:, c0:512], func=AF.Exp),
                      [bPB[zb]], [b_e[t % 3]])
                if j >= 0:
                    sc.op("dve", lambda e: e.tensor_tensor(out=sb_e[t % 3][:, c0:c0 + 128], in0=sb_e[t % 3][:, c0:c0 + 128],
                                                           in1=mask01, op=ALU.mult), [b_e[t % 3], bC], [b_e[t % 3]])
                sc.op("act", lambda e: e.activation(out=sb_sp[t % 3][:, c0:512], in_=sb_e[t % 3][:, c0:512],
                                                    func=AF.Ln, bias=1.0), [b_e[t % 3]], [b_sp[t % 3]])

            def stC(t):
                hd, kb, j, c0, base = geom(t)
                first = (kb == 4 * g + 3)
                sc.pe([mmx(PB[Cb[hd]][:, c0:512], NegTri, sb_sp[t % 3][:, c0:512], first, True)],
                      [b_sp[t % 3], bC], [bPB[Cb[hd]]])

            def stD(t):
                hd, kb, j, c0, base = geom(t)
                sc.op("act", lambda e: e.activation(out=sb_xc[t % 2][:, c0:512], in_=PB[Cb[hd]][:, c0:512],
                                                    func=AF.Exp), [bPB[Cb[hd]]], [b_xc[t % 2]])

            def stE(t):
                hd, kb, j, c0, base = geom(t)
                last = (kb == 0)
                sc.pe([mmx(PB[Cb[hd]][:, c0:512], NegSL, sb_sp[t % 3][:, c0:512], False, True)],
                      [b_sp[t % 3], bC], [bPB[Cb[hd]]])

            def stE2(t):
                hd, kb, j, c0, base = geom(t)
                sc.op("dve", lambda e: e.tensor_tensor(out=sb_at[t % 2][:, c0:512], in0=sb_e[t % 3][:, c0:512],
                                                       in1=sb_xc[t % 2][:, c0:512], op=ALU.mult),
                      [b_e[t % 3], b_xc[t % 2]], [b_at[t % 2]])

            def stF(t):
                hd, kb, j, c0, base = geom(t)
                first = (kb == 4 * g + 3)
                last = (kb == 0)
                sc.pe([mmx(PB[obk][base:base + 64, c0:512], Vb[:, kb, (2 * p + hd) * 64:(2 * p + hd + 1) * 64],
                          sb_at[t % 2][:, c0:512], first, last)],
                      [b_at[t % 2], B(bVb, (kb, (2 * p + hd) // 4))], [bPB[obk]])

            stA(0)
            for s in range(n + 2):
                if 0 <= s - 1 < n:
                    stC(s - 1)
                if s + 1 < n:
                    stA(s + 1)
                if 0 <= s - 2 < n:
                    stE(s - 2)
                    stE2(s - 2)
                    stF(s - 2)
                if s < n:
                    stB(s)
                if 0 <= s - 1 < n:
                    stD(s - 1)
            rmsnorm_pair(obk, 8 + p)
        if debug == "sb" and g == G - 1:
            return True
        return False

    def ln_load(i):
        sc.dma("pool", GBg, lnp_d[2 * i, :].partition_broadcast(128), [], bRA[0:8], "gbg")
        sc.dma("pool", GBb, lnp_d[2 * i + 1, :].partition_broadcast(128), [], bRA[8:16], "gbb")

    def ln_unit(tb, transposes, lst, blst, banks):
        h = RH[:, tb, :]
        st = lst[:, 0:24].rearrange("p (j s) -> p j s", s=6)
        mv = lst[:, 24:26]
        l = lst[:, 26:27]
        rstd = lst[:, 27:28]
        nmr = lst[:, 28:29]
        for j in range(4):
            sc.op("dve", lambda e, j=j: e.bn_stats(out=st[:, j, :], in_=RH[:, tb, j * 512:(j + 1) * 512]),
                  [bRH[tb]], [blst])
            yield
        sc.op("dve", lambda e: e.bn_aggr(out=mv, in_=lst[:, 0:24]), [blst], [blst])
        yield
        sc.op("act", lambda e: e.activation(out=l, in_=mv[:, 1:2], func=AF.Ln, bias=LN_EPS), [blst], [blst])
        yield
        sc.op("act", lambda e: e.activation(out=rstd, in_=l, func=AF.Exp, scale=-0.5), [blst], [blst])
        yield
        sc.op("dve", lambda e: e.scalar_tensor_tensor(out=nmr, in0=mv[:, 0:1], scalar=-1.0, in1=rstd,
                                                      op0=ALU.mult, op1=ALU.mult), [blst], [blst])
        yield
        sc.op("act", lambda e: e.activation(out=h, in_=h, func=AF.Identity, scale=rstd, bias=nmr),
              [bRH[tb], blst], [bRH[tb]])
        yield
        sc.op("dve", lambda e: e.tensor_tensor(out=h, in0=h, in1=GBg, op=ALU.mult),
              [bRH[tb]] + bRA[0:8], [bRH[tb]])
        yield
        sc.op("dve", lambda e: e.tensor_tensor(out=h, in0=h, in1=GBb, op=ALU.add),
              [bRH[tb]] + bRA[8:16], [bRH[tb]])
        yield
        if transposes:
            for q4 in range(4):
                bk = banks[q4 % 2]
                sc.pe([tp(PB[bk][:, i4 * 128:(i4 + 1) * 128],
                          RH[:, tb, (4 * q4 + i4) * 128:(4 * q4 + i4 + 1) * 128], ident32) for i4 in range(4)],
                      [bRH[tb], bC], [bPB[bk]])
                yield
                evac_copy(RY[:, 4 * q4:4 * q4 + 4, tb * 128:(tb + 1) * 128],
                          PB[bk][:, :].rearrange("p (c t) -> p c t", t=128), [bPB[bk]],
                          bRY[4 * q4:4 * q4 + 4])
                yield

    def layer_norm(i, transposes):
        ln_load(i)
        for tb0 in (0, 2):
            gens = [ln_unit(tb0, transposes, ln_st[:, :], b_lnst, (6, 7)),
                    ln_unit(tb0 + 1, transposes, ln_st2[:, :], b_lnst2, (4, 5))]
            while gens:
                for gen in list(gens):
                    try:
                        next(gen)
                    except StopIteration:
                        gens.remove(gen)

    def tok_proj(w_ap, t0, src, bsrc, nk, first=True):
        for cb in range(8):
            wv, ba, bb = wload(w_ap[t0 + cb], nk, 256)
            for tp2 in range(2):
                bk = next_bank(0, 4)
                for t2 in range(2):
                    tb = 2 * tp2 + t2
                    sc.pe([mm(PB[bk][:, t2 * 256:(t2 + 1) * 256], src[:, kc, tb * 128:(tb + 1) * 128], wv[:, kc, :],
                              kc == 0, kc == nk - 1) for kc in range(nk)],
                          reads=[ba, bb] + bsrc, writes=[bPB[bk]])
                hv = RH[:, 2 * tp2:2 * tp2 + 2, cb * 256:(cb + 1) * 256]
                pv = PB[bk][:, :].rearrange("p (t n) -> p t n", n=256)
                if first:
                    sc.op("dve", lambda e, hv=hv, pv=pv: e.scalar_tensor_tensor(
                        out=hv, in0=hv, scalar=ALPHA, in1=pv, op0=ALU.mult, op1=ALU.add),
                        [bPB[bk], bRH[2 * tp2], bRH[2 * tp2 + 1]], [bRH[2 * tp2], bRH[2 * tp2 + 1]])
                else:
                    sc.op("dve", lambda e, hv=hv, pv=pv: e.tensor_tensor(out=hv, in0=hv, in1=pv, op=ALU.add),
                          [bPB[bk], bRH[2 * tp2], bRH[2 * tp2 + 1]], [bRH[2 * tp2], bRH[2 * tp2 + 1]])

    def stage2(g):
        for tb in range(4):
            sc.dma("pool", RH[:, tb, :], x_d[g * 512 + tb * 128:g * 512 + (tb + 1) * 128, :], [], [bRH[tb]],
                   "h%d" % tb)
        if debug == "s2x":
            return True
        tok_proj(wo_d, 0, RY, bRY, 16)
        if debug == "s2a":
            return True
        if layer_norm(0, True):
            return True
        if debug == "ln1" and g == 0:
            return True
        for hd in range(4):
            q = qmT[hd % 2]
            bq = b_qm[hd % 2]
            for wt in range(2):
                wv, ba, bb = wload(wq_d[hd * 2 + wt], 16, 256)
                for half in range(2):
                    cc = 2 * wt + half
                    bk = next_bank(0, 4)
                    sc.pe([mm(PB[bk][:, :], wv[:, kc, half * 128:(half + 1) * 128], RY[:, kc, :], kc == 0, kc == 15)
                           for kc in range(16)], reads=[ba, bb] + bRY, writes=[bPB[bk]])
                    evac_copy(q[:, cc, :], PB[bk][:, :], [bPB[bk]], [bq])
            def ca_unit(tb):
                par = tb % 2
                sbk = 4 + par
                P = ca_P[par]
                bP = b_caP[par]
                Pn = ca_Pn if par == 0 else ca_Pn2[:, :]
                bPn = b_caPn if par == 0 else b_caPn2
                sc.pe([mm(PB[sbk][:, 0:MEM], q[:, cc, tb * 128:(tb + 1) * 128], KmT[:, hd * 4 + cc, :], cc == 0, cc == 3)
                       for cc in range(4)], [bq, bKm], [bPB[sbk]])
                yield
                rmax = ln_st[:, 32 + 4 * par:33 + 4 * par]
                nm = ln_st[:, 33 + 4 * par:34 + 4 * par]
                rs = ln_st[:, 34 + 4 * par:35 + 4 * par]
                rr = ln_st[:, 35 + 4 * par:36 + 4 * par]
                bst = b_caP[par]
                sc.op("dve", lambda e: e.tensor_reduce(out=rmax, in_=PB[sbk][:, 0:MEM], axis=AX.X, op=ALU.max),
                      [bPB[sbk]], [bst])
                yield
                sc.op("dve", lambda e: e.tensor_scalar(out=nm, in0=rmax, scalar1=-QSCALE_MEM, scalar2=None,
                                                       op0=ALU.mult), [bst], [bst])
                yield
                sc.op("act", lambda e: e.activation(out=P, in_=PB[sbk][:, 0:MEM], func=AF.Exp, scale=QSCALE_MEM,
                                                    bias=nm), [bPB[sbk], bst], [bP])
                yield
                sc.op("dve", lambda e: e.tensor_reduce(out=rs, in_=P, axis=AX.X, op=ALU.add), [bP], [bP])
                yield
                sc.op("dve", lambda e: e.reciprocal(out=rr, in_=rs), [bP], [bP])
                yield
                sc.op("dve", lambda e: e.tensor_scalar(out=Pn, in0=P, scalar1=rr, scalar2=None, op0=ALU.mult),
                      [bP], [bPn])
                yield
                ptps = PB[6 + par][:, 0:128].bitcast(BF16)
                sc.pe([tp(ptps[:, mb * 128:(mb + 1) * 128], Pn[:, mb * 128:(mb + 1) * 128], identb)
                       for mb in range(2)], [bPn, bC], [bPB[6 + par]])
                yield
                evac_copy(ca_PT[:, :, tb * 128:(tb + 1) * 128], ptps.rearrange("p (m t) -> p m t", t=128),
                          [bPB[6 + par]], [b_caPT])
                yield

            for tb0 in (0, 2):
                gens = [ca_unit(tb0), ca_unit(tb0 + 1)]
                while gens:
                    for gen in list(gens):
                        try:
                            next(gen)
                        except StopIteration:
                            gens.remove(gen)
            for cc in range(4):
                c = hd * 4 + cc
                bk = next_bank(0, 4)
                sc.pe([mm(PB[bk][:, :], Vm[:, mb, c * 128:(c + 1) * 128], ca_PT[:, mb, :], mb == 0, mb == 1)
                       for mb in range(2)], [bVm, b_caPT], [bPB[bk]])
                evac_copy(RA[:, c, :], PB[bk][:, :], [bPB[bk]], [bRA[c]])
        tok_proj(wom_d, 0, RA, bRA, 16)
        layer_norm(1, True)
        if debug == "ln2" and g == 0:
            return True
        for qd in range(4):
            for j in range(11):
                fc = qd * 11 + j
                wv, ba, bb = wload(wgu_d[fc], 16, 256)
                bg = next_bank(0, 4)
                sc.pe([mm(PB[bg][:, :], wv[:, kc, 0:128], RY[:, kc, :], kc == 0, kc == 15) for kc in range(16)],
                      reads=[ba] + bRY, writes=[bPB[bg]])
                bu = next_bank(0, 4)
                sc.pe([mm(PB[bu][:, :], wv[:, kc, 128:256], RY[:, kc, :], kc == 0, kc == 15) for kc in range(16)],
                      reads=[bb] + bRY, writes=[bPB[bu]])
                sc.op("act", lambda e: e.activation(out=sg, in_=PB[bg][:, :], func=AF.Silu), [bPB[bg]], [b_sg])
                sc.op("dve", lambda e: e.tensor_tensor(out=RA[:, j, :], in0=sg, in1=PB[bu][:, :], op=ALU.mult),
                      [b_sg, bPB[bu]], [bRA[j]])
            tok_proj(wdn_d, qd * 8, RA, bRA[0:11], 11, first=(qd == 0))
        layer_norm(2, False)
        for tb in range(4):
            sc.dma("sp", out_d[g * 512 + tb * 128:g * 512 + (tb + 1) * 128, :], RH[:, tb, :], [bRH[tb]], [],
                   "o%d" % tb)
        return False

    stop = False
    for g in range(G):
        stop = stage1(g)
        if stop:
            break
        sc.barrier()
        stop = stage2(g)
        if stop:
            break
        sc.barrier()
    if debug is not None:
        sc.barrier()
        dbg_views = {
            "inproj": [QaT, QbT],
            "swa": [RY],
            "sb": [RY],
            "ln1": [RY],
            "ln2": [RY],
        }
        if (debug.startswith("swa") and debug != "swa") or debug.startswith("s2"):
            sc.dma("sp", dbg_d[:, 0:NCF], CF[:, :], [bC], [], "dbg")
        elif debug in ("inproj",):
            sc.dma("sp", dbg_d[:, 0:4096], RH[:, 2:4, :].rearrange("p a b -> p (a b)"), bRH, [], "dbg")
        elif debug in ("swa", "sb", "ln1", "ln2", "full"):
            if debug == "swa":
                sc.dma("sp", dbg_d[:, 0:2048], RY[:, 0:8, :].rearrange("p a b -> p (a b)").bitcast(F32), bRY, [], "dbg")
            else:
                sc.dma("sp", dbg_d[:, 0:4096], RY[:, :, :].rearrange("p a b -> p (a b)").bitcast(F32), bRY, [], "dbg")
            if debug in ("ln1", "ln2"):
                pass
        sc.barrier()
    sc.finish()
    return nc


def _consts(sinks, g_swa, g_sb):
    cf = np.zeros((128, NCF), np.float32)
    cf[:, CF_ID:CF_ID + 128] = np.eye(128, dtype=np.float32)
    bd = np.zeros((128, 128), np.float32)
    bd[:64, :64] = 1.0
    bd[64:, 64:] = 1.0
    cf[:, CF_BD:CF_BD + 128] = bd
    q = np.arange(128)[:, None]
    s = np.arange(128)[None, :]
    prev = np.where(s > q, -(128.0 + q - s), -1e9)
    cur = np.where(s <= q, -(q - s).astype(np.float64), -1e9)
    cf[:, CF_ND:CF_ND + 128] = prev
    cf[:, CF_ND + 128:CF_ND + 256] = cur
    cf[:, CF_ND0:CF_ND0 + 128] = -1e9
    cf[:, CF_ND0 + 128:CF_ND0 + 256] = cur
    gcat = np.concatenate([g_swa.reshape(-1), g_sb.reshape(-1)]).astype(np.float32)
    cf[:, CF_G:CF_G + 16] = gcat.reshape(16, 128).T
    cf[:, CF_SK:CF_SK + 16] = np.broadcast_to(sinks.reshape(1, 16), (128, 16))
    cf[:, CF_M01:CF_M01 + 128] = np.where(q < s, 1.0, 0.0)
    cb = np.zeros((128, NCB), np.float32)
    cb[:, CB_ID:CB_ID + 128] = np.eye(128, dtype=np.float32)
    j = np.arange(128)[:, None]
    sidx = np.arange(128)[None, :]
    cb[:, CB_NT:CB_NT + 128] = np.where(j >= sidx, -1.0, 0.0)
    cb[:, CB_NSL:CB_NSL + 128] = np.where(j < sidx, -1.0, 0.0)
    cb[:, CB_MK:CB_MK + 128] = np.where(j >= sidx, NEG, 0.0)
    return cf, cb


def _tile(W, nk, col0s, r0=0):
    out = np.empty((len(col0s), 128, nk * 256), np.float32)
    for t, c0 in enumerate(col0s):
        out[t] = W[r0:r0 + nk * 128, c0:c0 + 256].reshape(nk, 128, 256).transpose(1, 0, 2).reshape(128, nk * 256)
    return out


def _w_in2(w_in):
    w = w_in
    qa = w[:, 0:1024]
    ka = w[:, 1024:1280]
    va = w[:, 1280:1536]
    qb = w[:, 1536:2560]
    kb = w[:, 2560:3584]
    vb = w[:, 3584:4608]
    kad = np.concatenate([np.concatenate([ka[:, k * 64:(k + 1) * 64]] * 2, axis=1) for k in range(4)], axis=1)
    return np.ascontiguousarray(np.concatenate([qa, kad, qb, kb, va, vb], axis=1))


_NC_CACHE = {}


def _prep_shared(inp):
    f = lambda a: np.ascontiguousarray(np.asarray(a, dtype=np.float32))
    cf, cb = _consts(f(inp["sinks"])[0], f(inp["g_swa"])[0], f(inp["g_sb"])[0])
    lnp = np.stack([f(inp[k])[0] for k in ("ln1_g", "ln1_b", "ln2_g", "ln2_b", "ln3_g", "ln3_b")])
    wgu = f(inp["w_gate_up"])[0]
    wgu2 = np.concatenate([wgu[:, :DFF].reshape(D, NFC, 128), wgu[:, DFF:].reshape(D, NFC, 128)], axis=2)
    wgu2 = wgu2.reshape(D, NFC * 256)
    wdn = f(inp["w_down"])[0]
    c8 = [cb * 256 for cb in range(8)]
    return {
        "w_in2": _tile(_w_in2(f(inp["w_in"])[0]), 16, [t * 256 for t in range(19)]),
        "w_o": _tile(f(inp["w_o"])[0], 16, c8),
        "w_q_mem": _tile(f(inp["w_q_mem"])[0], 16, c8),
        "w_kv_mem": _tile(f(inp["w_kv_mem"])[0], 16, [t * 256 for t in range(16)]),
        "w_o_mem": _tile(f(inp["w_o_mem"])[0], 16, c8),
        "w_gate_up": _tile(wgu2, 16, [t * 256 for t in range(NFC)]),
        "w_down": np.concatenate([_tile(wdn, 11, c8, r0=qd * 11 * 128) for qd in range(4)], axis=0),
        "cf32": cf, "cb16": cb, "lnp": np.ascontiguousarray(lnp),
    }


def kernel(**inputs):
    x = np.asarray(inputs["x"], dtype=np.float32)
    mem = np.asarray(inputs["mem"], dtype=np.float32)
    Bn, S, _ = x.shape
    shared = _prep_shared(inputs)
    if S not in _NC_CACHE:
        _NC_CACHE[S] = build(S)
    nc = _NC_CACHE[S]
    in_maps = []
    for b in range(Bn):
        m = dict(shared)
        m["x"] = np.ascontiguousarray(x[b])
        m["xT"] = np.ascontiguousarray(x[b].T)
        m["memT"] = np.ascontiguousarray(mem[b].T)
        in_maps.append(m)
    res = run_bass_kernel_spmd(nc, in_maps, core_ids=list(range(Bn)))
    return np.stack([r["out"] for r in res.results], axis=0).astype(np.float32)
```
